# Optimizing a Trainium2 kernel written in Bass

```python
import math
import jax, jax.numpy as jnp
from jax import lax
import numpy as np

D_MODEL = 1024
BATCH = 32
SEQ = 256
DEPTH = 1
DEC_BATCH = 2
DEC_SEQ = 2048
PAST_LEN = 512

GRID_W = 64
MLA_HEADS = 8
QK_NOPE = 128
QK_ROPE = 64
V_HEAD = 128
Q_LORA = 256
KV_LORA = 256
ROPE_BASE = 10000.0
ATTN_BLOCK = 128
SSD_HEADS = 16
SSD_HEADDIM = 64
SSD_INNER = SSD_HEADS * SSD_HEADDIM
SSD_GROUPS = 4
SSD_STATE = 64
SSD_CONV = 3
SSD_CHUNK = 128
SSD_CONV_DIM = SSD_INNER + 2 * SSD_GROUPS * SSD_STATE
D_FF = 2816
FFN_CONV = 3
EPS = 1e-6

IN_SPLITS = (Q_LORA, KV_LORA, QK_ROPE, SSD_INNER, SSD_INNER, SSD_GROUPS * SSD_STATE,
             SSD_GROUPS * SSD_STATE, 2 * SSD_HEADS, D_MODEL, D_MODEL)
IN_COLS = Q_LORA + KV_LORA + QK_ROPE + 2 * SSD_INNER + 2 * SSD_GROUPS * SSD_STATE + 2 * SSD_HEADS + 2 * D_MODEL

kernel_name = "hybrid_mla_ssd_convffn_diffusion_step"


def rmsnorm(x, g):
    x32 = x.astype(jnp.float32)
    y = x32 * lax.rsqrt(jnp.mean(x32 * x32, axis=-1, keepdims=True) + EPS)
    return (y * g.astype(jnp.float32)).astype(x.dtype)


def dwconv_centred(x, w, b):
    K = w.shape[0]
    pad = K // 2
    L = x.shape[1]
    xp = jnp.pad(x, ((0, 0), (pad, pad), (0, 0)))
    out = xp[:, 0:L] * w[0]
    for k in range(1, K):
        out = out + xp[:, k:k + L] * w[k]
    return out + b


def axial_angles(L):
    rows = L // GRID_W
    t = jnp.arange(rows * GRID_W)
    row = (t // GRID_W).astype(jnp.float32)
    col = (t % GRID_W).astype(jnp.float32)
    n = QK_ROPE // 4
    inv = ROPE_BASE ** (-jnp.arange(n, dtype=jnp.float32) / n)
    return jnp.stack([row[:, None] * inv, col[:, None] * inv], axis=1)


def rope_2d(x, ang):
    n = QK_ROPE // 4
    xr = x.reshape(x.shape[:-1] + (2, 2, n)).astype(jnp.float32)
    cos, sin = jnp.cos(ang), jnp.sin(ang)
    x0, x1 = xr[..., 0, :], xr[..., 1, :]
    out = jnp.stack([x0 * cos - x1 * sin, x0 * sin + x1 * cos], axis=-2)
    return out.reshape(x.shape).astype(x.dtype)


def block_attention(q, k, v):
    b, Lq, H, Dk = q.shape
    nb = Lq // ATTN_BLOCK
    scale = 1.0 / math.sqrt(QK_NOPE + QK_ROPE)
    qb = q.reshape(b, nb, ATTN_BLOCK, H, Dk).transpose(1, 0, 2, 3, 4)

    def one(qblk):
        s = jnp.einsum('bqhd,bkhd->bhqk', qblk, k).astype(jnp.float32) * scale
        p = jax.nn.softmax(s, axis=-1).astype(v.dtype)
        return jnp.einsum('bhqk,bkhd->bqhd', p, v)

    out = lax.map(one, qb)
    return out.transpose(1, 0, 2, 3, 4).reshape(b, Lq, H, v.shape[-1])


def mla_kv(ckv_n, kr, w_ukv):
    b, L, _ = ckv_n.shape
    kv = (ckv_n @ w_ukv).reshape(b, L, MLA_HEADS, QK_NOPE + V_HEAD)
    k_nope, v = kv[..., :QK_NOPE], kv[..., QK_NOPE:]
    k = jnp.concatenate([k_nope, jnp.broadcast_to(kr[:, :, None, :], (b, L, MLA_HEADS, QK_ROPE))], axis=-1)
    return k, v


def ssd_chunked_scan(x, dt, A, Bm, Cm, h0):
    b, L, H, P = x.shape
    Q = SSD_CHUNK
    nc = L // Q
    rep = H // SSD_GROUPS
    f32 = jnp.float32
    Bh = jnp.repeat(Bm.astype(f32), rep, axis=2).reshape(b, nc, Q, H, SSD_STATE)
    Ch = jnp.repeat(Cm.astype(f32), rep, axis=2).reshape(b, nc, Q, H, SSD_STATE)
    dtc = dt.astype(f32).reshape(b, nc, Q, H)
    xdt = x.astype(f32).reshape(b, nc, Q, H, P) * dtc[..., None]
    acum = jnp.cumsum(dtc * A.astype(f32), axis=2)
    seg = acum[:, :, :, None, :] - acum[:, :, None, :, :]
    causal = jnp.tril(jnp.ones((Q, Q), dtype=bool))[:, :, None]
    decay = jnp.where(causal, jnp.exp(jnp.where(causal, seg, 0.0)), 0.0)
    scores = jnp.einsum('bcihn,bcjhn->bcijh', Ch, Bh)
    y_diag = jnp.einsum('bcijh,bcjhp->bcihp', scores * decay, xdt)
    decay_end = jnp.exp(acum[:, :, -1:, :] - acum)
    chunk_states = jnp.einsum('bcjhn,bcjh,bcjhp->bchpn', Bh, decay_end, xdt)
    chunk_decay = jnp.exp(acum[:, :, -1, :])

    def step(h, inp):
        dec, st = inp
        return h * dec[:, :, None, None] + st, h

    hT, h_in = lax.scan(step, h0.astype(f32),
                        (chunk_decay.transpose(1, 0, 2), chunk_states.transpose(1, 0, 2, 3, 4)))
    h_in = h_in.transpose(1, 0, 2, 3, 4)
    y_off = jnp.einsum('bcihn,bchpn,bcih->bcihp', Ch, h_in, jnp.exp(acum))
    y = (y_diag + y_off).reshape(b, L, H, P)
    return y.astype(x.dtype), hT.astype(x.dtype)


def trunk_layer(x, c_vec, lp, latent, ctx):
    b, L, _ = x.shape
    mod = (jax.nn.silu(c_vec) @ lp['w_ada'] + lp['b_ada'])[:, None, :]
    sh1, sc1, g1, sh2, sc2, g2 = jnp.split(mod, 6, axis=-1)
    h = rmsnorm(x, lp['norm_attn_g']) * (1 + sc1) + sh1
    proj = h @ lp['w_in']
    cq, ckv, kr, z, xs, Bm, Cm, dt, gm, gs = jnp.split(proj, [int(s) for s in np.cumsum(IN_SPLITS)[:-1]], axis=-1)

    q = (rmsnorm(cq, lp['q_norm_g']) @ lp['w_uq']).reshape(b, L, MLA_HEADS, QK_NOPE + QK_ROPE)
    q_nope, q_rope = q[..., :QK_NOPE], q[..., QK_NOPE:]
    ckv_n = rmsnorm(ckv, lp['kv_norm_g'])
    if latent:
        ang = axial_angles(L)
        q_rope = rope_2d(q_rope, ang[:, None])
        kr = rope_2d(kr, ang)
    q = jnp.concatenate([q_nope, q_rope], axis=-1)
    k, v = mla_kv(ckv_n, kr, lp['w_ukv'])
    if ctx is not None:
        k_ctx, v_ctx = mla_kv(ctx[0], ctx[1], lp['w_ukv'])
        k = jnp.concatenate([k_ctx, k], axis=1)
        v = jnp.concatenate([v_ctx, v], axis=1)
    attn = block_attention(q, k, v)
    o_mla = attn.reshape(b, L, MLA_HEADS * V_HEAD) @ lp['w_o_mla']

    xbc = jax.nn.silu(dwconv_centred(jnp.concatenate([xs, Bm, Cm], axis=-1), lp['ssd_conv_w'], lp['ssd_conv_b']))
    xh = xbc[..., :SSD_INNER].reshape(b, L, SSD_HEADS, SSD_HEADDIM)
    Bg = xbc[..., SSD_INNER:SSD_INNER + SSD_GROUPS * SSD_STATE].reshape(b, L, SSD_GROUPS, SSD_STATE)
    Cg = xbc[..., SSD_INNER + SSD_GROUPS * SSD_STATE:].reshape(b, L, SSD_GROUPS, SSD_STATE)
    dt_all = jax.nn.softplus(dt.reshape(b, L, 2, SSD_HEADS).astype(jnp.float32) + lp['ssd_dt_bias'])
    A = -jnp.exp(lp['ssd_A_log'].astype(jnp.float32))
    if ctx is None:
        h0 = jnp.zeros((b, 2, SSD_HEADS, SSD_HEADDIM, SSD_STATE), jnp.float32)
    else:
        h0 = ctx[2]
    y_f, h_f = ssd_chunked_scan(xh, dt_all[:, :, 0], A[0], Bg, Cg, h0[:, 0])
    y_b, h_b = ssd_chunked_scan(jnp.flip(xh, 1), jnp.flip(dt_all[:, :, 1], 1), A[1],
                                jnp.flip(Bg, 1), jnp.flip(Cg, 1), h0[:, 1])
    y = y_f + jnp.flip(y_b, 1) + lp['ssd_D'][:, None] * xh
    y = rmsnorm(y.reshape(b, L, SSD_INNER) * jax.nn.silu(z), lp['ssd_norm_g'])
    o_ssd = y @ lp['w_o_ssd']

    merged = jax.nn.sigmoid(gm) * o_mla + jax.nn.sigmoid(gs) * o_ssd
    x = x + g1 * (merged @ lp['w_out'])

    h2 = rmsnorm(x, lp['norm_ffn_g']) * (1 + sc2) + sh2
    u = dwconv_centred(h2 @ lp['w_up'], lp['ffn_conv_w'], lp['ffn_conv_b'])
    x = x + g2 * ((jax.nn.silu(u[..., :D_FF]) * u[..., D_FF:]) @ lp['w_down'])
    return x, (ckv_n, kr, jnp.stack([h_f, h_b], axis=1))


def setup_inputs(seed: int = 0) -> dict:
    key = jax.random.key(seed)
    ks = iter(jax.random.split(key, 48))

    def nrm(shape, scale):
        return jax.random.normal(next(ks), shape, jnp.float32) * scale

    def gain(shape):
        return 1.0 + nrm(shape, 0.02)

    dt0 = jnp.exp(jax.random.uniform(next(ks), (DEPTH, 2, SSD_HEADS), jnp.float32,
                                     minval=math.log(1e-3), maxval=math.log(1e-1)))
    return {
        "x_prompt": nrm((BATCH, SEQ, D_MODEL), 1.0),
        "x_sample": nrm((DEC_BATCH, DEC_SEQ, D_MODEL), 1.0),
        "c": nrm((DEC_BATCH, D_MODEL), 1.0),
        "cache_ckv": nrm((DEC_BATCH, DEPTH, PAST_LEN, KV_LORA), 1.0),
        "cache_krope": nrm((DEC_BATCH, DEPTH, PAST_LEN, QK_ROPE), 1.0),
        "state_ssd": nrm((DEC_BATCH, DEPTH, 2, SSD_HEADS, SSD_HEADDIM, SSD_STATE), 0.1),
        "c_ctx": nrm((D_MODEL,), 1.0),
        "w_ada": nrm((DEPTH, D_MODEL, 6 * D_MODEL), D_MODEL ** -0.5),
        "b_ada": nrm((DEPTH, 6 * D_MODEL), 0.02),
        "norm_attn_g": gain((DEPTH, D_MODEL)),
        "w_in": nrm((DEPTH, D_MODEL, IN_COLS), D_MODEL ** -0.5),
        "q_norm_g": gain((DEPTH, Q_LORA)),
        "kv_norm_g": gain((DEPTH, KV_LORA)),
        "w_uq": nrm((DEPTH, Q_LORA, MLA_HEADS * (QK_NOPE + QK_ROPE)), Q_LORA ** -0.5),
        "w_ukv": nrm((DEPTH, KV_LORA, MLA_HEADS * (QK_NOPE + V_HEAD)), KV_LORA ** -0.5),
        "w_o_mla": nrm((DEPTH, MLA_HEADS * V_HEAD, D_MODEL), (MLA_HEADS * V_HEAD) ** -0.5),
        "ssd_conv_w": nrm((DEPTH, SSD_CONV, SSD_CONV_DIM), SSD_CONV ** -0.5),
        "ssd_conv_b": nrm((DEPTH, SSD_CONV_DIM), 0.02),
        "ssd_dt_bias": dt0 + jnp.log(-jnp.expm1(-dt0)),
        "ssd_A_log": jnp.log(jax.random.uniform(next(ks), (DEPTH, 2, SSD_HEADS), jnp.float32, minval=1.0, maxval=16.0)),
        "ssd_D": gain((DEPTH, SSD_HEADS)),
        "ssd_norm_g": gain((DEPTH, SSD_INNER)),
        "w_o_ssd": nrm((DEPTH, SSD_INNER, D_MODEL), SSD_INNER ** -0.5),
        "w_out": nrm((DEPTH, D_MODEL, D_MODEL), D_MODEL ** -0.5),
        "norm_ffn_g": gain((DEPTH, D_MODEL)),
        "w_up": nrm((DEPTH, D_MODEL, 2 * D_FF), D_MODEL ** -0.5),
        "ffn_conv_w": nrm((DEPTH, FFN_CONV, 2 * D_FF), FFN_CONV ** -0.5),
        "ffn_conv_b": nrm((DEPTH, 2 * D_FF), 0.02),
        "w_down": nrm((DEPTH, D_FF, D_MODEL), D_FF ** -0.5),
        "final_norm_g": gain((D_MODEL,)),
    }


def reference(x_prompt, x_sample, c, cache_ckv, cache_krope, state_ssd, c_ctx,
              w_ada, b_ada, norm_attn_g, w_in, q_norm_g, kv_norm_g, w_uq, w_ukv, w_o_mla,
              ssd_conv_w, ssd_conv_b, ssd_dt_bias, ssd_A_log, ssd_D, ssd_norm_g, w_o_ssd,
              w_out, norm_ffn_g, w_up, ffn_conv_w, ffn_conv_b, w_down, final_norm_g):
    xp = x_prompt
    xs = x_sample
    new_ckv, new_krope, new_ssd = [], [], []
    for l in range(DEPTH):
        lp = {
            'w_ada': w_ada[l], 'b_ada': b_ada[l], 'norm_attn_g': norm_attn_g[l], 'w_in': w_in[l],
            'q_norm_g': q_norm_g[l], 'kv_norm_g': kv_norm_g[l], 'w_uq': w_uq[l], 'w_ukv': w_ukv[l],
            'w_o_mla': w_o_mla[l], 'ssd_conv_w': ssd_conv_w[l], 'ssd_conv_b': ssd_conv_b[l],
            'ssd_dt_bias': ssd_dt_bias[l], 'ssd_A_log': ssd_A_log[l], 'ssd_D': ssd_D[l],
            'ssd_norm_g': ssd_norm_g[l], 'w_o_ssd': w_o_ssd[l], 'w_out': w_out[l],
            'norm_ffn_g': norm_ffn_g[l], 'w_up': w_up[l], 'ffn_conv_w': ffn_conv_w[l],
            'ffn_conv_b': ffn_conv_b[l], 'w_down': w_down[l],
        }
        xp, (ckv_l, kr_l, st_l) = trunk_layer(xp, c_ctx[None, :], lp, False, None)
        new_ckv.append(ckv_l)
        new_krope.append(kr_l)
        new_ssd.append(st_l)
        xs, _ = trunk_layer(xs, c, lp, True, (cache_ckv[:, l], cache_krope[:, l], state_ssd[:, l]))
    y_prompt = rmsnorm(xp, final_norm_g)
    y_sample = rmsnorm(xs, final_norm_g)
    return (y_prompt, y_sample, jnp.stack(new_ckv, axis=1), jnp.stack(new_krope, axis=1), jnp.stack(new_ssd, axis=1))
```

```python
import math
from contextlib import ExitStack, nullcontext
import numpy as np
import concourse.bass as bass
import concourse.mybir as mybir
from concourse.bass_utils import run_bass_kernel_spmd

F32 = mybir.dt.float32
BF16 = mybir.dt.bfloat16
AF = mybir.ActivationFunctionType
ALU = mybir.AluOpType

D = 1024
IN_COLS = 5216
C_CQ, C_CKV, C_KR, C_Z, C_X, C_DT, C_GM = 0, 256, 512, 576, 1600, 3136, 3168
DFF = 2816
EPS = 1e-6
NEG = -30000.0


class Sched:
    def __init__(self, nc):
        self.nc = nc
        self.e = {"pe": nc.tensor, "act": nc.scalar, "dve": nc.vector,
                  "pool": nc.gpsimd, "sp": nc.sync}
        self.sems = {}
        self.cnt = {}
        for k in ("pe", "act", "dve", "pool"):
            self.sems[k] = nc.alloc_semaphore("c_" + k)
            self.cnt[k] = 0
        self.seen = {k: {} for k in self.e}
        self.lastw = {}
        self.readers = {}
        self.nins = 0
        self.phys = {}
        self.physq = {}
        self.free = {}
        self.nphys = 0

    def _sem(self, sk, q):
        if sk not in self.phys:
            fl = self.free.setdefault(q, [])
            if fl:
                pid = fl.pop()
            else:
                pid = "d_%d" % self.nphys
                self.nphys += 1
                self.sems[pid] = self.nc.alloc_semaphore(pid)
                self.cnt[pid] = 0
            self.phys[sk] = pid
            self.physq[pid] = q
        return self.phys[sk]

    def _deps(self, reads, writes):
        best = {}

        def add(sk, v):
            if best.get(sk, 0) < v:
                best[sk] = v
        for k in reads:
            if k in self.lastw:
                add(*self.lastw[k])
        for k in writes:
            if k in self.lastw:
                add(*self.lastw[k])
            for sk, v in self.readers.get(k, {}).items():
                add(sk, v)
        return best

    def _wait(self, eng, best):
        for sk, v in best.items():
            if sk == eng and eng == "pe":
                continue
            if self.seen[eng].get(sk, 0) >= v:
                continue
            self.seen[eng][sk] = v
            self.e[eng].wait_ge(self.sems[sk], v)
            self.nins += 1

    def _record(self, tok, reads, writes):
        for k in writes:
            self.lastw[k] = tok
            self.readers[k] = {}
        for k in reads:
            r = self.readers.setdefault(k, {})
            if r.get(tok[0], 0) < tok[1]:
                r[tok[0]] = tok[1]

    def op(self, eng, fn, reads=(), writes=()):
        self._wait(eng, self._deps(reads, writes))
        ins = fn(self.e[eng])
        self.cnt[eng] += 1
        ins.then_inc(self.sems[eng], 1)
        self.nins += 1
        self._record((eng, self.cnt[eng]), reads, writes)

    def dma(self, q, out, in_, key, reads=(), writes=(), **kw):
        self._wait(q, self._deps(reads, writes))
        pid = self._sem("dma:%s:%s" % (q, key), q)
        ins = self.e[q].dma_start(out=out, in_=in_, **kw)
        self.cnt[pid] += 16
        ins.then_inc(self.sems[pid], 16)
        self.nins += 1
        self._record((pid, self.cnt[pid]), reads, writes)

    def _all(self):
        best = {}
        for sk, v in self.lastw.values():
            if best.get(sk, 0) < v:
                best[sk] = v
        for r in self.readers.values():
            for sk, v in r.items():
                if best.get(sk, 0) < v:
                    best[sk] = v
        return best

    def barrier(self):
        best = self._all()
        for eng in self.e:
            self._wait(eng, dict(best))
        self.lastw = {}
        self.readers = {}
        for pid in self.phys.values():
            self.free.setdefault(self.physq[pid], []).append(pid)
        self.phys = {}

    def finish(self):
        self._wait("sp", self._all())


PIPE = {"slot": None, "depth": 1}


class Rot:
    def __init__(self, aps, name):
        self.aps = aps
        self.name = name
        self.i = 0
        self.si = {}

    def next(self):
        sl = PIPE["slot"]
        d = PIPE["depth"]
        if sl is None or len(self.aps) < d:
            self.i = (self.i + 1) % len(self.aps)
            j = self.i
        else:
            idx = [i for i in range(len(self.aps)) if i % d == sl]
            c = (self.si.get(sl, 0) + 1) % len(idx)
            self.si[sl] = c
            j = idx[c]
        return self.aps[j], "%s%d" % (self.name, j)


def pipeline(make_body, items, depth):
    PIPE["depth"] = depth
    active = {}
    it = iter(items)
    done = False
    while True:
        for sl in range(depth):
            if sl not in active and not done:
                x = next(it, None)
                if x is None:
                    done = True
                else:
                    active[sl] = make_body(x)
        if not active:
            break
        for sl in sorted(active):
            PIPE["slot"] = sl
            try:
                next(active[sl])
            except StopIteration:
                del active[sl]
    PIPE["slot"] = None
    PIPE["depth"] = 1


STOP = [0]
PDEPTH = 2
SUB = [0]
DEBUG = [0]
DBG_OUT = []


def build(NP, LP, LS, PAST):
    nc = bass.Bass("TRN2", target_bir_lowering=False)
    S = Sched(nc)
    RPRM = NP * LP
    R = RPRM + LS
    NT = R // 128
    NPT = RPRM // 128
    NTS = LS // 128
    G4 = 4
    OWN = NTS // G4
    assert OWN * G4 == NTS and 1 <= OWN <= 4
    WIN = [NPT + NTS - 1] + [NPT + i for i in range(OWN + 1)]
    TWIN = list(range(NPT)) + WIN
    TWINS = set(TWIN)
    TOWN = list(range(NPT)) + [NPT + i for i in range(OWN)]
    seqs = [(i * LP, LP, 0) for i in range(NP)] + [(RPRM, LS, 1)]
    NSEQ = len(seqs)
    RP = R + 2 * NSEQ
    KRT = R + PAST

    def seq_of_tile(gt):
        r = gt * 128
        for si, (r0, L, m) in enumerate(seqs):
            if r0 <= r < r0 + L:
                return si, (r - r0) // 128
        raise ValueError

    def prow(gt):
        si, _ = seq_of_tile(gt)
        return gt * 128 + 2 * si + 1

    def krow(gt):
        si, _ = seq_of_tile(gt)
        return gt * 128 + (PAST if seqs[si][2] else 0)

    def din(name, shape):
        return nc.dram_tensor(name, list(shape), F32, kind="ExternalInput").ap()

    def dout(name, shape):
        return nc.dram_tensor(name, list(shape), F32, kind="ExternalOutput").ap()

    xp = din("xp", [RPRM, D]); xs = din("xs", [LS, D]); cvec = din("cvec", [2, D])
    cckv = din("cckv", [PAST, 256]); ckr = din("ckr", [PAST, 64]); st_in = din("st", [2, 16, 64, 64])
    w_ada = din("w_ada", [D, 6 * D]); b_ada = din("b_ada", [6 * D]); ga = din("ga", [D])
    w_in = din("w_in", [D, IN_COLS]); qg = din("qg", [256]); kvg = din("kvg", [256])
    w_uq = din("w_uq", [256, 1536]); w_ukv = din("w_ukv", [256, 2048]); w_o_mla = din("w_o_mla", [D, D])
    cw = din("cw", [3, 1536]); cb = din("cb", [1536]); dtb = din("dtb", [32]); alog = din("alog", [32])
    Dv = din("Dv", [16]); gn = din("gn", [D]); w_o_ssd = din("w_o_ssd", [D, D]); w_out = din("w_out", [D, D])
    gf = din("gf", [D]); w_up = din("w_up", [D, 2 * DFF]); fw = din("fw", [3, 2 * DFF]); fb = din("fb", [2 * DFF])
    w_down = din("w_down", [DFF, D]); fng = din("fng", [D])
    cos_t = din("cos_t", [LS, 32]); sin_t = din("sin_t", [LS, 32]); mk = din("mk", [128, 16])
    yp = dout("yp", [RPRM, D]); ys = dout("ys", [OWN * 128, D]); nckv = dout("nckv", [RPRM, 256])
    nkr = dout("nkr", [RPRM, 64]); nssd = dout("nssd", [NP, 2, 16, 64, 64])

    def xrows(gt):
        r = gt * 128
        return xp[r:r + 128, :] if r < RPRM else xs[r - RPRM:r - RPRM + 128, :]

    def yrows(gt):
        r = gt * 128
        return yp[r:r + 128, :] if r < RPRM else ys[r - RPRM:r - RPRM + 128, :]

    def scr(name, shape, dt):
        if DEBUG[0]:
            return nc.dram_tensor(name, list(shape), dt, kind="ExternalOutput").ap()
        return nc.dram_tensor(name, list(shape), dt).ap()
    HT = scr("HT", [NT, 128, D], BF16)
    PROJ = scr("PROJ", [RP, IN_COLS], F32)
    QS = scr("QS", [R, 1536], BF16)
    KVS = scr("KVS", [KRT, 2048], BF16)
    KRS = scr("KRS", [KRT, 64], BF16)
    XBC = scr("XBC", [R, 1536], BF16)
    STT = scr("STT", [R, 160], F32)
    GMT = scr("GMT", [NT, 32, 128], F32)
    HIN = scr("HIN", [NT, 2, 64, D], BF16)
    YN = scr("YN", [R, D], BF16)
    XMID = scr("XMID", [R, D], F32)
    H2T = scr("H2T", [NT, 128, D], BF16)

    def G(name, shape, dt):
        return nc.alloc_sbuf_tensor(name, list(shape), dt).ap()
    PB = [nc.alloc_psum_tensor("pb%d" % i, [128, 512], F32).ap() for i in range(8)]
    pst = {"i": 0, "n": 8, "a": 0}

    def bank():
        sl = PIPE["slot"]
        if sl is not None and PIPE["depth"] >= 2:
            nb_ = 8 // PIPE["depth"]
            k = "s%d" % sl
            pst[k] = (pst.get(k, 0) + 1) % nb_
            j = sl * nb_ + pst[k]
            return PB[j], "pb%d" % j
        pst["i"] = (pst["i"] + 1) % pst["n"]
        return PB[pst["i"]], "pb%d" % pst["i"]

    def bank_acc():
        pst["a"] = 1 - pst["a"]
        return PB[6 + pst["a"]], "pb%d" % (6 + pst["a"])

    ident = G("ident", [128, 128], BF16)
    ident32 = G("ident32", [128, 128], F32)
    ones32 = G("ones32", [128, 128], F32)
    tincl = G("tincl", [128, 128], F32)
    texcl = G("texcl", [128, 128], F32)
    epsT = G("epsT", [128, 1], F32)
    oneT = G("oneT", [128, 1], F32)
    zeroT = G("zeroT", [128, 1408], BF16)
    zero32 = G("zero32", [128, 1536], F32)
    MOD = [G("mod%d" % m, [128, 6 * D], F32) for m in range(2)]
    SH1, GSC1, G1, SH2, GSC2, G2 = range(6)

    def modp(m, part):
        return MOD[m][:, part * D:(part + 1) * D]

    def memset(eng, ap, val, key):
        S.op(eng, lambda e: e.memset(ap, val), writes=[key])

    def asel(ap, key, pattern, cmp, fill, base, cm):
        S.op("pool", lambda e: e.affine_select(ap, ap, pattern, cmp, fill, base=base, channel_multiplier=cm),
             reads=[key], writes=[key])

    memset("pool", ident, 1.0, "ident"); asel(ident, "ident", [[-1, 128]], ALU.is_equal, 0.0, 0, 1)
    memset("pool", ident32, 1.0, "ident32"); asel(ident32, "ident32", [[-1, 128]], ALU.is_equal, 0.0, 0, 1)
    memset("dve", ones32, 1.0, "ones32")
    memset("pool", tincl, 1.0, "tincl"); asel(tincl, "tincl", [[1, 128]], ALU.is_ge, 0.0, 0, -1)
    memset("pool", texcl, 1.0, "texcl"); asel(texcl, "texcl", [[1, 128]], ALU.is_gt, 0.0, 0, -1)
    memset("dve", epsT, EPS, "epsT"); memset("dve", oneT, 1.0, "oneT")
    mkt = G("mkt", [128, 16], F32)
    S.dma("sp", mkt, mk, "mkt", writes=["mkt"])
    memset("dve", zeroT, 0.0, "zeroT"); memset("dve", zero32, 0.0, "zero32")

    def copy(eng, out, in_, reads, writes):
        if eng == "act":
            S.op("act", lambda e: e.copy(out, in_), reads=reads, writes=writes)
        else:
            S.op(eng, lambda e: e.tensor_copy(out, in_), reads=reads, writes=writes)

    def tt(eng, out, a, b, op, reads, writes):
        S.op(eng, lambda e: e.tensor_tensor(out, a, b, op), reads=reads, writes=writes)

    def transpose_to(dst, dkey, src, skey, n, ceng="act", w=128):
        done = 0
        while done < n:
            m = min(8, n - done)
            pb, pk = bank()
            pbb = pb.bitcast(BF16)
            for k in range(m):
                S.op("pe", lambda e, k=k, done=done, pbb=pbb: e.transpose(
                    pbb[0:w, k * 128:(k + 1) * 128], src[:, (done + k) * w:(done + k + 1) * w], ident),
                    reads=[skey, "ident"], writes=[pk])
            copy(ceng, dst[0:w, done * 128:(done + m) * 128], pbb[0:w, 0:m * 128], [pk], [dkey])
            done += m

    def load(dst, src, key, reads=(), **kw):
        S.dma("sp", dst, src, key, reads=reads, writes=[key], **kw)

    def loadc(dst, src, key, reads=()):
        S.dma("pool", dst, src, key, reads=reads, writes=[key])

    def store(dst, src, key, writes):
        S.dma("pool", dst, src, key, reads=[key], writes=writes)

    def rstd_of(st, stk, src, skey, n, junk, jkey):
        S.op("act", lambda e: e.activation(junk, src, AF.Square, accum_out=st[:, 0:1]),
             reads=[skey], writes=[jkey, stk])
        S.op("act", lambda e: e.activation(st[:, 1:2], st[:, 0:1], AF.Ln, bias=epsT, scale=1.0 / n),
             reads=[stk, "epsT"], writes=[stk])
        S.op("act", lambda e: e.activation(st[:, 2:3], st[:, 1:2], AF.Exp, scale=-0.5),
             reads=[stk], writes=[stk])
        return st[:, 2:3]

    def stt(eng, out, a, sc, b, op0, op1, reads, writes):
        S.op(eng, lambda e: e.scalar_tensor_tensor(out, a, sc, b, op0, op1), reads=reads, writes=writes)

    es_keep = ExitStack()
    with nullcontext(es_keep) as es:
        def L(name, shape, dt):
            return es.enter_context(nc.sbuf_tensor(name, list(shape), dt)).ap()
        cT = L("cT", [128, 2, 8], F32)
        cS = L("cS", [128, 2, 8], F32)
        cB = L("cB", [128, 16, 128], BF16)
        wb = Rot([L("s0w%d" % i, [128, 8, 512], BF16) for i in range(2)], "s0w")
        bb = Rot([L("s0b%d" % i, [128, 512], F32) for i in range(2)], "s0b")
        gab = L("gab", [128, D], F32)
        gfb = L("gfb", [128, D], F32)
        load(cT, cvec.rearrange("m (k p) -> p m k", p=128), "cT", allow_slow_non_contiguous=True)
        load(gab, ga.partition_broadcast(128), "gab")
        load(gfb, gf.partition_broadcast(128), "gfb")
        S.op("act", lambda e: e.activation(cS, cT, AF.Silu), reads=["cT"], writes=["cS"])
        for m in range(2):
            for k in range(8):
                copy("dve", cB[:, m * 8 + k, :], cS[:, m, k:k + 1].to_broadcast([128, 128]), ["cS"], ["cB"])
        for j in range(12):
            wt, wk = wb.next()
            loadc(wt, w_ada[:, j * 512:(j + 1) * 512].rearrange("(k p) n -> p k n", p=128), wk)
            bt, bk = bb.next()
            load(bt, b_ada[j * 512:(j + 1) * 512].partition_broadcast(128), bk)
            for m in range(2):
                pb, pk = bank()
                for k in range(8):
                    S.op("pe", lambda e, k=k, m=m, pb=pb, wt=wt: e.matmul(pb, cB[:, m * 8 + k, :], wt[:, k, :], start=(k == 0), stop=(k == 7)),
                         reads=["cB", wk], writes=[pk])
                tt("dve", MOD[m][:, j * 512:(j + 1) * 512], pb, bt, ALU.add, [pk, bk], ["mod%d" % m])
        for m in range(2):
            stt("dve", modp(m, GSC1), modp(m, GSC1), 1.0, gab, ALU.add, ALU.mult, ["mod%d" % m, "gab"], ["mod%d" % m])
            stt("dve", modp(m, GSC2), modp(m, GSC2), 1.0, gfb, ALU.add, ALU.mult, ["mod%d" % m, "gfb"], ["mod%d" % m])
        if STOP[0] == 1:
            S.finish(); return nc, S

    with nullcontext(es_keep) as es:
        def L(name, shape, dt):
            return es.enter_context(nc.sbuf_tensor(name, list(shape), dt)).ap()
        xb = Rot([L("s1x%d" % i, [128, D], F32) for i in range(2)], "s1x")
        jb = Rot([L("s1j%d" % i, [128, D], F32) for i in range(2)], "s1j")
        hbb = Rot([L("s1h%d" % i, [128, D], BF16) for i in range(2)], "s1h")
        hTb = Rot([L("s1t%d" % i, [128, D], BF16) for i in range(2)], "s1t")
        stb = Rot([L("s1s%d" % i, [128, 4], F32) for i in range(2)], "s1s")
        def body(gt):
            si, _ = seq_of_tile(gt)
            yield
            m = seqs[si][2]
            yield
            xt, xk = xb.next(); load(xt, xrows(gt), xk)
            yield
            st, stk = stb.next(); junk, jk = jb.next(); hb, hk = hbb.next(); hT, hTk = hTb.next()
            yield
            r = rstd_of(st, stk, xt, xk, D, junk, jk)
            yield
            stt("dve", junk, xt, r, modp(m, GSC1), ALU.mult, ALU.mult, [xk, stk, "mod%d" % m], [jk])
            yield
            tt("pool", hb, junk, modp(m, SH1), ALU.add, [jk, "mod%d" % m], [hk])
            yield
            transpose_to(hT, hTk, hb, hk, 8, "act")
            yield
            store(HT[gt], hT, hTk, [("HT", gt)])
            yield
        pipeline(body, range(NT), PDEPTH)
        if STOP[0] == 2:
            S.finish(); return nc, S

    def proj_stage(SRC, W, groups, DST, dst_dt, tag):
        with ExitStack() as es:
            def L(name, shape, dt):
                return es.enter_context(nc.sbuf_tensor(name, list(shape), dt)).ap()
            wb = Rot([L(tag + "w%d" % i, [128, 8, 512], BF16) for i in range(3)], tag + "w")
            hall = L(tag + "hall", [128, NT, D], BF16)
            for gt in range(NT):
                load(hall[:, gt, :], SRC[gt], tag + "hall%d" % gt, reads=[("HT", gt)])
            ob = Rot([L(tag + "o%d" % i, [128, 512], dst_dt) for i in range(3)], tag + "o")
            blks = []
            for (g0, g1, tl) in groups:
                c0 = g0
                while c0 < g1:
                    cwid = min(512, g1 - c0)
                    blks.append((c0, cwid, tl))
                    c0 += cwid

            def pj_wload(bi):
                c0, cwid, tl = blks[bi]
                wt, wk = wb.next()
                loadc(wt[:, :, 0:cwid], W[:, c0:c0 + cwid].rearrange("(k p) n -> p k n", p=128), wk)
                return wt, wk
            pj_next = pj_wload(0)
            for bi, (c0, cwid, tl) in enumerate(blks):
                wt, wk = pj_next
                if bi + 1 < len(blks):
                    pj_next = pj_wload(bi + 1)
                for gt in tl:
                    ht, hk = hall[:, gt, :], tag + "hall%d" % gt
                    pb, pk = bank()
                    for k in range(8):
                        S.op("pe", lambda e, k=k, pb=pb, ht=ht, wt=wt, cwid=cwid: e.matmul(
                            pb[:, 0:cwid], ht[:, k * 128:(k + 1) * 128], wt[:, k, 0:cwid], start=(k == 0), stop=(k == 7)),
                            reads=[hk, wk], writes=[pk])
                    ot, ok = ob.next()
                    copy("act" if (gt + bi) % 2 else "dve", ot[:, 0:cwid], pb[:, 0:cwid], [pk], [ok])
                    pr = prow(gt)
                    store(DST[pr:pr + 128, c0:c0 + cwid], ot[:, 0:cwid], ok, [(tag, gt, bi)])
            S.barrier()
            if STOP[0] == 3:
                return True

    for si, (r0, Ls, m) in enumerate(seqs):
        for pr in (r0 + 2 * si, r0 + 2 * si + Ls + 1):
            S.dma("sp", PROJ[pr:pr + 1, C_X:C_X + 1536], zero32[0:1, :], "zero32", reads=["zero32"], writes=[("pad", pr)])
    ALLT = list(range(NT))
    s2_groups = [(C_CKV, C_Z, ALLT), (C_X, C_GM, ALLT), (C_CQ, C_CKV, TWIN), (C_Z, C_X, TWIN), (C_GM, IN_COLS, TWIN)]
    if proj_stage(HT, w_in, s2_groups, PROJ, F32, "s2"):
        S.finish(); return nc, S
    es_keep.close()

    with ExitStack() as es:
        def L(name, shape, dt):
            return es.enter_context(nc.sbuf_tensor(name, list(shape), dt)).ap()
        wuq = L("wuq", [128, 2, 1536], BF16); wukv = L("wukv", [128, 2, 2048], BF16)
        qgb = L("qgb", [128, 256], F32); kvgb = L("kvgb", [128, 256], F32)
        loadc(wuq, w_uq.rearrange("(k p) n -> p k n", p=128), "wuq")
        loadc(wukv, w_ukv.rearrange("(k p) n -> p k n", p=128), "wukv")
        load(qgb, qg.partition_broadcast(128), "qgb"); load(kvgb, kvg.partition_broadcast(128), "kvgb")
        inb = Rot([L("s3i%d" % i, [128, 576], F32) for i in range(3)], "s3i")
        stb = Rot([L("s3s%d" % i, [128, 8], F32) for i in range(3)], "s3s")
        jb = Rot([L("s3j%d" % i, [128, 256], F32) for i in range(3)], "s3j")
        cqb = Rot([L("s3c%d" % i, [128, 256], BF16) for i in range(3)], "s3c")
        cqT = Rot([L("s3ct%d" % i, [128, 256], BF16) for i in range(3)], "s3ct")
        qfb = Rot([L("s3q%d" % i, [128, 1536], F32) for i in range(3)], "s3q")
        qbb = Rot([L("s3qb%d" % i, [128, 1536], BF16) for i in range(3)], "s3qb")
        ckb = Rot([L("s3k%d" % i, [128, 256], F32) for i in range(3)], "s3k")
        ckbb = Rot([L("s3kb%d" % i, [128, 256], BF16) for i in range(3)], "s3kb")
        ckT = Rot([L("s3kt%d" % i, [128, 256], BF16) for i in range(3)], "s3kt")
        kvb = Rot([L("s3v%d" % i, [128, 2048], BF16) for i in range(3)], "s3v")
        krf = Rot([L("s3r%d" % i, [128, 64], F32) for i in range(3)], "s3r")
        krb = Rot([L("s3rb%d" % i, [128, 64], BF16) for i in range(3)], "s3rb")
        csb = Rot([L("s3cs%d" % i, [128, 64], F32) for i in range(3)], "s3cs")
        tmpb = Rot([L("s3t%d" % i, [128, 4, 256], F32) for i in range(3)], "s3t")

        def kv_from(ckn32, ck32k, key_row, kr_bf, kr_k):
            cbf, cbk = ckbb.next()
            copy("dve", cbf, ckn32, [ck32k], [cbk])
            ct, ctk = ckT.next()
            transpose_to(ct, ctk, cbf, cbk, 2, "act")
            kv, kvk = kvb.next()
            for nb in range(4):
                pb, pk = bank()
                for k in range(2):
                    S.op("pe", lambda e, k=k, nb=nb, pb=pb, ct=ct: e.matmul(
                        pb, ct[:, k * 128:(k + 1) * 128], wukv[:, k, nb * 512:(nb + 1) * 512], start=(k == 0), stop=(k == 1)),
                        reads=[ctk, "wukv"], writes=[pk])
                copy("act" if nb % 2 else "dve", kv[:, nb * 512:(nb + 1) * 512], pb, [pk], [kvk])
            store(KVS[key_row:key_row + 128, :], kv, kvk, [("KVS", key_row)])
            store(KRS[key_row:key_row + 128, :], kr_bf, kr_k, [("KRS", key_row)])

        def body(gt):
            si, ti = seq_of_tile(gt)
            yield
            r0s, Ls, lat = seqs[si]
            yield
            r = gt * 128
            yield
            pr = prow(gt)
            yield
            it, ik = inb.next()
            if gt in TWINS:
                load(it, PROJ[pr:pr + 128, 0:576], ik)
            else:
                load(it[:, 256:576], PROJ[pr:pr + 128, 256:576], ik)
            yield
            st, stk = stb.next(); junk, jk = jb.next()
            yield
            if lat:
                cs, csk = csb.next()
                t0 = ti * 128
                load(cs[:, 0:32], cos_t[t0:t0 + 128, :], csk); load(cs[:, 32:64], sin_t[t0:t0 + 128, :], csk)
            if gt in TWINS:
                yield
                rq = rstd_of(st, stk, it[:, 0:256], ik, 256, junk, jk)
                yield
                cq, cqk = cqb.next()
                yield
                stt("dve", cq, it[:, 0:256], rq, qgb, ALU.mult, ALU.mult, [ik, stk, "qgb"], [cqk])
                yield
                ct, ctk = cqT.next()
                yield
                transpose_to(ct, ctk, cq, cqk, 2, "act")
                yield
                qf, qfk = qfb.next()
                yield
                for nb in range(3):
                    pb, pk = bank()
                    for k in range(2):
                        S.op("pe", lambda e, k=k, nb=nb, pb=pb, ct=ct: e.matmul(
                            pb, ct[:, k * 128:(k + 1) * 128], wuq[:, k, nb * 512:(nb + 1) * 512], start=(k == 0), stop=(k == 1)),
                            reads=[ctk, "wuq"], writes=[pk])
                    copy("act" if nb % 2 else "dve", qf[:, nb * 512:(nb + 1) * 512], pb, [pk], [qfk])
                yield
                qb, qbk = qbb.next()
                yield
                copy("dve", qb, qf, [qfk], [qbk])
                yield
                if lat:
                    tmp, tk = tmpb.next()
                    qv = qf.rearrange("p (h d) -> p h d", d=192)[:, :, 128:192].rearrange("p h (a t n) -> p h a t n", a=2, t=2)
                    ov = qb.rearrange("p (h d) -> p h d", d=192)[:, :, 128:192].rearrange("p h (a t n) -> p h a t n", a=2, t=2)
                    x0 = qv[:, :, :, 0, :]; x1 = qv[:, :, :, 1, :]
                    cv = cs[:, 0:32].rearrange("p (a n) -> p a n", a=2).unsqueeze(1).to_broadcast([128, 8, 2, 16])
                    sv = cs[:, 32:64].rearrange("p (a n) -> p a n", a=2).unsqueeze(1).to_broadcast([128, 8, 2, 16])
                    tv = [tmp[:, i, :].rearrange("p (h a n) -> p h a n", h=8, a=2) for i in range(4)]
                    tt("dve", tv[0], x0, cv, ALU.mult, [qfk, csk], [tk])
                    tt("dve", tv[1], x1, sv, ALU.mult, [qfk, csk], [tk])
                    tt("dve", tv[2], x0, sv, ALU.mult, [qfk, csk], [tk])
                    tt("dve", tv[3], x1, cv, ALU.mult, [qfk, csk], [tk])
                    tt("dve", ov[:, :, :, 0, :], tv[0], tv[1], ALU.subtract, [tk], [qbk])
                    tt("dve", ov[:, :, :, 1, :], tv[2], tv[3], ALU.add, [tk], [qbk])
                yield
                store(QS[r:r + 128, :], qb, qbk, [("QS", gt)])
            yield
            rk = rstd_of(st[:, 4:8], stk, it[:, 256:512], ik, 256, junk, jk)
            yield
            ck, ckk = ckb.next()
            yield
            stt("dve", ck, it[:, 256:512], rk, kvgb, ALU.mult, ALU.mult, [ik, stk, "kvgb"], [ckk])
            yield
            kf, kfk = krf.next()
            yield
            if lat:
                tmp, tk = tmpb.next()
                kvw = it[:, 512:576].rearrange("p (a t n) -> p a t n", a=2, t=2)
                okv = kf.rearrange("p (a t n) -> p a t n", a=2, t=2)
                x0 = kvw[:, :, 0, :]; x1 = kvw[:, :, 1, :]
                cv = cs[:, 0:32].rearrange("p (a n) -> p a n", a=2)
                sv = cs[:, 32:64].rearrange("p (a n) -> p a n", a=2)
                tv = [tmp[:, i, 0:32].rearrange("p (a n) -> p a n", a=2) for i in range(4)]
                tt("dve", tv[0], x0, cv, ALU.mult, [ik, csk], [tk])
                tt("dve", tv[1], x1, sv, ALU.mult, [ik, csk], [tk])
                tt("dve", tv[2], x0, sv, ALU.mult, [ik, csk], [tk])
                tt("dve", tv[3], x1, cv, ALU.mult, [ik, csk], [tk])
                tt("dve", okv[:, :, 0, :], tv[0], tv[1], ALU.subtract, [tk], [kfk])
                tt("dve", okv[:, :, 1, :], tv[2], tv[3], ALU.add, [tk], [kfk])
            else:
                copy("dve", kf, it[:, 512:576], [ik], [kfk])
                S.dma("sp", nckv[r:r + 128, :], ck, ckk, reads=[ckk], writes=[("nckv", gt)])
                S.dma("sp", nkr[r:r + 128, :], kf, kfk, reads=[kfk], writes=[("nkr", gt)])
            yield
            kb, kbk = krb.next()
            yield
            copy("dve", kb, kf, [kfk], [kbk])
            yield
            kv_from(ck, ckk, krow(gt), kb, kbk)
            yield
        pipeline(body, range(NT), 3)
        for ct_i in range(PAST // 128):
            ck, ckk = ckb.next(); load(ck, cckv[ct_i * 128:(ct_i + 1) * 128, :], ckk)
            kf, kfk = krf.next(); load(kf, ckr[ct_i * 128:(ct_i + 1) * 128, :], kfk)
            kb, kbk = krb.next()
            copy("dve", kb, kf, [kfk], [kbk])
            kv_from(ck, ckk, RPRM + ct_i * 128, kb, kbk)
        S.barrier()
        if STOP[0] == 4:
            S.finish(); return nc, S

    NKMAX = PAST + LS
    ATTT = scr("ATTT", [8, 128, R], BF16)
    with ExitStack() as es:
        def L(name, shape, dt):
            return es.enter_context(nc.sbuf_tensor(name, list(shape), dt)).ap()
        knT = L("knT", [128, 8, NKMAX], BF16)
        krT = L("krT", [128, NKMAX], BF16)
        memset("dve", krT, 0.0, "krT")
        Vt = L("Vt", [128, NKMAX // 128, 8, 128], BF16)
        onesb = L("onesb", [128, 128], BF16)
        memset("dve", onesb, 1.0, "onesb")
        kvt_b = Rot([L("s4kv%d" % i, [128, 2048], BF16) for i in range(2)], "s4kv")
        krt_b = Rot([L("s4kr%d" % i, [128, 64], BF16) for i in range(2)], "s4kr")
        qt_b = Rot([L("s4q%d" % i, [128, 1536], BF16) for i in range(2)], "s4q")
        qnT_b = Rot([L("s4qn%d" % i, [128, 8, 512], BF16) for i in range(2)], "s4qn")
        qrT_b = Rot([L("s4qr%d" % i, [128, 8, 512], BF16) for i in range(2)], "s4qr")
        for i_ in range(2):
            memset("dve", qrT_b.aps[i_], 0.0, "s4qr%d" % i_)
        pT_b = Rot([L("s4p%d" % i, [128, 512], BF16) for i in range(4)], "s4p")
        pacc_b = Rot([L("s4pa%d" % i, [128, 512], F32) for i in range(2)], "s4pa")
        pab_b = Rot([L("s4pb%d" % i, [128, 512], BF16) for i in range(2)], "s4pb")
        rc_b = Rot([L("s4r%d" % i, [128, 512], F32) for i in range(2)], "s4r")
        ao_b = Rot([L("s4a%d" % i, [128, 512], BF16) for i in range(3)], "s4a")
        scale = 1.0 / math.sqrt(192.0)
        pst["n"] = 6
        pst["i"] = 0
        for si, (r0s, Ls, lat) in enumerate(seqs):
            nk = Ls + (PAST if lat else 0)
            nkt = nk // 128
            kr0 = r0s
            for kt in range(nkt):
                kvt, kvk = kvt_b.next(); load(kvt, KVS[kr0 + kt * 128:kr0 + (kt + 1) * 128, :], kvk)
                krt, krk = krt_b.next(); load(krt, KRS[kr0 + kt * 128:kr0 + (kt + 1) * 128, :], krk)
                pb, pk = bank(); pbb = pb.bitcast(BF16)
                for h in range(8):
                    S.op("pe", lambda e, h=h, pbb=pbb, kvt=kvt: e.transpose(pbb[:, h * 128:(h + 1) * 128], kvt[:, h * 256:h * 256 + 128], ident),
                         reads=[kvk, "ident"], writes=[pk])
                copy("act", knT[:, :, kt * 128:(kt + 1) * 128], pbb.rearrange("p (h k) -> p h k", h=8), [pk], ["knT"])
                pb2, pk2 = bank(); pbb2 = pb2.bitcast(BF16)
                S.op("pe", lambda e, pbb2=pbb2, krt=krt: e.transpose(pbb2[0:64, 0:128], krt, ident), reads=[krk, "ident"], writes=[pk2])
                copy("dve", krT[0:64, kt * 128:(kt + 1) * 128], pbb2[0:64, 0:128], [pk2], ["krT"])
                copy("dve", Vt[:, kt, :, :], kvt.rearrange("p (h d) -> p h d", d=256)[:, :, 128:256], [kvk], ["Vt"])
            if lat:
                qblocks = [[NPT + i for i in range(OWN)], [NPT + OWN, NPT + NTS - 1]]
            else:
                qblocks = [[r0s // 128 + i for i in range(Ls // 128)]]
            for qb_tiles in qblocks:
                nq = len(qb_tiles)
                n = nq * 128
                qn, qnk = qnT_b.next(); qr, qrk = qrT_b.next()
                for qi in range(nq):
                    r = qb_tiles[qi] * 128
                    qt, qk = qt_b.next(); load(qt, QS[r:r + 128, :], qk)
                    pb, pk = bank(); pbb = pb.bitcast(BF16)
                    for h in range(8):
                        S.op("pe", lambda e, h=h, pbb=pbb, qt=qt: e.transpose(pbb[:, h * 128:(h + 1) * 128], qt[:, h * 192:h * 192 + 128], ident),
                             reads=[qk, "ident"], writes=[pk])
                    copy("act", qn[:, :, qi * 128:(qi + 1) * 128], pbb.rearrange("p (h k) -> p h k", h=8), [pk], [qnk])
                    pb2, pk2 = bank(); pbb2 = pb2.bitcast(BF16)
                    for h in range(8):
                        S.op("pe", lambda e, h=h, pbb2=pbb2, qt=qt: e.transpose(pbb2[0:64, h * 128:(h + 1) * 128], qt[:, h * 192 + 128:h * 192 + 192], ident),
                             reads=[qk, "ident"], writes=[pk2])
                    copy("dve", qr[0:64, :, qi * 128:(qi + 1) * 128], pbb2[0:64, :].rearrange("p (h k) -> p h k", h=8), [pk2], [qrk])
                for h in range(8):
                    po, pok = bank_acc()
                    pacc, pack = pacc_b.next()
                    def score(kt, h=h, qn=qn, qr=qr, n=n):
                        ps_, psk = bank()
                        S.op("pe", lambda e: e.matmul(
                            ps_[:, 0:n], knT[:, h, kt * 128:(kt + 1) * 128], qn[:, h, 0:n], start=True, stop=False),
                            reads=["knT", qnk], writes=[psk])
                        S.op("pe", lambda e: e.matmul(
                            ps_[:, 0:n], krT[:, kt * 128:(kt + 1) * 128], qr[:, h, 0:n], start=False, stop=True),
                            reads=["krT", qrk], writes=[psk])
                        pt, ptk = pT_b.next()
                        S.op("act", lambda e: e.activation(pt[:, 0:n], ps_[:, 0:n], AF.Exp, scale=scale),
                             reads=[psk], writes=[ptk])
                        return pt, ptk
                    SK = 2
                    pend = [score(kt) for kt in range(min(SK, nkt))]
                    for kt in range(nkt):
                        pt, ptk = pend.pop(0)
                        if kt + SK < nkt:
                            pend.append(score(kt + SK))
                        S.op("pe", lambda e, kt=kt, h=h, po=po, pt=pt, n=n: e.matmul(
                            po[:, 0:n], Vt[:, kt, h, :], pt[:, 0:n], start=(kt == 0), stop=(kt == nkt - 1)),
                            reads=[ptk, "Vt"], writes=[pok])
                        if kt == 0:
                            copy("dve", pacc[:, 0:n], pt[:, 0:n], [ptk], [pack])
                        else:
                            tt("dve", pacc[:, 0:n], pacc[:, 0:n], pt[:, 0:n], ALU.add, [pack, ptk], [pack])
                    pab, pabk = pab_b.next()
                    copy("dve", pab[:, 0:n], pacc[:, 0:n], [pack], [pabk])
                    psm, psmk = bank()
                    S.op("pe", lambda e, psm=psm, pab=pab, n=n: e.matmul(psm[:, 0:n], onesb, pab[:, 0:n], start=True, stop=True),
                         reads=["onesb", pabk], writes=[psmk])
                    rc, rck = rc_b.next()
                    S.op("dve", lambda e, rc=rc, psm=psm, n=n: e.reciprocal(rc[:, 0:n], psm[:, 0:n]), reads=[psmk], writes=[rck])
                    ao, aok = ao_b.next()
                    tt("dve", ao[:, 0:n], po[:, 0:n], rc[:, 0:n], ALU.mult, [pok, rck], [aok])
                    for qi in range(nq):
                        g_ = qb_tiles[qi]
                        store(ATTT[h][:, g_ * 128:(g_ + 1) * 128], ao[:, qi * 128:(qi + 1) * 128], aok, [("ATTT", h, g_)])
            S.barrier()
            if STOP[0] == 5:
                S.finish(); return nc, S
        pst["n"] = 8

    with ExitStack() as es:
        def L(name, shape, dt):
            return es.enter_context(nc.sbuf_tensor(name, list(shape), dt)).ap()
        cwb = [L("cwb%d" % i, [128, 1536], F32) for i in range(3)]
        cbb = L("cbb", [128, 1536], F32)
        for i in range(3):
            load(cwb[i], cw[i].partition_broadcast(128), "cwb%d" % i)
        load(cbb, cb.partition_broadcast(128), "cbb")
        dtbb = L("dtbb", [128, 32], F32); Ab = L("Ab", [128, 32], F32)
        load(dtbb, dtb.partition_broadcast(128), "dtbb")
        load(Ab, alog.partition_broadcast(128), "Ab")
        S.op("act", lambda e: e.activation(Ab, Ab, AF.Exp), reads=["Ab"], writes=["Ab"])
        S.op("dve", lambda e: e.tensor_scalar(Ab, Ab, -1.0, None, ALU.mult), reads=["Ab"], writes=["Ab"])
        mfb = L("mfb", [32, 2], F32)
        memset("pool", mfb, 1.0, "mfb")
        asel(mfb[:, 0:1], "mfb", [[0, 1]], ALU.is_gt, 0.0, 16, -1)
        S.op("pool", lambda e: e.affine_select(mfb[:, 1:2], mfb[:, 1:2], [[0, 1]], ALU.is_ge, 0.0, base=-16, channel_multiplier=1),
             reads=["mfb"], writes=["mfb"])
        S.op("dve", lambda e: e.tensor_scalar(mfb[:, 1:2], mfb[:, 1:2], -1.0, None, ALU.mult), reads=["mfb"], writes=["mfb"])
        a_b = [Rot([L("s5a%d_%d" % (j, i), [128, 1536], F32) for i in range(2)], "s5a%d_" % j) for j in range(3)]
        t_b = Rot([L("s5t%d" % i, [128, 1536], F32) for i in range(2)], "s5t")
        t2_b = Rot([L("s5u%d" % i, [128, 1536], F32) for i in range(2)], "s5u")
        xo_b = Rot([L("s5x%d" % i, [128, 1536], BF16) for i in range(2)], "s5x")
        d_b = Rot([L("s5d%d" % i, [128, 32], F32) for i in range(2)], "s5d")
        w_b = Rot([L("s5w%d" % i, [128, 8, 32], F32) for i in range(2)], "s5w")
        so_b = Rot([L("s5s%d" % i, [128, 160], F32) for i in range(2)], "s5s")
        g_b = Rot([L("s5g%d" % i, [32, 128], F32) for i in range(2)], "s5g")
        g2_b = Rot([L("s5h%d" % i, [32, 128], F32) for i in range(2)], "s5h")
        def body(gt):
            pr = prow(gt)
            yield
            r = gt * 128
            yield
            av = []
            yield
            si5, ti5 = seq_of_tile(gt)
            lat5 = seqs[si5][2]
            for j in range(3):
                a, ak_ = a_b[j].next()
                if lat5 and j == 0 and ti5 == 0:
                    prl = prow(NPT + NTS - 1) + 127
                    load(a[0:1, :], PROJ[prl:prl + 1, C_X:C_X + 1536], ak_)
                    load(a[1:128, :], PROJ[pr:pr + 127, C_X:C_X + 1536], ak_)
                elif lat5 and j == 2 and ti5 == NTS - 1:
                    prf = prow(NPT)
                    load(a[0:127, :], PROJ[pr + 1:pr + 128, C_X:C_X + 1536], ak_)
                    load(a[127:128, :], PROJ[prf:prf + 1, C_X:C_X + 1536], ak_)
                else:
                    load(a, PROJ[pr - 1 + j:pr - 1 + j + 128, C_X:C_X + 1536], ak_)
                if lat5 and j == 0 and ti5 % OWN == 0:
                    b5 = ti5 // OWN
                    S.op("dve", lambda e, a=a, b5=b5: e.tensor_scalar(a, a, mkt[:, b5:b5 + 1], None, ALU.mult), reads=[ak_, "mkt"], writes=[ak_])
                if lat5 and j == 2 and (ti5 + 1) % OWN == 0:
                    b5 = ((ti5 + 1) % NTS) // OWN
                    S.op("dve", lambda e, a=a, b5=b5: e.tensor_scalar(a, a, mkt[:, 4 + b5:5 + b5], None, ALU.mult), reads=[ak_, "mkt"], writes=[ak_])
                av.append((a, ak_))
            yield
            t, tk = t_b.next(); t2, t2k = t2_b.next()
            yield
            tt("dve", t, av[1][0], cwb[1], ALU.mult, [av[1][1], "cwb1"], [tk])
            yield
            tt("pool", t2, av[0][0], cwb[0], ALU.mult, [av[0][1], "cwb0"], [t2k])
            yield
            tt("dve", t, t, t2, ALU.add, [tk, t2k], [tk])
            yield
            tt("pool", t2, av[2][0], cwb[2], ALU.mult, [av[2][1], "cwb2"], [t2k])
            yield
            tt("dve", t, t, t2, ALU.add, [tk, t2k], [tk])
            yield
            tt("dve", t, t, cbb, ALU.add, [tk, "cbb"], [tk])
            yield
            xo, xok = xo_b.next()
            yield
            S.op("act", lambda e, xo=xo, t=t: e.activation(xo, t, AF.Silu), reads=[tk], writes=[xok])
            yield
            store(XBC[r:r + 128, :], xo, xok, [("XBC", gt)])
            yield
            dt_, dk = d_b.next(); load(dt_, PROJ[pr:pr + 128, C_DT:C_DT + 32], dk)
            yield
            w, wk = w_b.next()
            yield
            V, E, DT, LN, A_, BL, GM, TMP = [w[:, i, :] for i in range(8)]
            yield
            tt("dve", V, dt_, dtbb, ALU.add, [dk, "dtbb"], [wk])
            yield
            S.op("act", lambda e, E=E, V=V: e.activation(E, V, AF.Exp), reads=[wk], writes=[wk])
            yield
            S.op("act", lambda e, E=E, DT=DT: e.activation(DT, E, AF.Ln, bias=oneT), reads=[wk, "oneT"], writes=[wk])
            yield
            S.op("act", lambda e, LN=LN, DT=DT: e.activation(LN, DT, AF.Ln), reads=[wk], writes=[wk])
            yield
            tt("dve", A_, DT, Ab, ALU.mult, [wk, "Ab"], [wk])
            yield
            pb, pk = bank()
            yield
            S.op("pe", lambda e, pb=pb, A_=A_: e.matmul(pb[:, 0:16], tincl, A_[:, 0:16], start=True, stop=True), reads=["tincl", wk], writes=[pk])
            yield
            S.op("pe", lambda e, pb=pb, A_=A_: e.matmul(pb[:, 16:32], texcl, A_[:, 16:32], start=True, stop=True), reads=["texcl", wk], writes=[pk])
            yield
            S.op("pe", lambda e, pb=pb, A_=A_: e.matmul(pb[:, 32:64], ones32, A_, start=True, stop=True), reads=["ones32", wk], writes=[pk])
            yield
            copy("dve", GM[:, 0:16], pb[:, 0:16], [pk], [wk])
            yield
            S.op("dve", lambda e, GM=GM, pb=pb: e.tensor_scalar(GM[:, 16:32], pb[:, 16:32], -1.0, None, ALU.mult), reads=[pk], writes=[wk])
            yield
            so, sok = so_b.next()
            yield
            tt("dve", so[:, 0:32], LN, GM, ALU.subtract, [wk], [sok])
            yield
            tt("dve", TMP[:, 0:16], so[:, 0:16], pb[:, 32:48], ALU.add, [sok, pk], [wk])
            yield
            copy("dve", TMP[:, 16:32], so[:, 16:32], [sok], [wk])
            yield
            S.op("act", lambda e, so=so, TMP=TMP: e.activation(so[:, 32:64], TMP, AF.Exp), reads=[wk], writes=[sok])
            yield
            copy("dve", TMP[:, 0:16], GM[:, 0:16], [wk], [wk])
            yield
            tt("dve", TMP[:, 16:32], GM[:, 16:32], pb[:, 48:64], ALU.add, [wk, pk], [wk])
            yield
            S.op("act", lambda e, so=so, TMP=TMP: e.activation(so[:, 64:96], TMP, AF.Exp), reads=[wk], writes=[sok])
            yield
            S.op("act", lambda e, so=so, pb=pb: e.activation(so[:, 96:128], pb[:, 32:64], AF.Exp), reads=[pk], writes=[sok])
            yield
            copy("dve", so[:, 128:144], A_[:, 0:16], [wk], [sok])
            yield
            S.op("dve", lambda e, so=so, A_=A_: e.tensor_scalar(so[:, 144:160], A_[:, 16:32], -1.0, None, ALU.mult), reads=[wk], writes=[sok])
            yield
            store(STT[r:r + 128, :], so, sok, [("STT", gt)])
            yield
        pipeline(body, range(NT), PDEPTH)
        S.barrier()
        if STOP[0] == 6:
            S.finish(); return nc, S

    with ExitStack() as es:
        def L(name, shape, dt):
            return es.enter_context(nc.sbuf_tensor(name, list(shape), dt)).ap()
        hst_b = Rot([L("hst%d" % i, [64, D], F32) for i in range(2)], "hst")
        hbf_b = Rot([L("s6hb%d" % i, [64, D], BF16) for i in range(2)], "s6hb")
        xb_b = Rot([L("s6x%d" % i, [128, 1536], BF16) for i in range(4)], "s6x")
        so_b = Rot([L("s6s%d" % i, [128, 160], F32) for i in range(4)], "s6s")
        xw_b = Rot([L("s6w%d" % i, [128, D], BF16) for i in range(4)], "s6w")
        sti_b = Rot([L("sti%d" % i, [64, 16, 64], F32) for i in range(2)], "sti")
        sto_b = Rot([L("sto%d" % i, [64, 16, 64], F32) for i in range(2)], "sto")
        h0_b = Rot([L("h0t%d" % i, [64, D], F32) for i in range(2)], "h0t")
        def chain(arg):
            si, d = arg
            r0s, Ls, lat = seqs[si]
            nch = Ls // 128
            hst, hstk = hst_b.next()
            sti, stik = sti_b.next()
            sto, stok = sto_b.next()
            h0t, h0k = h0_b.next()
            yield
            if True:
                if lat:
                    load(sti, st_in[d].rearrange("h p n -> p h n"), stik)
                    for half in range(2):
                        pb, pk = bank(); pb2, pk2 = bank()
                        for hh in range(8):
                            h = half * 8 + hh
                            tgt = pb if hh < 4 else pb2
                            S.op("pe", lambda e, h=h, hh=hh, tgt=tgt: e.transpose(tgt[0:64, (hh % 4) * 64:(hh % 4) * 64 + 64], sti[:, h, :], ident32[0:64, 0:64]),
                                 reads=[stik, "ident32"], writes=[pk if hh < 4 else pk2])
                        copy("dve", h0t[:, half * 512:half * 512 + 256], pb[0:64, 0:256], [pk], [h0k])
                        copy("dve", h0t[:, half * 512 + 256:half * 512 + 512], pb2[0:64, 0:256], [pk2], [h0k])
                    copy("dve", hst, h0t, [h0k], [hstk])
                else:
                    memset("dve", hst, 0.0, hstk)
                order = list(range(nch)) if d == 0 else list(range(nch - 1, -1, -1))
                if lat:
                    order = order + order
                def prep(c):
                    r_ = (r0s // 128 + c) * 128
                    xb_, xbk = xb_b.next(); load(xb_, XBC[r_:r_ + 128, :], xbk)
                    so, sok = so_b.next(); load(so, STT[r_:r_ + 128, :], sok)
                    xw, xwk = xw_b.next()
                    tt("pool", xw.rearrange("p (h q) -> p h q", h=16), xb_[:, 0:1024].rearrange("p (h q) -> p h q", h=16),
                       so[:, 32 + d * 16:48 + d * 16].unsqueeze(2).to_broadcast([128, 16, 64]), ALU.mult, [xbk, sok], [xwk])
                    pA, pAk = bank(); pB_, pBk = bank()
                    for g in range(4):
                        tgt, tk_ = (pA, pAk) if g < 2 else (pB_, pBk)
                        S.op("pe", lambda e, g=g, tgt=tgt, xb_=xb_, xw=xw: e.matmul(
                            tgt[0:64, (g % 2) * 256:(g % 2) * 256 + 256], xb_[:, 1024 + g * 64:1024 + (g + 1) * 64], xw[:, g * 256:(g + 1) * 256], start=True, stop=True),
                            reads=[xbk, xwk], writes=[tk_])
                    return so, sok, pA, pAk, pB_, pBk
                pend_ = prep(order[0])
                yield
                for oi_, c in enumerate(order):
                    gt = r0s // 128 + c
                    so, sok, pA, pAk, pB_, pBk = pend_
                    if lat:
                        bb_ = None
                        if d == 0 and c % OWN == 0:
                            bb_ = c // OWN
                        if d == 1 and (c + 1) % OWN == 0:
                            bb_ = ((c + 1) % NTS) // OWN
                        if bb_ is not None:
                            S.op("dve", lambda e, bb_=bb_: e.tensor_scalar(hst, hst, mkt[0:64, 8 + bb_:9 + bb_], None, ALU.mult),
                                 reads=[hstk, "mkt"], writes=[hstk])
                            stt("dve", hst, h0t, mkt[0:64, 12 + bb_:13 + bb_], hst, ALU.mult, ALU.add, [h0k, "mkt", hstk], [hstk])
                    if (not lat) or oi_ >= nch:
                        hb, hbk = hbf_b.next()
                        copy("act", hb, hst, [hstk], [hbk])
                        store(HIN[gt, d], hb, hbk, [("HIN", gt, d)])
                    yield
                    if oi_ + 1 < len(order):
                        pend_ = prep(order[oi_ + 1])
                        yield
                    tt("dve", hst.rearrange("p (h q) -> p h q", h=16), hst.rearrange("p (h q) -> p h q", h=16),
                       so[0:64, 96 + d * 16:112 + d * 16].unsqueeze(2).to_broadcast([64, 16, 64]), ALU.mult, [hstk, sok], [hstk])
                    tt("dve", hst[:, 0:512], hst[:, 0:512], pA[0:64, :], ALU.add, [hstk, pAk], [hstk])
                    tt("dve", hst[:, 512:1024], hst[:, 512:1024], pB_[0:64, :], ALU.add, [hstk, pBk], [hstk])
                    yield
                if not lat:
                    for half in range(2):
                        pb, pk = bank(); pb2, pk2 = bank()
                        for hh in range(8):
                            h = half * 8 + hh
                            tgt = pb if hh < 4 else pb2
                            S.op("pe", lambda e, h=h, hh=hh, tgt=tgt: e.transpose(tgt[0:64, (hh % 4) * 64:(hh % 4) * 64 + 64], hst[:, h * 64:(h + 1) * 64], ident32[0:64, 0:64]),
                                 reads=[hstk, "ident32"], writes=[pk if hh < 4 else pk2])
                        copy("dve", sto[:, half * 8:half * 8 + 4, :], pb[0:64, 0:256].rearrange("p (h n) -> p h n", h=4), [pk], [stok])
                        copy("dve", sto[:, half * 8 + 4:half * 8 + 8, :], pb2[0:64, 0:256].rearrange("p (h n) -> p h n", h=4), [pk2], [stok])
                    S.dma("sp", nssd[si, d].rearrange("h p n -> p h n"), sto, stok, reads=[stok], writes=[("nssd", si, d)])
        pipeline(chain, [(si_, d_) for si_ in range(NSEQ) for d_ in range(2)], PDEPTH)
        S.barrier()
        if STOP[0] == 7:
            S.finish(); return nc, S
    with ExitStack() as es:
        def L(name, shape, dt):
            return es.enter_context(nc.sbuf_tensor(name, list(shape), dt)).ap()
        xb_b = Rot([L("s6ox%d" % i, [128, 1536], BF16) for i in range(2)], "s6ox")
        so_b = Rot([L("s6os%d" % i, [128, 160], F32) for i in range(2)], "s6os")
        mneg = L("mneg", [128, 2, 4, 128], F32)
        memset("pool", mneg, 0.0, "mneg")
        asel(mneg[:, 0], "mneg", [[0, 4], [1, 128]], ALU.is_ge, NEG, 0, -1)
        asel(mneg[:, 1], "mneg", [[0, 4], [-1, 128]], ALU.is_ge, NEG, 0, 1)
        abc_b = Rot([L("abc%d" % i, [128, 32, 128], F32) for i in range(2)], "abc")
        Dbc = L("Dbc", [128, 16], F32); load(Dbc, Dv.partition_broadcast(128), "Dbc")
        gnb = L("gnb", [128, D], F32); load(gnb, gn.partition_broadcast(128), "gnb")
        gm_b = Rot([L("s6g%d" % i, [32, 128], F32) for i in range(2)], "s6g")
        hin_b = [Rot([L("s6i%d_%d" % (d, i), [128, D], BF16) for i in range(2)], "s6i%d_" % d) for d in range(2)]
        z_b = Rot([L("s6z%d" % i, [128, D], F32) for i in range(2)], "s6z")
        bc_b = Rot([L("s6bc%d" % i, [128, 512], BF16) for i in range(2)], "s6bc")
        scs_b = Rot([L("s6sc%d" % i, [128, 4, 128], BF16) for i in range(2)], "s6sc")
        lt_b = Rot([L("s6l%d" % i, [128, 4, 128], BF16) for i in range(2)], "s6l")
        mt_b = Rot([L("s6m%d" % i, [128, 16, 128], BF16) for i in range(2)], "s6m")
        mb_b = Rot([L("s6n%d" % i, [128, 4, 128], BF16) for i in range(2)], "s6n")
        y_b = Rot([L("s6y%d" % i, [128, D], F32) for i in range(2)], "s6y")
        y2_b = Rot([L("s6v%d" % i, [128, D], F32) for i in range(2)], "s6v")
        y3_b = Rot([L("s6u%d" % i, [128, D], F32) for i in range(2)], "s6u")
        yo_b = Rot([L("s6o%d" % i, [128, D], BF16) for i in range(2)], "s6o")
        st_b = Rot([L("s6t%d" % i, [128, 4], F32) for i in range(2)], "s6t")
        def body(gt):
            r = gt * 128
            yield
            pr = prow(gt)
            yield
            xb_, xbk = xb_b.next(); load(xb_, XBC[r:r + 128, :], xbk)
            yield
            so, sok = so_b.next(); load(so, STT[r:r + 128, :], sok)
            yield
            abc, abck = abc_b.next()
            yield
            copy("dve", abc, so[:, 128:160].unsqueeze(2).to_broadcast([128, 32, 128]), [sok], [abck])
            yield
            hin = []
            yield
            for d in range(2):
                hi, hik = hin_b[d].next()
                load(hi[0:64, :], HIN[gt, d], hik); load(hi[64:128, :], HIN[gt, d], hik)
                hin.append((hi, hik))
            yield
            z, zk = z_b.next(); load(z, PROJ[pr:pr + 128, C_Z:C_Z + 1024], zk)
            yield
            bc, bck = bc_b.next()
            yield
            transpose_to(bc, bck, xb_[:, 1024:1536], xbk, 4, "act")
            yield
            pS2 = [bank(), bank()]
            yield
            scs, sck = scs_b.next()
            yield
            for g in range(4):
                p0 = (g % 2) * 64
                pS, pSk = pS2[g % 2]
                S.op("pe", lambda e, g=g, p0=p0, pS=pS, bc=bc: e.matmul(
                    pS[:, (g // 2) * 128:(g // 2) * 128 + 128], bc[p0:p0 + 64, (g // 2) * 128:(g // 2) * 128 + 128],
                    bc[p0:p0 + 64, (2 + g // 2) * 128:(2 + g // 2) * 128 + 128], start=True, stop=True),
                    reads=[bck], writes=[pSk])
            yield
            for g in range(4):
                pS, pSk = pS2[g % 2]
                copy("dve", scs[:, g, :], pS[:, (g // 2) * 128:(g // 2) * 128 + 128], [pSk], [sck])
            yield
            mt, mtk = mt_b.next()
            yield
            for d in range(2):
                for g in range(4):
                    pL, pLk = bank()
                    S.op("pe", lambda e, d=d, pL=pL: e.matmul(pL, ident32, mneg[:, d].rearrange("p a k -> p (a k)"), start=True, stop=False),
                         reads=["ident32", "mneg"], writes=[pLk])
                    for hh in range(4):
                        dh = d * 16 + g * 4 + hh
                        S.op("pe", lambda e, hh=hh, dh=dh, pL=pL, d=d: e.matmul(
                            pL[:, hh * 128:(hh + 1) * 128], abc[:, dh, :], (tincl if d == 0 else texcl), start=False, stop=(hh == 3)),
                            reads=[abck, "tincl", "texcl"], writes=[pLk])
                    lt, ltk = lt_b.next()
                    for hh in range(4):
                        dh = d * 16 + g * 4 + hh
                        S.op("act", lambda e, hh=hh, dh=dh, lt=lt, pL=pL, so=so: e.activation(
                            lt[:, hh, :], pL[:, hh * 128:(hh + 1) * 128], AF.Exp, bias=so[:, dh:dh + 1]),
                            reads=[pLk, sok], writes=[ltk])
                    if d == 0:
                        tt("dve", mt[:, g * 4:(g + 1) * 4, :], lt, scs[:, g:g + 1, :].to_broadcast([128, 4, 128]), ALU.mult, [ltk, sck], [mtk])
                    else:
                        mb_, mbk = mb_b.next()
                        tt("dve", mb_, lt, scs[:, g:g + 1, :].to_broadcast([128, 4, 128]), ALU.mult, [ltk, sck], [mbk])
                        tt("pool", mt[:, g * 4:(g + 1) * 4, :], mt[:, g * 4:(g + 1) * 4, :], mb_, ALU.add, [mtk, mbk], [mtk])
            yield
            pY = [bank(), bank()]
            yield
            for h in range(16):
                tgt, tk_ = pY[h // 8]
                S.op("pe", lambda e, h=h, tgt=tgt, mt=mt, xb_=xb_: e.matmul(
                    tgt[:, (h % 8) * 64:(h % 8) * 64 + 64], mt[:, h, :], xb_[:, h * 64:(h + 1) * 64], start=True, stop=True),
                    reads=[mtk, xbk], writes=[tk_])
            yield
            y, yk = y_b.next()
            yield
            copy("act", y[:, 0:512], pY[0][0], [pY[0][1]], [yk])
            yield
            copy("act", y[:, 512:1024], pY[1][0], [pY[1][1]], [yk])
            yield
            y2, y2k = y2_b.next()
            yield
            for d in range(2):
                hi, hik = hin[d]
                pZ = [bank(), bank()]
                for g in range(4):
                    p0 = (g % 2) * 64
                    tgt, tk_ = pZ[g % 2]
                    S.op("pe", lambda e, g=g, p0=p0, tgt=tgt, bc=bc, hi=hi: e.matmul(
                        tgt[:, (g // 2) * 256:(g // 2) * 256 + 256], bc[p0:p0 + 64, (2 + g // 2) * 128:(2 + g // 2) * 128 + 128],
                        hi[p0:p0 + 64, g * 256:(g + 1) * 256], start=True, stop=True),
                        reads=[bck, hik], writes=[tk_])
                for g in range(4):
                    tgt, tk_ = pZ[g % 2]
                    src = tgt[:, (g // 2) * 256:(g // 2) * 256 + 256].rearrange("p (h q) -> p h q", h=4)
                    scv = so[:, 64 + d * 16 + g * 4:64 + d * 16 + g * 4 + 4].unsqueeze(2).to_broadcast([128, 4, 64])
                    if d == 0:
                        tt("dve", y2[:, g * 256:(g + 1) * 256].rearrange("p (h q) -> p h q", h=4), src, scv, ALU.mult, [tk_, sok], [y2k])
                    else:
                        y3, y3k = y3_b.next()
                        tt("dve", y3[:, 0:256].rearrange("p (h q) -> p h q", h=4), src, scv, ALU.mult, [tk_, sok], [y3k])
                        tt("pool", y2[:, g * 256:(g + 1) * 256], y2[:, g * 256:(g + 1) * 256], y3[:, 0:256], ALU.add, [y2k, y3k], [y2k])
            yield
            y3, y3k = y3_b.next()
            yield
            tt("dve", y3.rearrange("p (h q) -> p h q", h=16), xb_[:, 0:1024].rearrange("p (h q) -> p h q", h=16),
               Dbc.unsqueeze(2).to_broadcast([128, 16, 64]), ALU.mult, [xbk, "Dbc"], [y3k])
            yield
            tt("pool", y2, y2, y3, ALU.add, [y2k, y3k], [y2k])
            yield
            tt("dve", y, y, y2, ALU.add, [yk, y2k], [yk])
            yield
            S.op("act", lambda e, z=z: e.activation(z, z, AF.Silu), reads=[zk], writes=[zk])
            yield
            tt("dve", y, y, z, ALU.mult, [yk, zk], [yk])
            yield
            st, stk = st_b.next()
            yield
            rr = rstd_of(st, stk, y, yk, D, y2, y2k)
            yield
            yo, yok = yo_b.next()
            yield
            stt("dve", yo, y, rr, gnb, ALU.mult, ALU.mult, [yk, stk, "gnb"], [yok])
            yield
            store(YN[r:r + 128, :], yo, yok, [("YN", gt)])
            yield
        pipeline(body, TWIN, PDEPTH)
        S.barrier()
        if STOP[0] == 8:
            S.finish(); return nc, S

    with ExitStack() as es:
        def L(name, shape, dt):
            return es.enter_context(nc.sbuf_tensor(name, list(shape), dt)).ap()
        wm = L("wm", [128, 8, D], BF16); ws_ = L("ws", [128, 8, D], BF16); wo = L("wo", [128, 8, D], BF16)
        loadc(wm, w_o_mla.rearrange("(k p) n -> p k n", p=128), "wm")
        loadc(ws_, w_o_ssd.rearrange("(k p) n -> p k n", p=128), "ws")
        loadc(wo, w_out.rearrange("(k p) n -> p k n", p=128), "wo")
        at_b = Rot([L("s7a%d" % i, [128, D], BF16) for i in range(2)], "s7a")
        yn_b = Rot([L("s7y%d" % i, [128, D], BF16) for i in range(2)], "s7y")
        aT_b = Rot([L("s7at%d" % i, [128, D], BF16) for i in range(2)], "s7at")
        yT_b = Rot([L("s7yt%d" % i, [128, D], BF16) for i in range(2)], "s7yt")
        g_b = Rot([L("s7g%d" % i, [128, 2 * D], F32) for i in range(2)], "s7g")
        x_b = Rot([L("s7x%d" % i, [128, D], F32) for i in range(2)], "s7x")
        m1_b = Rot([L("s7m%d" % i, [128, D], F32) for i in range(2)], "s7m")
        m2_b = Rot([L("s7n%d" % i, [128, D], F32) for i in range(2)], "s7n")
        mg_b = Rot([L("s7mg%d" % i, [128, D], BF16) for i in range(2)], "s7mg")
        mT_b = Rot([L("s7mt%d" % i, [128, D], BF16) for i in range(2)], "s7mt")
        hb_b = Rot([L("s7h%d" % i, [128, D], BF16) for i in range(2)], "s7h")
        hT_b = Rot([L("s7ht%d" % i, [128, D], BF16) for i in range(2)], "s7ht")
        st_b = Rot([L("s7s%d" % i, [128, 4], F32) for i in range(2)], "s7s")

        def mm2(lhsT, lk, W, wkey):
            res = []
            for nb in range(2):
                pb, pk = bank()
                for k in range(8):
                    S.op("pe", lambda e, k=k, nb=nb, pb=pb: e.matmul(
                        pb, lhsT[:, k * 128:(k + 1) * 128], W[:, k, nb * 512:(nb + 1) * 512], start=(k == 0), stop=(k == 7)),
                        reads=[lk, wkey], writes=[pk])
                res.append((pb, pk))
            return res

        def body(gt):
            si, _ = seq_of_tile(gt)
            yield
            m = seqs[si][2]
            yield
            r = gt * 128
            yield
            pr = prow(gt)
            yield
            yn, ynk = yn_b.next(); load(yn, YN[r:r + 128, :], ynk)
            yield
            gg, gk = g_b.next(); load(gg, PROJ[pr:pr + 128, C_GM:C_GM + 2048], gk)
            yield
            xt, xk = x_b.next(); load(xt, xrows(gt), xk)
            yield
            aT, aTk = aT_b.next(); load(aT.rearrange("p (h t) -> p h t", h=8), ATTT[:, :, r:r + 128].rearrange("h p t -> p h t"), aTk)
            yield
            yT, yTk = yT_b.next(); transpose_to(yT, yTk, yn, ynk, 8, "dve")
            yield
            S.op("act", lambda e, gg=gg: e.activation(gg, gg, AF.Sigmoid), reads=[gk], writes=[gk])
            yield
            om = mm2(aT, aTk, wm, "wm")
            yield
            m1, m1k = m1_b.next()
            yield
            for nb in range(2):
                tt("dve", m1[:, nb * 512:(nb + 1) * 512], om[nb][0], gg[:, nb * 512:(nb + 1) * 512], ALU.mult, [om[nb][1], gk], [m1k])
            yield
            os_ = mm2(yT, yTk, ws_, "ws")
            yield
            m2, m2k = m2_b.next()
            yield
            for nb in range(2):
                tt("dve", m2[:, nb * 512:(nb + 1) * 512], os_[nb][0], gg[:, D + nb * 512:D + (nb + 1) * 512], ALU.mult, [os_[nb][1], gk], [m2k])
            yield
            mg, mgk = mg_b.next()
            yield
            tt("pool", mg, m1, m2, ALU.add, [m1k, m2k], [mgk])
            yield
            mT, mTk = mT_b.next(); transpose_to(mT, mTk, mg, mgk, 8, "act")
            yield
            op_ = mm2(mT, mTk, wo, "wo")
            yield
            for nb in range(2):
                tt("dve", m1[:, nb * 512:(nb + 1) * 512], op_[nb][0], modp(m, G1)[:, nb * 512:(nb + 1) * 512], ALU.mult, [op_[nb][1], "mod%d" % m], [m1k])
            yield
            tt("pool", xt, xt, m1, ALU.add, [xk, m1k], [xk])
            yield
            S.dma("pool", XMID[r:r + 128, :], xt, xk, reads=[xk], writes=[("XMID", gt)])
            yield
            st, stk = st_b.next(); hb, hk = hb_b.next(); hT, hTk = hT_b.next()
            yield
            rr = rstd_of(st, stk, xt, xk, D, m2, m2k)
            yield
            stt("dve", m2, xt, rr, modp(m, GSC2), ALU.mult, ALU.mult, [xk, stk, "mod%d" % m], [m2k])
            yield
            tt("pool", hb, m2, modp(m, SH2), ALU.add, [m2k, "mod%d" % m], [hk])
            yield
            transpose_to(hT, hTk, hb, hk, 8, "act")
            yield
            store(H2T[gt], hT, hTk, [("H2T", gt)])
            yield
        pipeline(body, TWIN, PDEPTH)
        S.barrier()
        if STOP[0] == 9:
            S.finish(); return nc, S

    blocks = []
    for si, (r0s, Ls, lat) in enumerate(seqs):
        if lat:
            blocks.append((NPT, OWN, NPT + NTS - 1, NPT + OWN, (8, 9)))
        else:
            assert Ls // 128 <= 4
            blocks.append((r0s // 128, Ls // 128, None, None, None))
    NCOL = sum(b_[1] * 128 + 2 for b_ in blocks)
    ACTT = scr("ACTT", [22, 128, R], BF16)
    es_wd = ExitStack()
    wd = es_wd.enter_context(nc.sbuf_tensor("wd", [128, 22, D], BF16)).ap()
    with ExitStack() as es:
        def L(name, shape, dt):
            return es.enter_context(nc.sbuf_tensor(name, list(shape), dt)).ap()
        h2all = L("h2all", [128, 8, NCOL], BF16)
        memset("dve", h2all, 0.0, "h2all")
        bcols = []
        c = 0
        for (gt0, ntl, lsrc, rsrc, mcols) in blocks:
            bcols.append(c)
            for ti in range(ntl):
                load(h2all[:, :, c + 1 + ti * 128:c + 1 + (ti + 1) * 128], H2T[gt0 + ti].rearrange("p (k t) -> p k t", k=8), "h2all")
            if lsrc is not None:
                load(h2all[:, :, c:c + 1], H2T[lsrc].rearrange("p (k t) -> p k t", k=8)[:, :, 127:128], "h2all",
                     allow_slow_non_contiguous=True)
                S.op("dve", lambda e, c=c, mc=mcols[0]: e.tensor_scalar(h2all[:, :, c:c + 1], h2all[:, :, c:c + 1], mkt[:, mc:mc + 1], None, ALU.mult),
                     reads=["h2all", "mkt"], writes=["h2all"])
            if rsrc is not None:
                ce = c + ntl * 128 + 1
                load(h2all[:, :, ce:ce + 1], H2T[rsrc].rearrange("p (k t) -> p k t", k=8)[:, :, 0:1], "h2all",
                     allow_slow_non_contiguous=True)
                S.op("dve", lambda e, ce=ce, mc=mcols[1]: e.tensor_scalar(h2all[:, :, ce:ce + 1], h2all[:, :, ce:ce + 1], mkt[:, mc:mc + 1], None, ALU.mult),
                     reads=["h2all", "mkt"], writes=["h2all"])
            c += ntl * 128 + 2
        fwT = L("fwT", [128, 2, 22, 3], F32)
        fbT = L("fbT", [128, 2, 22], F32)
        for t_ in range(2):
            for k in range(3):
                load(fwT[:, t_, :, k], fw[k, t_ * DFF:(t_ + 1) * DFF].rearrange("(j p) -> p j", p=128), "fwT", allow_slow_non_contiguous=True)
            load(fbT[:, t_, :], fb[t_ * DFF:(t_ + 1) * DFF].rearrange("(j p) -> p j", p=128), "fbT", allow_slow_non_contiguous=True)
        wg_b = Rot([L("s8w%d" % i, [128, 2, 8, 128], BF16) for i in range(3)], "s8w")
        ue_b = [Rot([L("s8u%d_%d" % (t_, i), [128, 514], F32) for i in range(2)], "s8u%d_" % t_) for t_ in range(2)]
        tc_b = [Rot([L("s8t%d_%d" % (t_, i), [128, 512], F32) for i in range(2)], "s8t%d_" % t_) for t_ in range(2)]
        sg_b = Rot([L("s8s%d" % i, [128, 512], F32) for i in range(2)], "s8s")
        ao_b = Rot([L("s8a%d" % i, [128, 512], BF16) for i in range(3)], "s8a")
        def s8_wload(j):
            wt, wk = wg_b.next()
            loadc(wt[:, 0], w_up[:, j * 128:(j + 1) * 128].rearrange("(k p) n -> p k n", p=128), wk)
            loadc(wt[:, 1], w_up[:, DFF + j * 128:DFF + (j + 1) * 128].rearrange("(k p) n -> p k n", p=128), wk)
            return wt, wk
        s8_next = s8_wload(0)
        for j in range(22):
            wt, wk = s8_next
            if j + 1 < 22:
                s8_next = s8_wload(j + 1)
            loadc(wd[:, j, :], w_down[j * 128:(j + 1) * 128, :], "wd")
            for bi, (gt0, ntl, lsrc_, rsrc_, mcols_) in enumerate(blocks):
                n = ntl * 128
                c = bcols[bi]
                tcs = []
                for t_ in range(2):
                    pa, pak = bank()
                    ph, phk = bank()
                    for k in range(8):
                        S.op("pe", lambda e, k=k, t_=t_, pa=pa, wt=wt, c=c, n=n: e.matmul(
                            pa[:, 0:n], wt[:, t_, k, :], h2all[:, k, c + 1:c + 1 + n], start=(k == 0), stop=(k == 7)),
                            reads=[wk, "h2all"], writes=[pak])
                    for k in range(8):
                        S.op("pe", lambda e, k=k, t_=t_, ph=ph, wt=wt, c=c, n=n: e.matmul(
                            ph[:, 0:2], wt[:, t_, k, :], h2all[:, k, c:c + n + 2:n + 1], start=(k == 0), stop=(k == 7)),
                            reads=[wk, "h2all"], writes=[phk])
                    ue, uek = ue_b[t_].next()
                    copy("act", ue[:, 1:n + 1], pa[:, 0:n], [pak], [uek])
                    copy("dve", ue[:, 0:n + 2:n + 1], ph[:, 0:2], [phk], [uek])
                    tcv, tck = tc_b[t_].next()
                    S.op("act", lambda e, tcv=tcv, pa=pa, t_=t_, j=j, n=n: e.activation(
                        tcv[:, 0:n], pa[:, 0:n], AF.Copy, scale=fwT[:, t_, j, 1:2]), reads=[pak, "fwT"], writes=[tck])
                    stt("dve", tcv[:, 0:n], ue[:, 0:n], fwT[:, t_, j, 0:1], tcv[:, 0:n], ALU.mult, ALU.add, [uek, "fwT", tck], [tck])
                    stt("dve", tcv[:, 0:n], ue[:, 2:n + 2], fwT[:, t_, j, 2:3], tcv[:, 0:n], ALU.mult, ALU.add, [uek, "fwT", tck], [tck])
                    tcs.append((tcv, tck))
                sg, sgk = sg_b.next()
                S.op("act", lambda e, sg=sg, tcv=tcs[0][0], j=j, n=n: e.activation(sg[:, 0:n], tcv[:, 0:n], AF.Silu, bias=fbT[:, 0, j:j + 1]),
                     reads=[tcs[0][1], "fbT"], writes=[sgk])
                ao, aok = ao_b.next()
                stt("dve", ao[:, 0:n], tcs[1][0][:, 0:n], fbT[:, 1, j:j + 1], sg[:, 0:n], ALU.add, ALU.mult, [tcs[1][1], "fbT", sgk], [aok])
                store(ACTT[j][:, gt0 * 128:gt0 * 128 + n], ao[:, 0:n], aok, [("ACTT", j, bi)])
        S.barrier()
        if STOP[0] == 10:
            S.finish(); return nc, S

    with ExitStack() as es:
        def L(name, shape, dt):
            return es.enter_context(nc.sbuf_tensor(name, list(shape), dt)).ap()
        fngb = L("fngb", [128, D], F32); load(fngb, fng.partition_broadcast(128), "fngb")
        aT_b = Rot([L("s10at%d" % i, [128, 22, 128], BF16) for i in range(3)], "s10at")
        x_b = Rot([L("s10x%d" % i, [128, D], F32) for i in range(3)], "s10x")
        o_b = Rot([L("s10o%d" % i, [128, D], F32) for i in range(3)], "s10o")
        st_b = Rot([L("s10s%d" % i, [128, 4], F32) for i in range(3)], "s10s")
        def body(gt):
            si, _ = seq_of_tile(gt)
            yield
            m = seqs[si][2]
            yield
            r = gt * 128
            yield
            aT, aTk = aT_b.next(); load(aT, ACTT[:, :, r:r + 128].rearrange("j p t -> p j t"), aTk)
            yield
            xt, xk = x_b.next(); load(xt, XMID[r:r + 128, :], xk)
            yield
            ot, ok = o_b.next()
            yield
            for nb in range(2):
                pb, pk = bank()
                for k in range(22):
                    S.op("pe", lambda e, k=k, nb=nb, pb=pb, aT=aT: e.matmul(
                        pb, aT[:, k, :], wd[:, k, nb * 512:(nb + 1) * 512], start=(k == 0), stop=(k == 21)),
                        reads=[aTk, "wd"], writes=[pk])
                tt("dve", ot[:, nb * 512:(nb + 1) * 512], pb, modp(m, G2)[:, nb * 512:(nb + 1) * 512], ALU.mult, [pk, "mod%d" % m], [ok])
            yield
            tt("pool", xt, xt, ot, ALU.add, [xk, ok], [xk])
            yield
            st, stk = st_b.next()
            yield
            rr = rstd_of(st, stk, xt, xk, D, ot, ok)
            yield
            stt("dve", ot, xt, rr, fngb, ALU.mult, ALU.mult, [xk, stk, "fngb"], [ok])
            yield
            S.dma("sp", yrows(gt), ot, ok, reads=[ok], writes=[("Y", gt)])
            yield
        pipeline(body, TOWN, 3)
        S.barrier()
    es_wd.close()
    S.finish()
    return nc, S


def rope_tables(LS):
    t = np.arange(LS)
    row = (t // 64).astype(np.float32)
    col = (t % 64).astype(np.float32)
    n = 16
    inv = (10000.0 ** (-np.arange(n, dtype=np.float32) / n)).astype(np.float32)
    ang = np.stack([row[:, None] * inv, col[:, None] * inv], axis=1).reshape(LS, 32).astype(np.float32)
    return np.cos(ang).astype(np.float32), np.sin(ang).astype(np.float32)


_CACHE = {}


def run(inputs, NP, LP, LS, PAST, n_cores=8):
    key = (NP, LP, LS, PAST)
    if key not in _CACHE:
        _CACHE[key] = build(NP, LP, LS, PAST)[0]
    nc = _CACHE[key]
    f = lambda a: np.ascontiguousarray(np.asarray(a, dtype=np.float32))
    cos_t, sin_t = rope_tables(LS)
    shared = {
        "w_ada": f(inputs["w_ada"][0]), "b_ada": f(inputs["b_ada"][0]), "ga": f(inputs["norm_attn_g"][0]),
        "w_in": f(inputs["w_in"][0]), "qg": f(inputs["q_norm_g"][0]), "kvg": f(inputs["kv_norm_g"][0]),
        "w_uq": f(inputs["w_uq"][0]), "w_ukv": f(inputs["w_ukv"][0]), "w_o_mla": f(inputs["w_o_mla"][0]),
        "cw": f(inputs["ssd_conv_w"][0]), "cb": f(inputs["ssd_conv_b"][0]),
        "dtb": f(inputs["ssd_dt_bias"][0]).reshape(32), "alog": f(inputs["ssd_A_log"][0]).reshape(32),
        "Dv": f(inputs["ssd_D"][0]), "gn": f(inputs["ssd_norm_g"][0]), "w_o_ssd": f(inputs["w_o_ssd"][0]),
        "w_out": f(inputs["w_out"][0]), "gf": f(inputs["norm_ffn_g"][0]), "w_up": f(inputs["w_up"][0]),
        "fw": f(inputs["ffn_conv_w"][0]), "fb": f(inputs["ffn_conv_b"][0]), "w_down": f(inputs["w_down"][0]),
        "fng": f(inputs["final_norm_g"]), "cos_t": cos_t, "sin_t": sin_t,
    }
    xpr = f(inputs["x_prompt"]); xsm = f(inputs["x_sample"]); c = f(inputs["c"]); cctx = f(inputs["c_ctx"])
    in_maps = []
    G4 = 4
    NTS = LS // 128
    OWN = NTS // G4
    rots = []
    for core in range(n_cores):
        sq = (core // G4) % xsm.shape[0]
        rr = core % G4
        rot = OWN * rr
        rots.append((sq, rot))
        d = dict(shared)
        d["xp"] = np.ascontiguousarray(xpr[core * NP:(core + 1) * NP].reshape(NP * LP, D))
        d["xs"] = np.ascontiguousarray(np.roll(xsm[sq], -rot * 128, axis=0))
        d["cos_t"] = np.ascontiguousarray(np.roll(cos_t, -rot * 128, axis=0))
        d["sin_t"] = np.ascontiguousarray(np.roll(sin_t, -rot * 128, axis=0))
        mkv = np.ones((128, 16), np.float32)
        for b_ in range(G4):
            m_ = 0.0 if (b_ + rr) % G4 == 0 else 1.0
            mkv[0, b_] = m_
            mkv[127, 4 + b_] = m_
            mkv[:, 8 + b_] = m_
            mkv[:, 12 + b_] = 1.0 - m_
        d["mk"] = mkv
        d["cvec"] = np.ascontiguousarray(np.stack([cctx, c[sq]], 0))
        d["cckv"] = f(inputs["cache_ckv"][sq, 0]); d["ckr"] = f(inputs["cache_krope"][sq, 0])
        d["st"] = f(inputs["state_ssd"][sq, 0])
        in_maps.append(d)
    res = run_bass_kernel_spmd(nc, in_maps, core_ids=list(range(n_cores)))
    outs = res.results
    if DEBUG[0]:
        DBG_OUT.append(outs)
    B = xpr.shape[0]
    y_prompt = np.concatenate([outs[i]["yp"].reshape(NP, LP, D) for i in range(n_cores)], 0)[:B]
    y_sample = np.zeros(xsm.shape, np.float32)
    for core in range(n_cores):
        sq, rot = rots[core]
        y_sample[sq, rot * 128:(rot + OWN) * 128] = outs[core]["ys"]
    new_ckv = np.concatenate([outs[i]["nckv"].reshape(NP, 1, LP, 256) for i in range(n_cores)], 0)[:B]
    new_kr = np.concatenate([outs[i]["nkr"].reshape(NP, 1, LP, 64) for i in range(n_cores)], 0)[:B]
    new_ssd = np.concatenate([outs[i]["nssd"].reshape(NP, 1, 2, 16, 64, 64) for i in range(n_cores)], 0)[:B]
    return (y_prompt.astype(np.float32), y_sample.astype(np.float32), new_ckv.astype(np.float32),
            new_kr.astype(np.float32), new_ssd.astype(np.float32))


def kernel(**inputs):
    return run(inputs, 4, 256, 2048, 512, 8)
```

```python
import math
from contextlib import ExitStack, nullcontext
import numpy as np
import concourse.bass as bass
import concourse.mybir as mybir
from concourse.bass_utils import run_bass_kernel_spmd

F32 = mybir.dt.float32
BF16 = mybir.dt.bfloat16
AF = mybir.ActivationFunctionType
ALU = mybir.AluOpType

D = 1024
IN_COLS = 5216
C_CQ, C_CKV, C_KR, C_Z, C_X, C_DT, C_GM = 0, 256, 512, 576, 1600, 3136, 3168
DFF = 2816
EPS = 1e-6
NEG = -30000.0


class Sched:
    def __init__(self, nc):
        self.nc = nc
        self.e = {"pe": nc.tensor, "act": nc.scalar, "dve": nc.vector,
                  "pool": nc.gpsimd, "sp": nc.sync}
        self.sems = {}
        self.cnt = {}
        for k in ("pe", "act", "dve", "pool"):
            self.sems[k] = nc.alloc_semaphore("c_" + k)
            self.cnt[k] = 0
        self.seen = {k: {} for k in self.e}
        self.lastw = {}
        self.readers = {}
        self.nins = 0
        self.phys = {}
        self.physq = {}
        self.free = {}
        self.nphys = 0

    def _sem(self, sk, q):
        if sk not in self.phys:
            fl = self.free.setdefault(q, [])
            if fl:
                pid = fl.pop()
            else:
                pid = "d_%d" % self.nphys
                self.nphys += 1
                self.sems[pid] = self.nc.alloc_semaphore(pid)
                self.cnt[pid] = 0
            self.phys[sk] = pid
            self.physq[pid] = q
        return self.phys[sk]

    def _deps(self, reads, writes):
        best = {}

        def add(sk, v):
            if best.get(sk, 0) < v:
                best[sk] = v
        for k in reads:
            if k in self.lastw:
                add(*self.lastw[k])
        for k in writes:
            if k in self.lastw:
                add(*self.lastw[k])
            for sk, v in self.readers.get(k, {}).items():
                add(sk, v)
        return best

    def _wait(self, eng, best):
        for sk, v in best.items():
            if sk == eng and eng == "pe":
                continue
            if self.seen[eng].get(sk, 0) >= v:
                continue
            self.seen[eng][sk] = v
            self.e[eng].wait_ge(self.sems[sk], v)
            self.nins += 1

    def _record(self, tok, reads, writes):
        for k in writes:
            self.lastw[k] = tok
            self.readers[k] = {}
        for k in reads:
            r = self.readers.setdefault(k, {})
            if r.get(tok[0], 0) < tok[1]:
                r[tok[0]] = tok[1]

    def op(self, eng, fn, reads=(), writes=()):
        self._wait(eng, self._deps(reads, writes))
        ins = fn(self.e[eng])
        self.cnt[eng] += 1
        ins.then_inc(self.sems[eng], 1)
        self.nins += 1
        self._record((eng, self.cnt[eng]), reads, writes)

    def dma(self, q, out, in_, key, reads=(), writes=(), **kw):
        self._wait(q, self._deps(reads, writes))
        pid = self._sem("dma:%s:%s" % (q, key), q)
        ins = self.e[q].dma_start(out=out, in_=in_, **kw)
        self.cnt[pid] += 16
        ins.then_inc(self.sems[pid], 16)
        self.nins += 1
        self._record((pid, self.cnt[pid]), reads, writes)

    def _all(self):
        best = {}
        for sk, v in self.lastw.values():
            if best.get(sk, 0) < v:
                best[sk] = v
        for r in self.readers.values():
            for sk, v in r.items():
                if best.get(sk, 0) < v:
                    best[sk] = v
        return best

    def barrier(self):
        best = self._all()
        for eng in self.e:
            self._wait(eng, dict(best))
        self.lastw = {}
        self.readers = {}
        for pid in self.phys.values():
            self.free.setdefault(self.physq[pid], []).append(pid)
        self.phys = {}

    def finish(self):
        self._wait("sp", self._all())


PIPE = {"slot": None, "depth": 1}


class Rot:
    def __init__(self, aps, name):
        self.aps = aps
        self.name = name
        self.i = 0
        self.si = {}

    def next(self):
        sl = PIPE["slot"]
        d = PIPE["depth"]
        if sl is None or len(self.aps) < d:
            self.i = (self.i + 1) % len(self.aps)
            j = self.i
        else:
            idx = [i for i in range(len(self.aps)) if i % d == sl]
            c = (self.si.get(sl, 0) + 1) % len(idx)
            self.si[sl] = c
            j = idx[c]
        return self.aps[j], "%s%d" % (self.name, j)


def pipeline(make_body, items, depth):
    PIPE["depth"] = depth
    active = {}
    it = iter(items)
    done = False
    while True:
        for sl in range(depth):
            if sl not in active and not done:
                x = next(it, None)
                if x is None:
                    done = True
                else:
                    active[sl] = make_body(x)
        if not active:
            break
        for sl in sorted(active):
            PIPE["slot"] = sl
            try:
                next(active[sl])
            except StopIteration:
                del active[sl]
    PIPE["slot"] = None
    PIPE["depth"] = 1


STOP = [0]
PDEPTH = 2
SUB = [0]
DEBUG = [0]
DBG_OUT = []


def build(NP, LP, LS, PAST):
    nc = bass.Bass("TRN2", target_bir_lowering=False)
    S = Sched(nc)
    RPRM = NP * LP
    R = RPRM + LS
    NT = R // 128
    NPT = RPRM // 128
    NTS = LS // 128
    G4 = 4
    OWN = NTS // G4
    assert OWN * G4 == NTS and 1 <= OWN <= 4
    WIN = [NPT + NTS - 1] + [NPT + i for i in range(OWN + 1)]
    TWIN = list(range(NPT)) + WIN
    TWINS = set(TWIN)
    TOWN = list(range(NPT)) + [NPT + i for i in range(OWN)]
    seqs = [(i * LP, LP, 0) for i in range(NP)] + [(RPRM, LS, 1)]
    NSEQ = len(seqs)
    RP = R + 2 * NSEQ
    KRT = R + PAST

    def seq_of_tile(gt):
        r = gt * 128
        for si, (r0, L, m) in enumerate(seqs):
            if r0 <= r < r0 + L:
                return si, (r - r0) // 128
        raise ValueError

    def prow(gt):
        si, _ = seq_of_tile(gt)
        return gt * 128 + 2 * si + 1

    def krow(gt):
        si, _ = seq_of_tile(gt)
        return gt * 128 + (PAST if seqs[si][2] else 0)

    def din(name, shape):
        return nc.dram_tensor(name, list(shape), F32, kind="ExternalInput").ap()

    def dout(name, shape):
        return nc.dram_tensor(name, list(shape), F32, kind="ExternalOutput").ap()

    xp = din("xp", [RPRM, D]); xs = din("xs", [LS, D]); cvec = din("cvec", [2, D])
    cckv = din("cckv", [PAST, 256]); ckr = din("ckr", [PAST, 64]); st_in = din("st", [2, 16, 64, 64])
    w_ada = din("w_ada", [D, 6 * D]); b_ada = din("b_ada", [6 * D]); ga = din("ga", [D])
    w_in = din("w_in", [D, IN_COLS]); qg = din("qg", [256]); kvg = din("kvg", [256])
    w_uq = din("w_uq", [256, 1536]); w_ukv = din("w_ukv", [256, 2048]); w_o_mla = din("w_o_mla", [D, D])
    cw = din("cw", [3, 1536]); cb = din("cb", [1536]); dtb = din("dtb", [32]); alog = din("alog", [32])
    Dv = din("Dv", [16]); gn = din("gn", [D]); w_o_ssd = din("w_o_ssd", [D, D]); w_out = din("w_out", [D, D])
    gf = din("gf", [D]); w_up = din("w_up", [D, 2 * DFF]); fw = din("fw", [3, 2 * DFF]); fb = din("fb", [2 * DFF])
    w_down = din("w_down", [DFF, D]); fng = din("fng", [D])
    cos_t = din("cos_t", [LS, 32]); sin_t = din("sin_t", [LS, 32]); mk = din("mk", [128, 16])
    yp = dout("yp", [RPRM, D]); ys = dout("ys", [OWN * 128, D]); nckv = dout("nckv", [RPRM, 256])
    nkr = dout("nkr", [RPRM, 64]); nssd = dout("nssd", [NP, 2, 16, 64, 64])

    def xrows(gt):
        r = gt * 128
        return xp[r:r + 128, :] if r < RPRM else xs[r - RPRM:r - RPRM + 128, :]

    def yrows(gt):
        r = gt * 128
        return yp[r:r + 128, :] if r < RPRM else ys[r - RPRM:r - RPRM + 128, :]

    def scr(name, shape, dt):
        if DEBUG[0]:
            return nc.dram_tensor(name, list(shape), dt, kind="ExternalOutput").ap()
        return nc.dram_tensor(name, list(shape), dt).ap()
    HT = scr("HT", [NT, 128, D], BF16)
    PROJ = scr("PROJ", [RP, IN_COLS], F32)
    QS = scr("QS", [R, 1536], BF16)
    KVS = scr("KVS", [KRT, 2048], BF16)
    KRS = scr("KRS", [KRT, 64], BF16)
    XBC = scr("XBC", [R, 1536], BF16)
    STT = scr("STT", [R, 160], F32)
    GMT = scr("GMT", [NT, 32, 128], F32)
    HIN = scr("HIN", [NT, 2, 64, D], BF16)
    YN = scr("YN", [R, D], BF16)
    XMID = scr("XMID", [R, D], F32)
    H2T = scr("H2T", [NT, 128, D], BF16)

    def G(name, shape, dt):
        return nc.alloc_sbuf_tensor(name, list(shape), dt).ap()
    PB = [nc.alloc_psum_tensor("pb%d" % i, [128, 512], F32).ap() for i in range(8)]
    pst = {"i": 0, "n": 8, "a": 0}

    def bank():
        sl = PIPE["slot"]
        if sl is not None and PIPE["depth"] >= 2:
            nb_ = 8 // PIPE["depth"]
            k = "s%d" % sl
            pst[k] = (pst.get(k, 0) + 1) % nb_
            j = sl * nb_ + pst[k]
            return PB[j], "pb%d" % j
        pst["i"] = (pst["i"] + 1) % pst["n"]
        return PB[pst["i"]], "pb%d" % pst["i"]

    def bank_acc():
        pst["a"] = 1 - pst["a"]
        return PB[6 + pst["a"]], "pb%d" % (6 + pst["a"])

    ident = G("ident", [128, 128], BF16)
    ident32 = G("ident32", [128, 128], F32)
    ones32 = G("ones32", [128, 128], F32)
    tincl = G("tincl", [128, 128], F32)
    texcl = G("texcl", [128, 128], F32)
    epsT = G("epsT", [128, 1], F32)
    oneT = G("oneT", [128, 1], F32)
    zeroT = G("zeroT", [128, 1408], BF16)
    zero32 = G("zero32", [128, 1536], F32)
    MOD = [G("mod%d" % m, [128, 6 * D], F32) for m in range(2)]
    SH1, GSC1, G1, SH2, GSC2, G2 = range(6)

    def modp(m, part):
        return MOD[m][:, part * D:(part + 1) * D]

    def memset(eng, ap, val, key):
        S.op(eng, lambda e: e.memset(ap, val), writes=[key])

    def asel(ap, key, pattern, cmp, fill, base, cm):
        S.op("pool", lambda e: e.affine_select(ap, ap, pattern, cmp, fill, base=base, channel_multiplier=cm),
             reads=[key], writes=[key])

    memset("pool", ident, 1.0, "ident"); asel(ident, "ident", [[-1, 128]], ALU.is_equal, 0.0, 0, 1)
    memset("pool", ident32, 1.0, "ident32"); asel(ident32, "ident32", [[-1, 128]], ALU.is_equal, 0.0, 0, 1)
    memset("dve", ones32, 1.0, "ones32")
    memset("pool", tincl, 1.0, "tincl"); asel(tincl, "tincl", [[1, 128]], ALU.is_ge, 0.0, 0, -1)
    memset("pool", texcl, 1.0, "texcl"); asel(texcl, "texcl", [[1, 128]], ALU.is_gt, 0.0, 0, -1)
    memset("dve", epsT, EPS, "epsT"); memset("dve", oneT, 1.0, "oneT")
    mkt = G("mkt", [128, 16], F32)
    S.dma("sp", mkt, mk, "mkt", writes=["mkt"])
    memset("dve", zeroT, 0.0, "zeroT"); memset("dve", zero32, 0.0, "zero32")

    def copy(eng, out, in_, reads, writes):
        if eng == "act":
            S.op("act", lambda e: e.copy(out, in_), reads=reads, writes=writes)
        else:
            S.op(eng, lambda e: e.tensor_copy(out, in_), reads=reads, writes=writes)

    def tt(eng, out, a, b, op, reads, writes):
        S.op(eng, lambda e: e.tensor_tensor(out, a, b, op), reads=reads, writes=writes)

    def transpose_to(dst, dkey, src, skey, n, ceng="act", w=128):
        done = 0
        while done < n:
            m = min(8, n - done)
            pb, pk = bank()
            pbb = pb.bitcast(BF16)
            for k in range(m):
                S.op("pe", lambda e, k=k, done=done, pbb=pbb: e.transpose(
                    pbb[0:w, k * 128:(k + 1) * 128], src[:, (done + k) * w:(done + k + 1) * w], ident),
                    reads=[skey, "ident"], writes=[pk])
            copy(ceng, dst[0:w, done * 128:(done + m) * 128], pbb[0:w, 0:m * 128], [pk], [dkey])
            done += m

    def load(dst, src, key, reads=(), **kw):
        S.dma("sp", dst, src, key, reads=reads, writes=[key], **kw)

    def loadc(dst, src, key, reads=()):
        S.dma("pool", dst, src, key, reads=reads, writes=[key])

    def store(dst, src, key, writes):
        S.dma("pool", dst, src, key, reads=[key], writes=writes)

    def rstd_of(st, stk, src, skey, n, junk, jkey):
        S.op("act", lambda e: e.activation(junk, src, AF.Square, accum_out=st[:, 0:1]),
             reads=[skey], writes=[jkey, stk])
        S.op("act", lambda e: e.activation(st[:, 1:2], st[:, 0:1], AF.Ln, bias=epsT, scale=1.0 / n),
             reads=[stk, "epsT"], writes=[stk])
        S.op("act", lambda e: e.activation(st[:, 2:3], st[:, 1:2], AF.Exp, scale=-0.5),
             reads=[stk], writes=[stk])
        return st[:, 2:3]

    def stt(eng, out, a, sc, b, op0, op1, reads, writes):
        S.op(eng, lambda e: e.scalar_tensor_tensor(out, a, sc, b, op0, op1), reads=reads, writes=writes)

    es_keep = ExitStack()
    with nullcontext(es_keep) as es:
        def L(name, shape, dt):
            return es.enter_context(nc.sbuf_tensor(name, list(shape), dt)).ap()
        cT = L("cT", [128, 2, 8], F32)
        cS = L("cS", [128, 2, 8], F32)
        cB = L("cB", [128, 16, 128], BF16)
        wb = Rot([L("s0w%d" % i, [128, 8, 512], BF16) for i in range(2)], "s0w")
        bb = Rot([L("s0b%d" % i, [128, 512], F32) for i in range(2)], "s0b")
        gab = L("gab", [128, D], F32)
        gfb = L("gfb", [128, D], F32)
        load(cT, cvec.rearrange("m (k p) -> p m k", p=128), "cT", allow_slow_non_contiguous=True)
        load(gab, ga.partition_broadcast(128), "gab")
        load(gfb, gf.partition_broadcast(128), "gfb")
        S.op("act", lambda e: e.activation(cS, cT, AF.Silu), reads=["cT"], writes=["cS"])
        for m in range(2):
            for k in range(8):
                copy("dve", cB[:, m * 8 + k, :], cS[:, m, k:k + 1].to_broadcast([128, 128]), ["cS"], ["cB"])
        for j in range(12):
            wt, wk = wb.next()
            loadc(wt, w_ada[:, j * 512:(j + 1) * 512].rearrange("(k p) n -> p k n", p=128), wk)
            bt, bk = bb.next()
            load(bt, b_ada[j * 512:(j + 1) * 512].partition_broadcast(128), bk)
            for m in range(2):
                pb, pk = bank()
                for k in range(8):
                    S.op("pe", lambda e, k=k, m=m, pb=pb, wt=wt: e.matmul(pb, cB[:, m * 8 + k, :], wt[:, k, :], start=(k == 0), stop=(k == 7)),
                         reads=["cB", wk], writes=[pk])
                tt("dve", MOD[m][:, j * 512:(j + 1) * 512], pb, bt, ALU.add, [pk, bk], ["mod%d" % m])
        for m in range(2):
            stt("dve", modp(m, GSC1), modp(m, GSC1), 1.0, gab, ALU.add, ALU.mult, ["mod%d" % m, "gab"], ["mod%d" % m])
            stt("dve", modp(m, GSC2), modp(m, GSC2), 1.0, gfb, ALU.add, ALU.mult, ["mod%d" % m, "gfb"], ["mod%d" % m])
        if STOP[0] == 1:
            S.finish(); return nc, S

    with nullcontext(es_keep) as es:
        def L(name, shape, dt):
            return es.enter_context(nc.sbuf_tensor(name, list(shape), dt)).ap()
        xb = Rot([L("s1x%d" % i, [128, D], F32) for i in range(2)], "s1x")
        jb = Rot([L("s1j%d" % i, [128, D], F32) for i in range(2)], "s1j")
        hbb = Rot([L("s1h%d" % i, [128, D], BF16) for i in range(2)], "s1h")
        hTb = Rot([L("s1t%d" % i, [128, D], BF16) for i in range(2)], "s1t")
        stb = Rot([L("s1s%d" % i, [128, 4], F32) for i in range(2)], "s1s")
        def body(gt):
            si, _ = seq_of_tile(gt)
            yield
            m = seqs[si][2]
            yield
            xt, xk = xb.next(); load(xt, xrows(gt), xk)
            yield
            st, stk = stb.next(); junk, jk = jb.next(); hb, hk = hbb.next(); hT, hTk = hTb.next()
            yield
            r = rstd_of(st, stk, xt, xk, D, junk, jk)
            yield
            stt("dve", junk, xt, r, modp(m, GSC1), ALU.mult, ALU.mult, [xk, stk, "mod%d" % m], [jk])
            yield
            tt("pool", hb, junk, modp(m, SH1), ALU.add, [jk, "mod%d" % m], [hk])
            yield
            transpose_to(hT, hTk, hb, hk, 8, "act")
            yield
            store(HT[gt], hT, hTk, [("HT", gt)])
            yield
        pipeline(body, range(NT), PDEPTH)
        if STOP[0] == 2:
            S.finish(); return nc, S

    def proj_stage(SRC, W, groups, DST, dst_dt, tag):
        with ExitStack() as es:
            def L(name, shape, dt):
                return es.enter_context(nc.sbuf_tensor(name, list(shape), dt)).ap()
            wb = Rot([L(tag + "w%d" % i, [128, 8, 512], BF16) for i in range(3)], tag + "w")
            hall = L(tag + "hall", [128, NT, D], BF16)
            for gt in range(NT):
                load(hall[:, gt, :], SRC[gt], tag + "hall%d" % gt, reads=[("HT", gt)])
            ob = Rot([L(tag + "o%d" % i, [128, 512], dst_dt) for i in range(3)], tag + "o")
            blks = []
            for (g0, g1, tl) in groups:
                c0 = g0
                while c0 < g1:
                    cwid = min(512, g1 - c0)
                    blks.append((c0, cwid, tl))
                    c0 += cwid

            def pj_wload(bi):
                c0, cwid, tl = blks[bi]
                wt, wk = wb.next()
                loadc(wt[:, :, 0:cwid], W[:, c0:c0 + cwid].rearrange("(k p) n -> p k n", p=128), wk)
                return wt, wk
            pj_next = pj_wload(0)
            for bi, (c0, cwid, tl) in enumerate(blks):
                wt, wk = pj_next
                if bi + 1 < len(blks):
                    pj_next = pj_wload(bi + 1)
                for gt in tl:
                    ht, hk = hall[:, gt, :], tag + "hall%d" % gt
                    pb, pk = bank()
                    for k in range(8):
                        S.op("pe", lambda e, k=k, pb=pb, ht=ht, wt=wt, cwid=cwid: e.matmul(
                            pb[:, 0:cwid], ht[:, k * 128:(k + 1) * 128], wt[:, k, 0:cwid], start=(k == 0), stop=(k == 7)),
                            reads=[hk, wk], writes=[pk])
                    ot, ok = ob.next()
                    copy("act" if (gt + bi) % 2 else "dve", ot[:, 0:cwid], pb[:, 0:cwid], [pk], [ok])
                    pr = prow(gt)
                    store(DST[pr:pr + 128, c0:c0 + cwid], ot[:, 0:cwid], ok, [(tag, gt, bi)])
            S.barrier()
            if STOP[0] == 3:
                return True

    for si, (r0, Ls, m) in enumerate(seqs):
        for pr in (r0 + 2 * si, r0 + 2 * si + Ls + 1):
            S.dma("sp", PROJ[pr:pr + 1, C_X:C_X + 1536], zero32[0:1, :], "zero32", reads=["zero32"], writes=[("pad", pr)])
    ALLT = list(range(NT))
    s2_groups = [(C_CKV, C_Z, ALLT), (C_X, C_GM, ALLT), (C_CQ, C_CKV, TWIN), (C_Z, C_X, TWIN), (C_GM, IN_COLS, TWIN)]
    if proj_stage(HT, w_in, s2_groups, PROJ, F32, "s2"):
        S.finish(); return nc, S
    es_keep.close()

    with ExitStack() as es:
        def L(name, shape, dt):
            return es.enter_context(nc.sbuf_tensor(name, list(shape), dt)).ap()
        wuq = L("wuq", [128, 2, 1536], BF16); wukv = L("wukv", [128, 2, 2048], BF16)
        qgb = L("qgb", [128, 256], F32); kvgb = L("kvgb", [128, 256], F32)
        loadc(wuq, w_uq.rearrange("(k p) n -> p k n", p=128), "wuq")
        loadc(wukv, w_ukv.rearrange("(k p) n -> p k n", p=128), "wukv")
        load(qgb, qg.partition_broadcast(128), "qgb"); load(kvgb, kvg.partition_broadcast(128), "kvgb")
        inb = Rot([L("s3i%d" % i, [128, 576], F32) for i in range(3)], "s3i")
        stb = Rot([L("s3s%d" % i, [128, 8], F32) for i in range(3)], "s3s")
        jb = Rot([L("s3j%d" % i, [128, 256], F32) for i in range(3)], "s3j")
        cqb = Rot([L("s3c%d" % i, [128, 256], BF16) for i in range(3)], "s3c")
        cqT = Rot([L("s3ct%d" % i, [128, 256], BF16) for i in range(3)], "s3ct")
        qfb = Rot([L("s3q%d" % i, [128, 1536], F32) for i in range(3)], "s3q")
        qbb = Rot([L("s3qb%d" % i, [128, 1536], BF16) for i in range(3)], "s3qb")
        ckb = Rot([L("s3k%d" % i, [128, 256], F32) for i in range(3)], "s3k")
        ckbb = Rot([L("s3kb%d" % i, [128, 256], BF16) for i in range(3)], "s3kb")
        ckT = Rot([L("s3kt%d" % i, [128, 256], BF16) for i in range(3)], "s3kt")
        kvb = Rot([L("s3v%d" % i, [128, 2048], BF16) for i in range(3)], "s3v")
        krf = Rot([L("s3r%d" % i, [128, 64], F32) for i in range(3)], "s3r")
        krb = Rot([L("s3rb%d" % i, [128, 64], BF16) for i in range(3)], "s3rb")
        csb = Rot([L("s3cs%d" % i, [128, 64], F32) for i in range(3)], "s3cs")
        tmpb = Rot([L("s3t%d" % i, [128, 4, 256], F32) for i in range(3)], "s3t")

        def kv_from(ckn32, ck32k, key_row, kr_bf, kr_k):
            cbf, cbk = ckbb.next()
            copy("dve", cbf, ckn32, [ck32k], [cbk])
            ct, ctk = ckT.next()
            transpose_to(ct, ctk, cbf, cbk, 2, "act")
            kv, kvk = kvb.next()
            for nb in range(4):
                pb, pk = bank()
                for k in range(2):
                    S.op("pe", lambda e, k=k, nb=nb, pb=pb, ct=ct: e.matmul(
                        pb, ct[:, k * 128:(k + 1) * 128], wukv[:, k, nb * 512:(nb + 1) * 512], start=(k == 0), stop=(k == 1)),
                        reads=[ctk, "wukv"], writes=[pk])
                copy("act" if nb % 2 else "dve", kv[:, nb * 512:(nb + 1) * 512], pb, [pk], [kvk])
            store(KVS[key_row:key_row + 128, :], kv, kvk, [("KVS", key_row)])
            store(KRS[key_row:key_row + 128, :], kr_bf, kr_k, [("KRS", key_row)])

        def body(gt):
            si, ti = seq_of_tile(gt)
            yield
            r0s, Ls, lat = seqs[si]
            yield
            r = gt * 128
            yield
            pr = prow(gt)
            yield
            it, ik = inb.next()
            if gt in TWINS:
                load(it, PROJ[pr:pr + 128, 0:576], ik)
            else:
                load(it[:, 256:576], PROJ[pr:pr + 128, 256:576], ik)
            yield
            st, stk = stb.next(); junk, jk = jb.next()
            yield
            if lat:
                cs, csk = csb.next()
                t0 = ti * 128
                load(cs[:, 0:32], cos_t[t0:t0 + 128, :], csk); load(cs[:, 32:64], sin_t[t0:t0 + 128, :], csk)
            if gt in TWINS:
                yield
                rq = rstd_of(st, stk, it[:, 0:256], ik, 256, junk, jk)
                yield
                cq, cqk = cqb.next()
                yield
                stt("dve", cq, it[:, 0:256], rq, qgb, ALU.mult, ALU.mult, [ik, stk, "qgb"], [cqk])
                yield
                ct, ctk = cqT.next()
                yield
                transpose_to(ct, ctk, cq, cqk, 2, "act")
                yield
                qf, qfk = qfb.next()
                yield
                for nb in range(3):
                    pb, pk = bank()
                    for k in range(2):
                        S.op("pe", lambda e, k=k, nb=nb, pb=pb, ct=ct: e.matmul(
                            pb, ct[:, k * 128:(k + 1) * 128], wuq[:, k, nb * 512:(nb + 1) * 512], start=(k == 0), stop=(k == 1)),
                            reads=[ctk, "wuq"], writes=[pk])
                    copy("act" if nb % 2 else "dve", qf[:, nb * 512:(nb + 1) * 512], pb, [pk], [qfk])
                yield
                qb, qbk = qbb.next()
                yield
                copy("dve", qb, qf, [qfk], [qbk])
                yield
                if lat:
                    tmp, tk = tmpb.next()
                    qv = qf.rearrange("p (h d) -> p h d", d=192)[:, :, 128:192].rearrange("p h (a t n) -> p h a t n", a=2, t=2)
                    ov = qb.rearrange("p (h d) -> p h d", d=192)[:, :, 128:192].rearrange("p h (a t n) -> p h a t n", a=2, t=2)
                    x0 = qv[:, :, :, 0, :]; x1 = qv[:, :, :, 1, :]
                    cv = cs[:, 0:32].rearrange("p (a n) -> p a n", a=2).unsqueeze(1).to_broadcast([128, 8, 2, 16])
                    sv = cs[:, 32:64].rearrange("p (a n) -> p a n", a=2).unsqueeze(1).to_broadcast([128, 8, 2, 16])
                    tv = [tmp[:, i, :].rearrange("p (h a n) -> p h a n", h=8, a=2) for i in range(4)]
                    tt("dve", tv[0], x0, cv, ALU.mult, [qfk, csk], [tk])
                    tt("dve", tv[1], x1, sv, ALU.mult, [qfk, csk], [tk])
                    tt("dve", tv[2], x0, sv, ALU.mult, [qfk, csk], [tk])
                    tt("dve", tv[3], x1, cv, ALU.mult, [qfk, csk], [tk])
                    tt("dve", ov[:, :, :, 0, :], tv[0], tv[1], ALU.subtract, [tk], [qbk])
                    tt("dve", ov[:, :, :, 1, :], tv[2], tv[3], ALU.add, [tk], [qbk])
                yield
                store(QS[r:r + 128, :], qb, qbk, [("QS", gt)])
            yield
            rk = rstd_of(st[:, 4:8], stk, it[:, 256:512], ik, 256, junk, jk)
            yield
            ck, ckk = ckb.next()
            yield
            stt("dve", ck, it[:, 256:512], rk, kvgb, ALU.mult, ALU.mult, [ik, stk, "kvgb"], [ckk])
            yield
            kf, kfk = krf.next()
            yield
            if lat:
                tmp, tk = tmpb.next()
                kvw = it[:, 512:576].rearrange("p (a t n) -> p a t n", a=2, t=2)
                okv = kf.rearrange("p (a t n) -> p a t n", a=2, t=2)
                x0 = kvw[:, :, 0, :]; x1 = kvw[:, :, 1, :]
                cv = cs[:, 0:32].rearrange("p (a n) -> p a n", a=2)
                sv = cs[:, 32:64].rearrange("p (a n) -> p a n", a=2)
                tv = [tmp[:, i, 0:32].rearrange("p (a n) -> p a n", a=2) for i in range(4)]
                tt("dve", tv[0], x0, cv, ALU.mult, [ik, csk], [tk])
                tt("dve", tv[1], x1, sv, ALU.mult, [ik, csk], [tk])
                tt("dve", tv[2], x0, sv, ALU.mult, [ik, csk], [tk])
                tt("dve", tv[3], x1, cv, ALU.mult, [ik, csk], [tk])
                tt("dve", okv[:, :, 0, :], tv[0], tv[1], ALU.subtract, [tk], [kfk])
                tt("dve", okv[:, :, 1, :], tv[2], tv[3], ALU.add, [tk], [kfk])
            else:
                copy("dve", kf, it[:, 512:576], [ik], [kfk])
                S.dma("sp", nckv[r:r + 128, :], ck, ckk, reads=[ckk], writes=[("nckv", gt)])
                S.dma("sp", nkr[r:r + 128, :], kf, kfk, reads=[kfk], writes=[("nkr", gt)])
            yield
            kb, kbk = krb.next()
            yield
            copy("dve", kb, kf, [kfk], [kbk])
            yield
            kv_from(ck, ckk, krow(gt), kb, kbk)
            yield
        pipeline(body, range(NT), 3)
        for ct_i in range(PAST // 128):
            ck, ckk = ckb.next(); load(ck, cckv[ct_i * 128:(ct_i + 1) * 128, :], ckk)
            kf, kfk = krf.next(); load(kf, ckr[ct_i * 128:(ct_i + 1) * 128, :], kfk)
            kb, kbk = krb.next()
            copy("dve", kb, kf, [kfk], [kbk])
            kv_from(ck, ckk, RPRM + ct_i * 128, kb, kbk)
        S.barrier()
        if STOP[0] == 4:
            S.finish(); return nc, S

    NKMAX = PAST + LS
    ATTT = scr("ATTT", [8, 128, R], BF16)
    with ExitStack() as es:
        def L(name, shape, dt):
            return es.enter_context(nc.sbuf_tensor(name, list(shape), dt)).ap()
        knT = L("knT", [128, 8, NKMAX], BF16)
        krT = L("krT", [128, NKMAX], BF16)
        memset("dve", krT, 0.0, "krT")
        Vt = L("Vt", [128, NKMAX // 128, 8, 128], BF16)
        onesb = L("onesb", [128, 128], BF16)
        memset("dve", onesb, 1.0, "onesb")
        kvt_b = Rot([L("s4kv%d" % i, [128, 2048], BF16) for i in range(2)], "s4kv")
        krt_b = Rot([L("s4kr%d" % i, [128, 64], BF16) for i in range(2)], "s4kr")
        qt_b = Rot([L("s4q%d" % i, [128, 1536], BF16) for i in range(2)], "s4q")
        qnT_b = Rot([L("s4qn%d" % i, [128, 8, 512], BF16) for i in range(2)], "s4qn")
        qrT_b = Rot([L("s4qr%d" % i, [128, 8, 512], BF16) for i in range(2)], "s4qr")
        for i_ in range(2):
            memset("dve", qrT_b.aps[i_], 0.0, "s4qr%d" % i_)
        pT_b = Rot([L("s4p%d" % i, [128, 512], BF16) for i in range(4)], "s4p")
        pacc_b = Rot([L("s4pa%d" % i, [128, 512], F32) for i in range(2)], "s4pa")
        pab_b = Rot([L("s4pb%d" % i, [128, 512], BF16) for i in range(2)], "s4pb")
        rc_b = Rot([L("s4r%d" % i, [128, 512], F32) for i in range(2)], "s4r")
        ao_b = Rot([L("s4a%d" % i, [128, 512], BF16) for i in range(3)], "s4a")
        scale = 1.0 / math.sqrt(192.0)
        pst["n"] = 6
        pst["i"] = 0
        for si, (r0s, Ls, lat) in enumerate(seqs):
            nk = Ls + (PAST if lat else 0)
            nkt = nk // 128
            kr0 = r0s
            for kt in range(nkt):
                kvt, kvk = kvt_b.next(); load(kvt, KVS[kr0 + kt * 128:kr0 + (kt + 1) * 128, :], kvk)
                krt, krk = krt_b.next(); load(krt, KRS[kr0 + kt * 128:kr0 + (kt + 1) * 128, :], krk)
                pb, pk = bank(); pbb = pb.bitcast(BF16)
                for h in range(8):
                    S.op("pe", lambda e, h=h, pbb=pbb, kvt=kvt: e.transpose(pbb[:, h * 128:(h + 1) * 128], kvt[:, h * 256:h * 256 + 128], ident),
                         reads=[kvk, "ident"], writes=[pk])
                copy("act", knT[:, :, kt * 128:(kt + 1) * 128], pbb.rearrange("p (h k) -> p h k", h=8), [pk], ["knT"])
                pb2, pk2 = bank(); pbb2 = pb2.bitcast(BF16)
                S.op("pe", lambda e, pbb2=pbb2, krt=krt: e.transpose(pbb2[0:64, 0:128], krt, ident), reads=[krk, "ident"], writes=[pk2])
                copy("dve", krT[0:64, kt * 128:(kt + 1) * 128], pbb2[0:64, 0:128], [pk2], ["krT"])
                copy("dve", Vt[:, kt, :, :], kvt.rearrange("p (h d) -> p h d", d=256)[:, :, 128:256], [kvk], ["Vt"])
            if lat:
                qblocks = [[NPT + i for i in range(OWN)], [NPT + OWN, NPT + NTS - 1]]
            else:
                qblocks = [[r0s // 128 + i for i in range(Ls // 128)]]
            for qb_tiles in qblocks:
                nq = len(qb_tiles)
                n = nq * 128
                qn, qnk = qnT_b.next(); qr, qrk = qrT_b.next()
                for qi in range(nq):
                    r = qb_tiles[qi] * 128
                    qt, qk = qt_b.next(); load(qt, QS[r:r + 128, :], qk)
                    pb, pk = bank(); pbb = pb.bitcast(BF16)
                    for h in range(8):
                        S.op("pe", lambda e, h=h, pbb=pbb, qt=qt: e.transpose(pbb[:, h * 128:(h + 1) * 128], qt[:, h * 192:h * 192 + 128], ident),
                             reads=[qk, "ident"], writes=[pk])
                    copy("act", qn[:, :, qi * 128:(qi + 1) * 128], pbb.rearrange("p (h k) -> p h k", h=8), [pk], [qnk])
                    pb2, pk2 = bank(); pbb2 = pb2.bitcast(BF16)
                    for h in range(8):
                        S.op("pe", lambda e, h=h, pbb2=pbb2, qt=qt: e.transpose(pbb2[0:64, h * 128:(h + 1) * 128], qt[:, h * 192 + 128:h * 192 + 192], ident),
                             reads=[qk, "ident"], writes=[pk2])
                    copy("dve", qr[0:64, :, qi * 128:(qi + 1) * 128], pbb2[0:64, :].rearrange("p (h k) -> p h k", h=8), [pk2], [qrk])
                for h in range(8):
                    po, pok = bank_acc()
                    pacc, pack = pacc_b.next()
                    def score(kt, h=h, qn=qn, qr=qr, n=n):
                        ps_, psk = bank()
                        S.op("pe", lambda e: e.matmul(
                            ps_[:, 0:n], knT[:, h, kt * 128:(kt + 1) * 128], qn[:, h, 0:n], start=True, stop=False),
                            reads=["knT", qnk], writes=[psk])
                        S.op("pe", lambda e: e.matmul(
                            ps_[:, 0:n], krT[:, kt * 128:(kt + 1) * 128], qr[:, h, 0:n], start=False, stop=True),
                            reads=["krT", qrk], writes=[psk])
                        pt, ptk = pT_b.next()
                        S.op("act", lambda e: e.activation(pt[:, 0:n], ps_[:, 0:n], AF.Exp, scale=scale),
                             reads=[psk], writes=[ptk])
                        return pt, ptk
                    SK = 2
                    pend = [score(kt) for kt in range(min(SK, nkt))]
                    for kt in range(nkt):
                        pt, ptk = pend.pop(0)
                        if kt + SK < nkt:
                            pend.append(score(kt + SK))
                        S.op("pe", lambda e, kt=kt, h=h, po=po, pt=pt, n=n: e.matmul(
                            po[:, 0:n], Vt[:, kt, h, :], pt[:, 0:n], start=(kt == 0), stop=(kt == nkt - 1)),
                            reads=[ptk, "Vt"], writes=[pok])
                        if kt == 0:
                            copy("dve", pacc[:, 0:n], pt[:, 0:n], [ptk], [pack])
                        else:
                            tt("dve", pacc[:, 0:n], pacc[:, 0:n], pt[:, 0:n], ALU.add, [pack, ptk], [pack])
                    pab, pabk = pab_b.next()
                    copy("dve", pab[:, 0:n], pacc[:, 0:n], [pack], [pabk])
                    psm, psmk = bank()
                    S.op("pe", lambda e, psm=psm, pab=pab, n=n: e.matmul(psm[:, 0:n], onesb, pab[:, 0:n], start=True, stop=True),
                         reads=["onesb", pabk], writes=[psmk])
                    rc, rck = rc_b.next()
                    S.op("dve", lambda e, rc=rc, psm=psm, n=n: e.reciprocal(rc[:, 0:n], psm[:, 0:n]), reads=[psmk], writes=[rck])
                    ao, aok = ao_b.next()
                    tt("dve", ao[:, 0:n], po[:, 0:n], rc[:, 0:n], ALU.mult, [pok, rck], [aok])
                    for qi in range(nq):
                        g_ = qb_tiles[qi]
                        store(ATTT[h][:, g_ * 128:(g_ + 1) * 128], ao[:, qi * 128:(qi + 1) * 128], aok, [("ATTT", h, g_)])
            S.barrier()
            if STOP[0] == 5:
                S.finish(); return nc, S
        pst["n"] = 8

    with ExitStack() as es:
        def L(name, shape, dt):
            return es.enter_context(nc.sbuf_tensor(name, list(shape), dt)).ap()
        cwb = [L("cwb%d" % i, [128, 1536], F32) for i in range(3)]
        cbb = L("cbb", [128, 1536], F32)
        for i in range(3):
            load(cwb[i], cw[i].partition_broadcast(128), "cwb%d" % i)
        load(cbb, cb.partition_broadcast(128), "cbb")
        dtbb = L("dtbb", [128, 32], F32); Ab = L("Ab", [128, 32], F32)
        load(dtbb, dtb.partition_broadcast(128), "dtbb")
        load(Ab, alog.partition_broadcast(128), "Ab")
        S.op("act", lambda e: e.activation(Ab, Ab, AF.Exp), reads=["Ab"], writes=["Ab"])
        S.op("dve", lambda e: e.tensor_scalar(Ab, Ab, -1.0, None, ALU.mult), reads=["Ab"], writes=["Ab"])
        mfb = L("mfb", [32, 2], F32)
        memset("pool", mfb, 1.0, "mfb")
        asel(mfb[:, 0:1], "mfb", [[0, 1]], ALU.is_gt, 0.0, 16, -1)
        S.op("pool", lambda e: e.affine_select(mfb[:, 1:2], mfb[:, 1:2], [[0, 1]], ALU.is_ge, 0.0, base=-16, channel_multiplier=1),
             reads=["mfb"], writes=["mfb"])
        S.op("dve", lambda e: e.tensor_scalar(mfb[:, 1:2], mfb[:, 1:2], -1.0, None, ALU.mult), reads=["mfb"], writes=["mfb"])
        a_b = [Rot([L("s5a%d_%d" % (j, i), [128, 1536], F32) for i in range(2)], "s5a%d_" % j) for j in range(3)]
        t_b = Rot([L("s5t%d" % i, [128, 1536], F32) for i in range(2)], "s5t")
        t2_b = Rot([L("s5u%d" % i, [128, 1536], F32) for i in range(2)], "s5u")
        xo_b = Rot([L("s5x%d" % i, [128, 1536], BF16) for i in range(2)], "s5x")
        d_b = Rot([L("s5d%d" % i, [128, 32], F32) for i in range(2)], "s5d")
        w_b = Rot([L("s5w%d" % i, [128, 8, 32], F32) for i in range(2)], "s5w")
        so_b = Rot([L("s5s%d" % i, [128, 160], F32) for i in range(2)], "s5s")
        g_b = Rot([L("s5g%d" % i, [32, 128], F32) for i in range(2)], "s5g")
        g2_b = Rot([L("s5h%d" % i, [32, 128], F32) for i in range(2)], "s5h")
        def body(gt):
            pr = prow(gt)
            yield
            r = gt * 128
            yield
            av = []
            yield
            si5, ti5 = seq_of_tile(gt)
            lat5 = seqs[si5][2]
            for j in range(3):
                a, ak_ = a_b[j].next()
                if lat5 and j == 0 and ti5 == 0:
                    prl = prow(NPT + NTS - 1) + 127
                    load(a[0:1, :], PROJ[prl:prl + 1, C_X:C_X + 1536], ak_)
                    load(a[1:128, :], PROJ[pr:pr + 127, C_X:C_X + 1536], ak_)
                elif lat5 and j == 2 and ti5 == NTS - 1:
                    prf = prow(NPT)
                    load(a[0:127, :], PROJ[pr + 1:pr + 128, C_X:C_X + 1536], ak_)
                    load(a[127:128, :], PROJ[prf:prf + 1, C_X:C_X + 1536], ak_)
                else:
                    load(a, PROJ[pr - 1 + j:pr - 1 + j + 128, C_X:C_X + 1536], ak_)
                if lat5 and j == 0 and ti5 % OWN == 0:
                    b5 = ti5 // OWN
                    S.op("dve", lambda e, a=a, b5=b5: e.tensor_scalar(a, a, mkt[:, b5:b5 + 1], None, ALU.mult), reads=[ak_, "mkt"], writes=[ak_])
                if lat5 and j == 2 and (ti5 + 1) % OWN == 0:
                    b5 = ((ti5 + 1) % NTS) // OWN
                    S.op("dve", lambda e, a=a, b5=b5: e.tensor_scalar(a, a, mkt[:, 4 + b5:5 + b5], None, ALU.mult), reads=[ak_, "mkt"], writes=[ak_])
                av.append((a, ak_))
            yield
            t, tk = t_b.next(); t2, t2k = t2_b.next()
            yield
            tt("dve", t, av[1][0], cwb[1], ALU.mult, [av[1][1], "cwb1"], [tk])
            yield
            tt("pool", t2, av[0][0], cwb[0], ALU.mult, [av[0][1], "cwb0"], [t2k])
            yield
            tt("dve", t, t, t2, ALU.add, [tk, t2k], [tk])
            yield
            tt("pool", t2, av[2][0], cwb[2], ALU.mult, [av[2][1], "cwb2"], [t2k])
            yield
            tt("dve", t, t, t2, ALU.add, [tk, t2k], [tk])
            yield
            tt("dve", t, t, cbb, ALU.add, [tk, "cbb"], [tk])
            yield
            xo, xok = xo_b.next()
            yield
            S.op("act", lambda e, xo=xo, t=t: e.activation(xo, t, AF.Silu), reads=[tk], writes=[xok])
            yield
            store(XBC[r:r + 128, :], xo, xok, [("XBC", gt)])
            yield
            dt_, dk = d_b.next(); load(dt_, PROJ[pr:pr + 128, C_DT:C_DT + 32], dk)
            yield
            w, wk = w_b.next()
            yield
            V, E, DT, LN, A_, BL, GM, TMP = [w[:, i, :] for i in range(8)]
            yield
            tt("dve", V, dt_, dtbb, ALU.add, [dk, "dtbb"], [wk])
            yield
            S.op("act", lambda e, E=E, V=V: e.activation(E, V, AF.Exp), reads=[wk], writes=[wk])
            yield
            S.op("act", lambda e, E=E, DT=DT: e.activation(DT, E, AF.Ln, bias=oneT), reads=[wk, "oneT"], writes=[wk])
            yield
            S.op("act", lambda e, LN=LN, DT=DT: e.activation(LN, DT, AF.Ln), reads=[wk], writes=[wk])
            yield
            tt("dve", A_, DT, Ab, ALU.mult, [wk, "Ab"], [wk])
            yield
            pb, pk = bank()
            yield
            S.op("pe", lambda e, pb=pb, A_=A_: e.matmul(pb[:, 0:16], tincl, A_[:, 0:16], start=True, stop=True), reads=["tincl", wk], writes=[pk])
            yield
            S.op("pe", lambda e, pb=pb, A_=A_: e.matmul(pb[:, 16:32], texcl, A_[:, 16:32], start=True, stop=True), reads=["texcl", wk], writes=[pk])
            yield
            S.op("pe", lambda e, pb=pb, A_=A_: e.matmul(pb[:, 32:64], ones32, A_, start=True, stop=True), reads=["ones32", wk], writes=[pk])
            yield
            copy("dve", GM[:, 0:16], pb[:, 0:16], [pk], [wk])
            yield
            S.op("dve", lambda e, GM=GM, pb=pb: e.tensor_scalar(GM[:, 16:32], pb[:, 16:32], -1.0, None, ALU.mult), reads=[pk], writes=[wk])
            yield
            so, sok = so_b.next()
            yield
            tt("dve", so[:, 0:32], LN, GM, ALU.subtract, [wk], [sok])
            yield
            tt("dve", TMP[:, 0:16], so[:, 0:16], pb[:, 32:48], ALU.add, [sok, pk], [wk])
            yield
            copy("dve", TMP[:, 16:32], so[:, 16:32], [sok], [wk])
            yield
            S.op("act", lambda e, so=so, TMP=TMP: e.activation(so[:, 32:64], TMP, AF.Exp), reads=[wk], writes=[sok])
            yield
            copy("dve", TMP[:, 0:16], GM[:, 0:16], [wk], [wk])
            yield
            tt("dve", TMP[:, 16:32], GM[:, 16:32], pb[:, 48:64], ALU.add, [wk, pk], [wk])
            yield
            S.op("act", lambda e, so=so, TMP=TMP: e.activation(so[:, 64:96], TMP, AF.Exp), reads=[wk], writes=[sok])
            yield
            S.op("act", lambda e, so=so, pb=pb: e.activation(so[:, 96:128], pb[:, 32:64], AF.Exp), reads=[pk], writes=[sok])
            yield
            copy("dve", so[:, 128:144], A_[:, 0:16], [wk], [sok])
            yield
            S.op("dve", lambda e, so=so, A_=A_: e.tensor_scalar(so[:, 144:160], A_[:, 16:32], -1.0, None, ALU.mult), reads=[wk], writes=[sok])
            yield
            store(STT[r:r + 128, :], so, sok, [("STT", gt)])
            yield
        pipeline(body, range(NT), PDEPTH)
        S.barrier()
        if STOP[0] == 6:
            S.finish(); return nc, S

    with ExitStack() as es:
        def L(name, shape, dt):
            return es.enter_context(nc.sbuf_tensor(name, list(shape), dt)).ap()
        hst_b = Rot([L("hst%d" % i, [64, D], F32) for i in range(2)], "hst")
        hbf_b = Rot([L("s6hb%d" % i, [64, D], BF16) for i in range(2)], "s6hb")
        xb_b = Rot([L("s6x%d" % i, [128, 1536], BF16) for i in range(4)], "s6x")
        so_b = Rot([L("s6s%d" % i, [128, 160], F32) for i in range(4)], "s6s")
        xw_b = Rot([L("s6w%d" % i, [128, D], BF16) for i in range(4)], "s6w")
        sti_b = Rot([L("sti%d" % i, [64, 16, 64], F32) for i in range(2)], "sti")
        sto_b = Rot([L("sto%d" % i, [64, 16, 64], F32) for i in range(2)], "sto")
        h0_b = Rot([L("h0t%d" % i, [64, D], F32) for i in range(2)], "h0t")
        def chain(arg):
            si, d = arg
            r0s, Ls, lat = seqs[si]
            nch = Ls // 128
            hst, hstk = hst_b.next()
            sti, stik = sti_b.next()
            sto, stok = sto_b.next()
            h0t, h0k = h0_b.next()
            yield
            if True:
                if lat:
                    load(sti, st_in[d].rearrange("h p n -> p h n"), stik)
                    for half in range(2):
                        pb, pk = bank(); pb2, pk2 = bank()
                        for hh in range(8):
                            h = half * 8 + hh
                            tgt = pb if hh < 4 else pb2
                            S.op("pe", lambda e, h=h, hh=hh, tgt=tgt: e.transpose(tgt[0:64, (hh % 4) * 64:(hh % 4) * 64 + 64], sti[:, h, :], ident32[0:64, 0:64]),
                                 reads=[stik, "ident32"], writes=[pk if hh < 4 else pk2])
                        copy("dve", h0t[:, half * 512:half * 512 + 256], pb[0:64, 0:256], [pk], [h0k])
                        copy("dve", h0t[:, half * 512 + 256:half * 512 + 512], pb2[0:64, 0:256], [pk2], [h0k])
                    copy("dve", hst, h0t, [h0k], [hstk])
                else:
                    memset("dve", hst, 0.0, hstk)
                order = list(range(nch)) if d == 0 else list(range(nch - 1, -1, -1))
                if lat:
                    order = order + order
                def prep(c):
                    r_ = (r0s // 128 + c) * 128
                    xb_, xbk = xb_b.next(); load(xb_, XBC[r_:r_ + 128, :], xbk)
                    so, sok = so_b.next(); load(so, STT[r_:r_ + 128, :], sok)
                    xw, xwk = xw_b.next()
                    tt("pool", xw.rearrange("p (h q) -> p h q", h=16), xb_[:, 0:1024].rearrange("p (h q) -> p h q", h=16),
                       so[:, 32 + d * 16:48 + d * 16].unsqueeze(2).to_broadcast([128, 16, 64]), ALU.mult, [xbk, sok], [xwk])
                    pA, pAk = bank(); pB_, pBk = bank()
                    for g in range(4):
                        tgt, tk_ = (pA, pAk) if g < 2 else (pB_, pBk)
                        S.op("pe", lambda e, g=g, tgt=tgt, xb_=xb_, xw=xw: e.matmul(
                            tgt[0:64, (g % 2) * 256:(g % 2) * 256 + 256], xb_[:, 1024 + g * 64:1024 + (g + 1) * 64], xw[:, g * 256:(g + 1) * 256], start=True, stop=True),
                            reads=[xbk, xwk], writes=[tk_])
                    return so, sok, pA, pAk, pB_, pBk
                pend_ = prep(order[0])
                yield
                for oi_, c in enumerate(order):
                    gt = r0s // 128 + c
                    so, sok, pA, pAk, pB_, pBk = pend_
                    if lat:
                        bb_ = None
                        if d == 0 and c % OWN == 0:
                            bb_ = c // OWN
                        if d == 1 and (c + 1) % OWN == 0:
                            bb_ = ((c + 1) % NTS) // OWN
                        if bb_ is not None:
                            S.op("dve", lambda e, bb_=bb_: e.tensor_scalar(hst, hst, mkt[0:64, 8 + bb_:9 + bb_], None, ALU.mult),
                                 reads=[hstk, "mkt"], writes=[hstk])
                            stt("dve", hst, h0t, mkt[0:64, 12 + bb_:13 + bb_], hst, ALU.mult, ALU.add, [h0k, "mkt", hstk], [hstk])
                    if (not lat) or oi_ >= nch:
                        hb, hbk = hbf_b.next()
                        copy("act", hb, hst, [hstk], [hbk])
                        store(HIN[gt, d], hb, hbk, [("HIN", gt, d)])
                    yield
                    if oi_ + 1 < len(order):
                        pend_ = prep(order[oi_ + 1])
                        yield
                    tt("dve", hst.rearrange("p (h q) -> p h q", h=16), hst.rearrange("p (h q) -> p h q", h=16),
                       so[0:64, 96 + d * 16:112 + d * 16].unsqueeze(2).to_broadcast([64, 16, 64]), ALU.mult, [hstk, sok], [hstk])
                    tt("dve", hst[:, 0:512], hst[:, 0:512], pA[0:64, :], ALU.add, [hstk, pAk], [hstk])
                    tt("dve", hst[:, 512:1024], hst[:, 512:1024], pB_[0:64, :], ALU.add, [hstk, pBk], [hstk])
                    yield
                if not lat:
                    for half in range(2):
                        pb, pk = bank(); pb2, pk2 = bank()
                        for hh in range(8):
                            h = half * 8 + hh
                            tgt = pb if hh < 4 else pb2
                            S.op("pe", lambda e, h=h, hh=hh, tgt=tgt: e.transpose(tgt[0:64, (hh % 4) * 64:(hh % 4) * 64 + 64], hst[:, h * 64:(h + 1) * 64], ident32[0:64, 0:64]),
                                 reads=[hstk, "ident32"], writes=[pk if hh < 4 else pk2])
                        copy("dve", sto[:, half * 8:half * 8 + 4, :], pb[0:64, 0:256].rearrange("p (h n) -> p h n", h=4), [pk], [stok])
                        copy("dve", sto[:, half * 8 + 4:half * 8 + 8, :], pb2[0:64, 0:256].rearrange("p (h n) -> p h n", h=4), [pk2], [stok])
                    S.dma("sp", nssd[si, d].rearrange("h p n -> p h n"), sto, stok, reads=[stok], writes=[("nssd", si, d)])
        pipeline(chain, [(si_, d_) for si_ in range(NSEQ) for d_ in range(2)], PDEPTH)
        S.barrier()
        if STOP[0] == 7:
            S.finish(); return nc, S
    es_w7 = ExitStack()
    wm = es_w7.enter_context(nc.sbuf_tensor("wm", [128, 8, D], BF16)).ap()
    ws_ = es_w7.enter_context(nc.sbuf_tensor("ws", [128, 8, D], BF16)).ap()
    w7_chunks = [(wm, w_o_mla, k, "wm") for k in range(8)] + [(ws_, w_o_ssd, k, "ws") for k in range(8)]

    def w7_prefetch(nmax):
        for _ in range(nmax):
            if w7_chunks:
                wt_, wsrc_, k_, key_ = w7_chunks.pop(0)
                loadc(wt_[:, k_, :], wsrc_[k_ * 128:(k_ + 1) * 128, :], key_)
    with ExitStack() as es:
        def L(name, shape, dt):
            return es.enter_context(nc.sbuf_tensor(name, list(shape), dt)).ap()
        xb_b = Rot([L("s6ox%d" % i, [128, 1536], BF16) for i in range(2)], "s6ox")
        so_b = Rot([L("s6os%d" % i, [128, 160], F32) for i in range(2)], "s6os")
        mneg = L("mneg", [128, 2, 4, 128], F32)
        memset("pool", mneg, 0.0, "mneg")
        asel(mneg[:, 0], "mneg", [[0, 4], [1, 128]], ALU.is_ge, NEG, 0, -1)
        asel(mneg[:, 1], "mneg", [[0, 4], [-1, 128]], ALU.is_ge, NEG, 0, 1)
        abc_b = Rot([L("abc%d" % i, [128, 32, 128], F32) for i in range(2)], "abc")
        Dbc = L("Dbc", [128, 16], F32); load(Dbc, Dv.partition_broadcast(128), "Dbc")
        gnb = L("gnb", [128, D], F32); load(gnb, gn.partition_broadcast(128), "gnb")
        gm_b = Rot([L("s6g%d" % i, [32, 128], F32) for i in range(2)], "s6g")
        hin_b = [Rot([L("s6i%d_%d" % (d, i), [128, D], BF16) for i in range(2)], "s6i%d_" % d) for d in range(2)]
        z_b = Rot([L("s6z%d" % i, [128, D], F32) for i in range(2)], "s6z")
        bc_b = Rot([L("s6bc%d" % i, [128, 512], BF16) for i in range(2)], "s6bc")
        scs_b = Rot([L("s6sc%d" % i, [128, 4, 128], BF16) for i in range(2)], "s6sc")
        lt_b = Rot([L("s6l%d" % i, [128, 4, 128], BF16) for i in range(2)], "s6l")
        mt_b = Rot([L("s6m%d" % i, [128, 16, 128], BF16) for i in range(2)], "s6m")
        mb_b = Rot([L("s6n%d" % i, [128, 4, 128], BF16) for i in range(2)], "s6n")
        y_b = Rot([L("s6y%d" % i, [128, D], F32) for i in range(2)], "s6y")
        y2_b = Rot([L("s6v%d" % i, [128, D], F32) for i in range(2)], "s6v")
        y3_b = Rot([L("s6u%d" % i, [128, D], F32) for i in range(2)], "s6u")
        yo_b = Rot([L("s6o%d" % i, [128, D], BF16) for i in range(2)], "s6o")
        st_b = Rot([L("s6t%d" % i, [128, 4], F32) for i in range(2)], "s6t")
        def body(gt):
            r = gt * 128
            yield
            pr = prow(gt)
            yield
            xb_, xbk = xb_b.next(); load(xb_, XBC[r:r + 128, :], xbk)
            yield
            so, sok = so_b.next(); load(so, STT[r:r + 128, :], sok)
            yield
            w7_prefetch(2)
            abc, abck = abc_b.next()
            yield
            copy("dve", abc, so[:, 128:160].unsqueeze(2).to_broadcast([128, 32, 128]), [sok], [abck])
            yield
            hin = []
            yield
            for d in range(2):
                hi, hik = hin_b[d].next()
                load(hi[0:64, :], HIN[gt, d], hik); load(hi[64:128, :], HIN[gt, d], hik)
                hin.append((hi, hik))
            yield
            z, zk = z_b.next(); load(z, PROJ[pr:pr + 128, C_Z:C_Z + 1024], zk)
            yield
            bc, bck = bc_b.next()
            yield
            transpose_to(bc, bck, xb_[:, 1024:1536], xbk, 4, "act")
            yield
            pS2 = [bank(), bank()]
            yield
            scs, sck = scs_b.next()
            yield
            for g in range(4):
                p0 = (g % 2) * 64
                pS, pSk = pS2[g % 2]
                S.op("pe", lambda e, g=g, p0=p0, pS=pS, bc=bc: e.matmul(
                    pS[:, (g // 2) * 128:(g // 2) * 128 + 128], bc[p0:p0 + 64, (g // 2) * 128:(g // 2) * 128 + 128],
                    bc[p0:p0 + 64, (2 + g // 2) * 128:(2 + g // 2) * 128 + 128], start=True, stop=True),
                    reads=[bck], writes=[pSk])
            yield
            for g in range(4):
                pS, pSk = pS2[g % 2]
                copy("dve", scs[:, g, :], pS[:, (g // 2) * 128:(g // 2) * 128 + 128], [pSk], [sck])
            yield
            mt, mtk = mt_b.next()
            yield
            for d in range(2):
                for g in range(4):
                    pL, pLk = bank()
                    S.op("pe", lambda e, d=d, pL=pL: e.matmul(pL, ident32, mneg[:, d].rearrange("p a k -> p (a k)"), start=True, stop=False),
                         reads=["ident32", "mneg"], writes=[pLk])
                    for hh in range(4):
                        dh = d * 16 + g * 4 + hh
                        S.op("pe", lambda e, hh=hh, dh=dh, pL=pL, d=d: e.matmul(
                            pL[:, hh * 128:(hh + 1) * 128], abc[:, dh, :], (tincl if d == 0 else texcl), start=False, stop=(hh == 3)),
                            reads=[abck, "tincl", "texcl"], writes=[pLk])
                    lt, ltk = lt_b.next()
                    for hh in range(4):
                        dh = d * 16 + g * 4 + hh
                        S.op("act", lambda e, hh=hh, dh=dh, lt=lt, pL=pL, so=so: e.activation(
                            lt[:, hh, :], pL[:, hh * 128:(hh + 1) * 128], AF.Exp, bias=so[:, dh:dh + 1]),
                            reads=[pLk, sok], writes=[ltk])
                    if d == 0:
                        tt("dve", mt[:, g * 4:(g + 1) * 4, :], lt, scs[:, g:g + 1, :].to_broadcast([128, 4, 128]), ALU.mult, [ltk, sck], [mtk])
                    else:
                        mb_, mbk = mb_b.next()
                        tt("dve", mb_, lt, scs[:, g:g + 1, :].to_broadcast([128, 4, 128]), ALU.mult, [ltk, sck], [mbk])
                        tt("pool", mt[:, g * 4:(g + 1) * 4, :], mt[:, g * 4:(g + 1) * 4, :], mb_, ALU.add, [mtk, mbk], [mtk])
            yield
            pY = [bank(), bank()]
            yield
            for h in range(16):
                tgt, tk_ = pY[h // 8]
                S.op("pe", lambda e, h=h, tgt=tgt, mt=mt, xb_=xb_: e.matmul(
                    tgt[:, (h % 8) * 64:(h % 8) * 64 + 64], mt[:, h, :], xb_[:, h * 64:(h + 1) * 64], start=True, stop=True),
                    reads=[mtk, xbk], writes=[tk_])
            yield
            y, yk = y_b.next()
            yield
            copy("act", y[:, 0:512], pY[0][0], [pY[0][1]], [yk])
            yield
            copy("act", y[:, 512:1024], pY[1][0], [pY[1][1]], [yk])
            yield
            y2, y2k = y2_b.next()
            yield
            for d in range(2):
                hi, hik = hin[d]
                pZ = [bank(), bank()]
                for g in range(4):
                    p0 = (g % 2) * 64
                    tgt, tk_ = pZ[g % 2]
                    S.op("pe", lambda e, g=g, p0=p0, tgt=tgt, bc=bc, hi=hi: e.matmul(
                        tgt[:, (g // 2) * 256:(g // 2) * 256 + 256], bc[p0:p0 + 64, (2 + g // 2) * 128:(2 + g // 2) * 128 + 128],
                        hi[p0:p0 + 64, g * 256:(g + 1) * 256], start=True, stop=True),
                        reads=[bck, hik], writes=[tk_])
                for g in range(4):
                    tgt, tk_ = pZ[g % 2]
                    src = tgt[:, (g // 2) * 256:(g // 2) * 256 + 256].rearrange("p (h q) -> p h q", h=4)
                    scv = so[:, 64 + d * 16 + g * 4:64 + d * 16 + g * 4 + 4].unsqueeze(2).to_broadcast([128, 4, 64])
                    if d == 0:
                        tt("dve", y2[:, g * 256:(g + 1) * 256].rearrange("p (h q) -> p h q", h=4), src, scv, ALU.mult, [tk_, sok], [y2k])
                    else:
                        y3, y3k = y3_b.next()
                        tt("dve", y3[:, 0:256].rearrange("p (h q) -> p h q", h=4), src, scv, ALU.mult, [tk_, sok], [y3k])
                        tt("pool", y2[:, g * 256:(g + 1) * 256], y2[:, g * 256:(g + 1) * 256], y3[:, 0:256], ALU.add, [y2k, y3k], [y2k])
            yield
            y3, y3k = y3_b.next()
            yield
            tt("dve", y3.rearrange("p (h q) -> p h q", h=16), xb_[:, 0:1024].rearrange("p (h q) -> p h q", h=16),
               Dbc.unsqueeze(2).to_broadcast([128, 16, 64]), ALU.mult, [xbk, "Dbc"], [y3k])
            yield
            tt("pool", y2, y2, y3, ALU.add, [y2k, y3k], [y2k])
            yield
            tt("dve", y, y, y2, ALU.add, [yk, y2k], [yk])
            yield
            S.op("act", lambda e, z=z: e.activation(z, z, AF.Silu), reads=[zk], writes=[zk])
            yield
            tt("dve", y, y, z, ALU.mult, [yk, zk], [yk])
            yield
            st, stk = st_b.next()
            yield
            rr = rstd_of(st, stk, y, yk, D, y2, y2k)
            yield
            yo, yok = yo_b.next()
            yield
            stt("dve", yo, y, rr, gnb, ALU.mult, ALU.mult, [yk, stk, "gnb"], [yok])
            yield
            store(YN[r:r + 128, :], yo, yok, [("YN", gt)])
            yield
        pipeline(body, TWIN, PDEPTH)
        w7_prefetch(16)
        S.barrier()
        if STOP[0] == 8:
            S.finish(); return nc, S

    with ExitStack() as es:
        def L(name, shape, dt):
            return es.enter_context(nc.sbuf_tensor(name, list(shape), dt)).ap()
        wo = L("wo", [128, 8, D], BF16)
        loadc(wo, w_out.rearrange("(k p) n -> p k n", p=128), "wo")
        at_b = Rot([L("s7a%d" % i, [128, D], BF16) for i in range(2)], "s7a")
        yn_b = Rot([L("s7y%d" % i, [128, D], BF16) for i in range(2)], "s7y")
        aT_b = Rot([L("s7at%d" % i, [128, D], BF16) for i in range(2)], "s7at")
        yT_b = Rot([L("s7yt%d" % i, [128, D], BF16) for i in range(2)], "s7yt")
        g_b = Rot([L("s7g%d" % i, [128, 2 * D], F32) for i in range(2)], "s7g")
        x_b = Rot([L("s7x%d" % i, [128, D], F32) for i in range(2)], "s7x")
        m1_b = Rot([L("s7m%d" % i, [128, D], F32) for i in range(2)], "s7m")
        m2_b = Rot([L("s7n%d" % i, [128, D], F32) for i in range(2)], "s7n")
        mg_b = Rot([L("s7mg%d" % i, [128, D], BF16) for i in range(2)], "s7mg")
        mT_b = Rot([L("s7mt%d" % i, [128, D], BF16) for i in range(2)], "s7mt")
        hb_b = Rot([L("s7h%d" % i, [128, D], BF16) for i in range(2)], "s7h")
        hT_b = Rot([L("s7ht%d" % i, [128, D], BF16) for i in range(2)], "s7ht")
        st_b = Rot([L("s7s%d" % i, [128, 4], F32) for i in range(2)], "s7s")

        def mm2(lhsT, lk, W, wkey):
            res = []
            for nb in range(2):
                pb, pk = bank()
                for k in range(8):
                    S.op("pe", lambda e, k=k, nb=nb, pb=pb: e.matmul(
                        pb, lhsT[:, k * 128:(k + 1) * 128], W[:, k, nb * 512:(nb + 1) * 512], start=(k == 0), stop=(k == 7)),
                        reads=[lk, wkey], writes=[pk])
                res.append((pb, pk))
            return res

        def body(gt):
            si, _ = seq_of_tile(gt)
            yield
            m = seqs[si][2]
            yield
            r = gt * 128
            yield
            pr = prow(gt)
            yield
            yn, ynk = yn_b.next(); load(yn, YN[r:r + 128, :], ynk)
            yield
            gg, gk = g_b.next(); load(gg, PROJ[pr:pr + 128, C_GM:C_GM + 2048], gk)
            yield
            xt, xk = x_b.next(); load(xt, xrows(gt), xk)
            yield
            aT, aTk = aT_b.next(); load(aT.rearrange("p (h t) -> p h t", h=8), ATTT[:, :, r:r + 128].rearrange("h p t -> p h t"), aTk)
            yield
            yT, yTk = yT_b.next(); transpose_to(yT, yTk, yn, ynk, 8, "dve")
            yield
            S.op("act", lambda e, gg=gg: e.activation(gg, gg, AF.Sigmoid), reads=[gk], writes=[gk])
            yield
            om = mm2(aT, aTk, wm, "wm")
            yield
            m1, m1k = m1_b.next()
            yield
            for nb in range(2):
                tt("dve", m1[:, nb * 512:(nb + 1) * 512], om[nb][0], gg[:, nb * 512:(nb + 1) * 512], ALU.mult, [om[nb][1], gk], [m1k])
            yield
            os_ = mm2(yT, yTk, ws_, "ws")
            yield
            m2, m2k = m2_b.next()
            yield
            for nb in range(2):
                tt("dve", m2[:, nb * 512:(nb + 1) * 512], os_[nb][0], gg[:, D + nb * 512:D + (nb + 1) * 512], ALU.mult, [os_[nb][1], gk], [m2k])
            yield
            mg, mgk = mg_b.next()
            yield
            tt("pool", mg, m1, m2, ALU.add, [m1k, m2k], [mgk])
            yield
            mT, mTk = mT_b.next(); transpose_to(mT, mTk, mg, mgk, 8, "act")
            yield
            op_ = mm2(mT, mTk, wo, "wo")
            yield
            for nb in range(2):
                tt("dve", m1[:, nb * 512:(nb + 1) * 512], op_[nb][0], modp(m, G1)[:, nb * 512:(nb + 1) * 512], ALU.mult, [op_[nb][1], "mod%d" % m], [m1k])
            yield
            tt("pool", xt, xt, m1, ALU.add, [xk, m1k], [xk])
            yield
            S.dma("pool", XMID[r:r + 128, :], xt, xk, reads=[xk], writes=[("XMID", gt)])
            yield
            st, stk = st_b.next(); hb, hk = hb_b.next(); hT, hTk = hT_b.next()
            yield
            rr = rstd_of(st, stk, xt, xk, D, m2, m2k)
            yield
            stt("dve", m2, xt, rr, modp(m, GSC2), ALU.mult, ALU.mult, [xk, stk, "mod%d" % m], [m2k])
            yield
            tt("pool", hb, m2, modp(m, SH2), ALU.add, [m2k, "mod%d" % m], [hk])
            yield
            transpose_to(hT, hTk, hb, hk, 8, "act")
            yield
            store(H2T[gt], hT, hTk, [("H2T", gt)])
            yield
        pipeline(body, TWIN, PDEPTH)
        S.barrier()
        if STOP[0] == 9:
            S.finish(); return nc, S

    es_w7.close()
    blocks = []
    for si, (r0s, Ls, lat) in enumerate(seqs):
        if lat:
            blocks.append((NPT, OWN, NPT + NTS - 1, NPT + OWN, (8, 9)))
        else:
            assert Ls // 128 <= 4
            blocks.append((r0s // 128, Ls // 128, None, None, None))
    NCOL = sum(b_[1] * 128 + 2 for b_ in blocks)
    ACTT = scr("ACTT", [22, 128, R], BF16)
    es_wd = ExitStack()
    wd = es_wd.enter_context(nc.sbuf_tensor("wd", [128, 22, D], BF16)).ap()
    with ExitStack() as es:
        def L(name, shape, dt):
            return es.enter_context(nc.sbuf_tensor(name, list(shape), dt)).ap()
        h2all = L("h2all", [128, 8, NCOL], BF16)
        memset("dve", h2all, 0.0, "h2all")
        bcols = []
        c = 0
        for (gt0, ntl, lsrc, rsrc, mcols) in blocks:
            bcols.append(c)
            for ti in range(ntl):
                load(h2all[:, :, c + 1 + ti * 128:c + 1 + (ti + 1) * 128], H2T[gt0 + ti].rearrange("p (k t) -> p k t", k=8), "h2all")
            if lsrc is not None:
                load(h2all[:, :, c:c + 1], H2T[lsrc].rearrange("p (k t) -> p k t", k=8)[:, :, 127:128], "h2all",
                     allow_slow_non_contiguous=True)
                S.op("dve", lambda e, c=c, mc=mcols[0]: e.tensor_scalar(h2all[:, :, c:c + 1], h2all[:, :, c:c + 1], mkt[:, mc:mc + 1], None, ALU.mult),
                     reads=["h2all", "mkt"], writes=["h2all"])
            if rsrc is not None:
                ce = c + ntl * 128 + 1
                load(h2all[:, :, ce:ce + 1], H2T[rsrc].rearrange("p (k t) -> p k t", k=8)[:, :, 0:1], "h2all",
                     allow_slow_non_contiguous=True)
                S.op("dve", lambda e, ce=ce, mc=mcols[1]: e.tensor_scalar(h2all[:, :, ce:ce + 1], h2all[:, :, ce:ce + 1], mkt[:, mc:mc + 1], None, ALU.mult),
                     reads=["h2all", "mkt"], writes=["h2all"])
            c += ntl * 128 + 2
        fwT = L("fwT", [128, 2, 22, 3], F32)
        fbT = L("fbT", [128, 2, 22], F32)
        for t_ in range(2):
            for k in range(3):
                load(fwT[:, t_, :, k], fw[k, t_ * DFF:(t_ + 1) * DFF].rearrange("(j p) -> p j", p=128), "fwT", allow_slow_non_contiguous=True)
            load(fbT[:, t_, :], fb[t_ * DFF:(t_ + 1) * DFF].rearrange("(j p) -> p j", p=128), "fbT", allow_slow_non_contiguous=True)
        wg_b = Rot([L("s8w%d" % i, [128, 2, 8, 128], BF16) for i in range(3)], "s8w")
        ue_b = [Rot([L("s8u%d_%d" % (t_, i), [128, 514], F32) for i in range(2)], "s8u%d_" % t_) for t_ in range(2)]
        tc_b = [Rot([L("s8t%d_%d" % (t_, i), [128, 512], F32) for i in range(2)], "s8t%d_" % t_) for t_ in range(2)]
        sg_b = Rot([L("s8s%d" % i, [128, 512], F32) for i in range(2)], "s8s")
        ao_b = Rot([L("s8a%d" % i, [128, 512], BF16) for i in range(3)], "s8a")
        def s8_wload(j):
            wt, wk = wg_b.next()
            loadc(wt[:, 0], w_up[:, j * 128:(j + 1) * 128].rearrange("(k p) n -> p k n", p=128), wk)
            loadc(wt[:, 1], w_up[:, DFF + j * 128:DFF + (j + 1) * 128].rearrange("(k p) n -> p k n", p=128), wk)
            return wt, wk
        s8_next = s8_wload(0)
        for j in range(22):
            wt, wk = s8_next
            if j + 1 < 22:
                s8_next = s8_wload(j + 1)
            loadc(wd[:, j, :], w_down[j * 128:(j + 1) * 128, :], "wd")
            for bi, (gt0, ntl, lsrc_, rsrc_, mcols_) in enumerate(blocks):
                n = ntl * 128
                c = bcols[bi]
                tcs = []
                for t_ in range(2):
                    pa, pak = bank()
                    ph, phk = bank()
                    for k in range(8):
                        S.op("pe", lambda e, k=k, t_=t_, pa=pa, wt=wt, c=c, n=n: e.matmul(
                            pa[:, 0:n], wt[:, t_, k, :], h2all[:, k, c + 1:c + 1 + n], start=(k == 0), stop=(k == 7)),
                            reads=[wk, "h2all"], writes=[pak])
                    for k in range(8):
                        S.op("pe", lambda e, k=k, t_=t_, ph=ph, wt=wt, c=c, n=n: e.matmul(
                            ph[:, 0:2], wt[:, t_, k, :], h2all[:, k, c:c + n + 2:n + 1], start=(k == 0), stop=(k == 7)),
                            reads=[wk, "h2all"], writes=[phk])
                    ue, uek = ue_b[t_].next()
                    copy("act", ue[:, 1:n + 1], pa[:, 0:n], [pak], [uek])
                    copy("dve", ue[:, 0:n + 2:n + 1], ph[:, 0:2], [phk], [uek])
                    tcv, tck = tc_b[t_].next()
                    S.op("act", lambda e, tcv=tcv, pa=pa, t_=t_, j=j, n=n: e.activation(
                        tcv[:, 0:n], pa[:, 0:n], AF.Copy, scale=fwT[:, t_, j, 1:2]), reads=[pak, "fwT"], writes=[tck])
                    stt("dve", tcv[:, 0:n], ue[:, 0:n], fwT[:, t_, j, 0:1], tcv[:, 0:n], ALU.mult, ALU.add, [uek, "fwT", tck], [tck])
                    stt("dve", tcv[:, 0:n], ue[:, 2:n + 2], fwT[:, t_, j, 2:3], tcv[:, 0:n], ALU.mult, ALU.add, [uek, "fwT", tck], [tck])
                    tcs.append((tcv, tck))
                sg, sgk = sg_b.next()
                S.op("act", lambda e, sg=sg, tcv=tcs[0][0], j=j, n=n: e.activation(sg[:, 0:n], tcv[:, 0:n], AF.Silu, bias=fbT[:, 0, j:j + 1]),
                     reads=[tcs[0][1], "fbT"], writes=[sgk])
                ao, aok = ao_b.next()
                stt("dve", ao[:, 0:n], tcs[1][0][:, 0:n], fbT[:, 1, j:j + 1], sg[:, 0:n], ALU.add, ALU.mult, [tcs[1][1], "fbT", sgk], [aok])
                store(ACTT[j][:, gt0 * 128:gt0 * 128 + n], ao[:, 0:n], aok, [("ACTT", j, bi)])
        S.barrier()
        if STOP[0] == 10:
            S.finish(); return nc, S

    with ExitStack() as es:
        def L(name, shape, dt):
            return es.enter_context(nc.sbuf_tensor(name, list(shape), dt)).ap()
        fngb = L("fngb", [128, D], F32); load(fngb, fng.partition_broadcast(128), "fngb")
        aT_b = Rot([L("s10at%d" % i, [128, 22, 128], BF16) for i in range(3)], "s10at")
        x_b = Rot([L("s10x%d" % i, [128, D], F32) for i in range(3)], "s10x")
        o_b = Rot([L("s10o%d" % i, [128, D], F32) for i in range(3)], "s10o")
        st_b = Rot([L("s10s%d" % i, [128, 4], F32) for i in range(3)], "s10s")
        def body(gt):
            si, _ = seq_of_tile(gt)
            yield
            m = seqs[si][2]
            yield
            r = gt * 128
            yield
            aT, aTk = aT_b.next(); load(aT, ACTT[:, :, r:r + 128].rearrange("j p t -> p j t"), aTk)
            yield
            xt, xk = x_b.next(); load(xt, XMID[r:r + 128, :], xk)
            yield
            ot, ok = o_b.next()
            yield
            for nb in range(2):
                pb, pk = bank()
                for k in range(22):
                    S.op("pe", lambda e, k=k, nb=nb, pb=pb, aT=aT: e.matmul(
                        pb, aT[:, k, :], wd[:, k, nb * 512:(nb + 1) * 512], start=(k == 0), stop=(k == 21)),
                        reads=[aTk, "wd"], writes=[pk])
                tt("dve", ot[:, nb * 512:(nb + 1) * 512], pb, modp(m, G2)[:, nb * 512:(nb + 1) * 512], ALU.mult, [pk, "mod%d" % m], [ok])
            yield
            tt("pool", xt, xt, ot, ALU.add, [xk, ok], [xk])
            yield
            st, stk = st_b.next()
            yield
            rr = rstd_of(st, stk, xt, xk, D, ot, ok)
            yield
            stt("dve", ot, xt, rr, fngb, ALU.mult, ALU.mult, [xk, stk, "fngb"], [ok])
            yield
            S.dma("sp", yrows(gt), ot, ok, reads=[ok], writes=[("Y", gt)])
            yield
        pipeline(body, TOWN, 3)
        S.barrier()
    es_wd.close()
    S.finish()
    return nc, S


def rope_tables(LS):
    t = np.arange(LS)
    row = (t // 64).astype(np.float32)
    col = (t % 64).astype(np.float32)
    n = 16
    inv = (10000.0 ** (-np.arange(n, dtype=np.float32) / n)).astype(np.float32)
    ang = np.stack([row[:, None] * inv, col[:, None] * inv], axis=1).reshape(LS, 32).astype(np.float32)
    return np.cos(ang).astype(np.float32), np.sin(ang).astype(np.float32)


_CACHE = {}


def run(inputs, NP, LP, LS, PAST, n_cores=8):
    key = (NP, LP, LS, PAST)
    if key not in _CACHE:
        _CACHE[key] = build(NP, LP, LS, PAST)[0]
    nc = _CACHE[key]
    f = lambda a: np.ascontiguousarray(np.asarray(a, dtype=np.float32))
    cos_t, sin_t = rope_tables(LS)
    shared = {
        "w_ada": f(inputs["w_ada"][0]), "b_ada": f(inputs["b_ada"][0]), "ga": f(inputs["norm_attn_g"][0]),
        "w_in": f(inputs["w_in"][0]), "qg": f(inputs["q_norm_g"][0]), "kvg": f(inputs["kv_norm_g"][0]),
        "w_uq": f(inputs["w_uq"][0]), "w_ukv": f(inputs["w_ukv"][0]), "w_o_mla": f(inputs["w_o_mla"][0]),
        "cw": f(inputs["ssd_conv_w"][0]), "cb": f(inputs["ssd_conv_b"][0]),
        "dtb": f(inputs["ssd_dt_bias"][0]).reshape(32), "alog": f(inputs["ssd_A_log"][0]).reshape(32),
        "Dv": f(inputs["ssd_D"][0]), "gn": f(inputs["ssd_norm_g"][0]), "w_o_ssd": f(inputs["w_o_ssd"][0]),
        "w_out": f(inputs["w_out"][0]), "gf": f(inputs["norm_ffn_g"][0]), "w_up": f(inputs["w_up"][0]),
        "fw": f(inputs["ffn_conv_w"][0]), "fb": f(inputs["ffn_conv_b"][0]), "w_down": f(inputs["w_down"][0]),
        "fng": f(inputs["final_norm_g"]), "cos_t": cos_t, "sin_t": sin_t,
    }
    xpr = f(inputs["x_prompt"]); xsm = f(inputs["x_sample"]); c = f(inputs["c"]); cctx = f(inputs["c_ctx"])
    in_maps = []
    G4 = 4
    NTS = LS // 128
    OWN = NTS // G4
    rots = []
    for core in range(n_cores):
        sq = (core // G4) % xsm.shape[0]
        rr = core % G4
        rot = OWN * rr
        rots.append((sq, rot))
        d = dict(shared)
        d["xp"] = np.ascontiguousarray(xpr[core * NP:(core + 1) * NP].reshape(NP * LP, D))
        d["xs"] = np.ascontiguousarray(np.roll(xsm[sq], -rot * 128, axis=0))
        d["cos_t"] = np.ascontiguousarray(np.roll(cos_t, -rot * 128, axis=0))
        d["sin_t"] = np.ascontiguousarray(np.roll(sin_t, -rot * 128, axis=0))
        mkv = np.ones((128, 16), np.float32)
        for b_ in range(G4):
            m_ = 0.0 if (b_ + rr) % G4 == 0 else 1.0
            mkv[0, b_] = m_
            mkv[127, 4 + b_] = m_
            mkv[:, 8 + b_] = m_
            mkv[:, 12 + b_] = 1.0 - m_
        d["mk"] = mkv
        d["cvec"] = np.ascontiguousarray(np.stack([cctx, c[sq]], 0))
        d["cckv"] = f(inputs["cache_ckv"][sq, 0]); d["ckr"] = f(inputs["cache_krope"][sq, 0])
        d["st"] = f(inputs["state_ssd"][sq, 0])
        in_maps.append(d)
    res = run_bass_kernel_spmd(nc, in_maps, core_ids=list(range(n_cores)))
    outs = res.results
    if DEBUG[0]:
        DBG_OUT.append(outs)
    B = xpr.shape[0]
    y_prompt = np.concatenate([outs[i]["yp"].reshape(NP, LP, D) for i in range(n_cores)], 0)[:B]
    y_sample = np.zeros(xsm.shape, np.float32)
    for core in range(n_cores):
        sq, rot = rots[core]
        y_sample[sq, rot * 128:(rot + OWN) * 128] = outs[core]["ys"]
    new_ckv = np.concatenate([outs[i]["nckv"].reshape(NP, 1, LP, 256) for i in range(n_cores)], 0)[:B]
    new_kr = np.concatenate([outs[i]["nkr"].reshape(NP, 1, LP, 64) for i in range(n_cores)], 0)[:B]
    new_ssd = np.concatenate([outs[i]["nssd"].reshape(NP, 1, 2, 16, 64, 64) for i in range(n_cores)], 0)[:B]
    return (y_prompt.astype(np.float32), y_sample.astype(np.float32), new_ckv.astype(np.float32),
            new_kr.astype(np.float32), new_ssd.astype(np.float32))


def kernel(**inputs):
    return run(inputs, 4, 256, 2048, 512, 8)
```

```python
import math
from contextlib import ExitStack, nullcontext
import numpy as np
import concourse.bass as bass
import concourse.mybir as mybir
from concourse.bass_utils import run_bass_kernel_spmd

F32 = mybir.dt.float32
BF16 = mybir.dt.bfloat16
AF = mybir.ActivationFunctionType
ALU = mybir.AluOpType

D = 1024
IN_COLS = 5216
C_CQ, C_CKV, C_KR, C_Z, C_X, C_DT, C_GM = 0, 256, 512, 576, 1600, 3136, 3168
DFF = 2816
EPS = 1e-6
NEG = -30000.0


class Sched:
    def __init__(self, nc):
        self.nc = nc
        self.e = {"pe": nc.tensor, "act": nc.scalar, "dve": nc.vector,
                  "pool": nc.gpsimd, "sp": nc.sync}
        self.sems = {}
        self.cnt = {}
        for k in ("pe", "act", "dve", "pool"):
            self.sems[k] = nc.alloc_semaphore("c_" + k)
            self.cnt[k] = 0
        self.seen = {k: {} for k in self.e}
        self.lastw = {}
        self.readers = {}
        self.nins = 0
        self.phys = {}
        self.physq = {}
        self.free = {}
        self.nphys = 0

    def _sem(self, sk, q):
        if sk not in self.phys:
            fl = self.free.setdefault(q, [])
            if fl:
                pid = fl.pop()
            else:
                pid = "d_%d" % self.nphys
                self.nphys += 1
                self.sems[pid] = self.nc.alloc_semaphore(pid)
                self.cnt[pid] = 0
            self.phys[sk] = pid
            self.physq[pid] = q
        return self.phys[sk]

    def _deps(self, reads, writes):
        best = {}

        def add(sk, v):
            if best.get(sk, 0) < v:
                best[sk] = v
        for k in reads:
            if k in self.lastw:
                add(*self.lastw[k])
        for k in writes:
            if k in self.lastw:
                add(*self.lastw[k])
            for sk, v in self.readers.get(k, {}).items():
                add(sk, v)
        return best

    def _wait(self, eng, best):
        for sk, v in best.items():
            if sk == eng and eng == "pe":
                continue
            if self.seen[eng].get(sk, 0) >= v:
                continue
            self.seen[eng][sk] = v
            self.e[eng].wait_ge(self.sems[sk], v)
            self.nins += 1

    def _record(self, tok, reads, writes):
        for k in writes:
            self.lastw[k] = tok
            self.readers[k] = {}
        for k in reads:
            r = self.readers.setdefault(k, {})
            if r.get(tok[0], 0) < tok[1]:
                r[tok[0]] = tok[1]

    def op(self, eng, fn, reads=(), writes=()):
        self._wait(eng, self._deps(reads, writes))
        ins = fn(self.e[eng])
        self.cnt[eng] += 1
        ins.then_inc(self.sems[eng], 1)
        self.nins += 1
        self._record((eng, self.cnt[eng]), reads, writes)

    def dma(self, q, out, in_, key, reads=(), writes=(), **kw):
        self._wait(q, self._deps(reads, writes))
        pid = self._sem("dma:%s:%s" % (q, key), q)
        ins = self.e[q].dma_start(out=out, in_=in_, **kw)
        self.cnt[pid] += 16
        ins.then_inc(self.sems[pid], 16)
        self.nins += 1
        self._record((pid, self.cnt[pid]), reads, writes)

    def _all(self):
        best = {}
        for sk, v in self.lastw.values():
            if best.get(sk, 0) < v:
                best[sk] = v
        for r in self.readers.values():
            for sk, v in r.items():
                if best.get(sk, 0) < v:
                    best[sk] = v
        return best

    def barrier(self):
        best = self._all()
        for eng in self.e:
            self._wait(eng, dict(best))
        self.lastw = {}
        self.readers = {}
        for pid in self.phys.values():
            self.free.setdefault(self.physq[pid], []).append(pid)
        self.phys = {}

    def finish(self):
        self._wait("sp", self._all())


PIPE = {"slot": None, "depth": 1}


class Rot:
    def __init__(self, aps, name):
        self.aps = aps
        self.name = name
        self.i = 0
        self.si = {}

    def next(self):
        sl = PIPE["slot"]
        d = PIPE["depth"]
        if sl is None or len(self.aps) < d:
            self.i = (self.i + 1) % len(self.aps)
            j = self.i
        else:
            idx = [i for i in range(len(self.aps)) if i % d == sl]
            c = (self.si.get(sl, 0) + 1) % len(idx)
            self.si[sl] = c
            j = idx[c]
        return self.aps[j], "%s%d" % (self.name, j)


def pipeline(make_body, items, depth):
    PIPE["depth"] = depth
    active = {}
    it = iter(items)
    done = False
    while True:
        for sl in range(depth):
            if sl not in active and not done:
                x = next(it, None)
                if x is None:
                    done = True
                else:
                    active[sl] = make_body(x)
        if not active:
            break
        for sl in sorted(active):
            PIPE["slot"] = sl
            try:
                next(active[sl])
            except StopIteration:
                del active[sl]
    PIPE["slot"] = None
    PIPE["depth"] = 1


STOP = [0]
PDEPTH = 2
SUB = [0]
DEBUG = [0]
DBG_OUT = []


def build(NP, LP, LS, PAST):
    nc = bass.Bass("TRN2", target_bir_lowering=False)
    S = Sched(nc)
    RPRM = NP * LP
    R = RPRM + LS
    NT = R // 128
    NPT = RPRM // 128
    NTS = LS // 128
    G4 = 4
    OWN = NTS // G4
    assert OWN * G4 == NTS and 1 <= OWN <= 4
    WIN = [NPT + NTS - 1] + [NPT + i for i in range(OWN + 1)]
    TWIN = list(range(NPT)) + WIN
    TWINS = set(TWIN)
    TOWN = list(range(NPT)) + [NPT + i for i in range(OWN)]
    seqs = [(i * LP, LP, 0) for i in range(NP)] + [(RPRM, LS, 1)]
    NSEQ = len(seqs)
    RP = R + 2 * NSEQ
    KRT = R + PAST

    def seq_of_tile(gt):
        r = gt * 128
        for si, (r0, L, m) in enumerate(seqs):
            if r0 <= r < r0 + L:
                return si, (r - r0) // 128
        raise ValueError

    def prow(gt):
        si, _ = seq_of_tile(gt)
        return gt * 128 + 2 * si + 1

    def krow(gt):
        si, _ = seq_of_tile(gt)
        return gt * 128 + (PAST if seqs[si][2] else 0)

    def din(name, shape):
        return nc.dram_tensor(name, list(shape), F32, kind="ExternalInput").ap()

    def dout(name, shape):
        return nc.dram_tensor(name, list(shape), F32, kind="ExternalOutput").ap()

    xp = din("xp", [RPRM, D]); xs = din("xs", [LS, D]); cvec = din("cvec", [2, D])
    cckv = din("cckv", [PAST, 256]); ckr = din("ckr", [PAST, 64]); st_in = din("st", [2, 16, 64, 64])
    w_ada = din("w_ada", [D, 6 * D]); b_ada = din("b_ada", [6 * D]); ga = din("ga", [D])
    w_in = din("w_in", [D, IN_COLS]); qg = din("qg", [256]); kvg = din("kvg", [256])
    w_uq = din("w_uq", [256, 1536]); w_ukv = din("w_ukv", [256, 2048]); w_o_mla = din("w_o_mla", [D, D])
    cw = din("cw", [3, 1536]); cb = din("cb", [1536]); dtb = din("dtb", [32]); alog = din("alog", [32])
    Dv = din("Dv", [16]); gn = din("gn", [D]); w_o_ssd = din("w_o_ssd", [D, D]); w_out = din("w_out", [D, D])
    gf = din("gf", [D]); w_up = din("w_up", [D, 2 * DFF]); fw = din("fw", [3, 2 * DFF]); fb = din("fb", [2 * DFF])
    w_down = din("w_down", [DFF, D]); fng = din("fng", [D])
    cos_t = din("cos_t", [LS, 32]); sin_t = din("sin_t", [LS, 32]); mk = din("mk", [128, 16])
    yp = dout("yp", [RPRM, D]); ys = dout("ys", [OWN * 128, D]); nckv = dout("nckv", [RPRM, 256])
    nkr = dout("nkr", [RPRM, 64]); nssd = dout("nssd", [NP, 2, 16, 64, 64])

    def xrows(gt):
        r = gt * 128
        return xp[r:r + 128, :] if r < RPRM else xs[r - RPRM:r - RPRM + 128, :]

    def yrows(gt):
        r = gt * 128
        return yp[r:r + 128, :] if r < RPRM else ys[r - RPRM:r - RPRM + 128, :]

    def scr(name, shape, dt):
        if DEBUG[0]:
            return nc.dram_tensor(name, list(shape), dt, kind="ExternalOutput").ap()
        return nc.dram_tensor(name, list(shape), dt).ap()
    HT = scr("HT", [NT, 128, D], BF16)
    PROJ = scr("PROJ", [RP, IN_COLS], F32)
    QS = scr("QS", [R, 1536], BF16)
    KVS = scr("KVS", [KRT, 2048], BF16)
    KRS = scr("KRS", [KRT, 64], BF16)
    XBC = scr("XBC", [R, 1536], BF16)
    STT = scr("STT", [R, 160], F32)
    GMT = scr("GMT", [NT, 32, 128], F32)
    HIN = scr("HIN", [NT, 2, 64, D], BF16)
    YN = scr("YN", [R, D], BF16)
    XMID = scr("XMID", [R, D], F32)
    H2T = scr("H2T", [NT, 128, D], BF16)

    def G(name, shape, dt):
        return nc.alloc_sbuf_tensor(name, list(shape), dt).ap()
    PB = [nc.alloc_psum_tensor("pb%d" % i, [128, 512], F32).ap() for i in range(8)]
    pst = {"i": 0, "n": 8, "a": 0}

    def bank():
        sl = PIPE["slot"]
        if sl is not None and PIPE["depth"] >= 2:
            nb_ = 8 // PIPE["depth"]
            k = "s%d" % sl
            pst[k] = (pst.get(k, 0) + 1) % nb_
            j = sl * nb_ + pst[k]
            return PB[j], "pb%d" % j
        pst["i"] = (pst["i"] + 1) % pst["n"]
        return PB[pst["i"]], "pb%d" % pst["i"]

    def bank_acc():
        pst["a"] = 1 - pst["a"]
        return PB[6 + pst["a"]], "pb%d" % (6 + pst["a"])

    ident = G("ident", [128, 128], BF16)
    ident32 = G("ident32", [128, 128], F32)
    ones32 = G("ones32", [128, 128], F32)
    tincl = G("tincl", [128, 128], F32)
    texcl = G("texcl", [128, 128], F32)
    epsT = G("epsT", [128, 1], F32)
    oneT = G("oneT", [128, 1], F32)
    zeroT = G("zeroT", [128, 1408], BF16)
    zero32 = G("zero32", [128, 1536], F32)
    MOD = [G("mod%d" % m, [128, 6 * D], F32) for m in range(2)]
    SH1, GSC1, G1, SH2, GSC2, G2 = range(6)

    def modp(m, part):
        return MOD[m][:, part * D:(part + 1) * D]

    def memset(eng, ap, val, key):
        S.op(eng, lambda e: e.memset(ap, val), writes=[key])

    def asel(ap, key, pattern, cmp, fill, base, cm):
        S.op("pool", lambda e: e.affine_select(ap, ap, pattern, cmp, fill, base=base, channel_multiplier=cm),
             reads=[key], writes=[key])

    memset("pool", ident, 1.0, "ident"); asel(ident, "ident", [[-1, 128]], ALU.is_equal, 0.0, 0, 1)
    memset("pool", ident32, 1.0, "ident32"); asel(ident32, "ident32", [[-1, 128]], ALU.is_equal, 0.0, 0, 1)
    memset("dve", ones32, 1.0, "ones32")
    memset("pool", tincl, 1.0, "tincl"); asel(tincl, "tincl", [[1, 128]], ALU.is_ge, 0.0, 0, -1)
    memset("pool", texcl, 1.0, "texcl"); asel(texcl, "texcl", [[1, 128]], ALU.is_gt, 0.0, 0, -1)
    memset("dve", epsT, EPS, "epsT"); memset("dve", oneT, 1.0, "oneT")
    mkt = G("mkt", [128, 16], F32)
    S.dma("sp", mkt, mk, "mkt", writes=["mkt"])
    memset("dve", zeroT, 0.0, "zeroT"); memset("dve", zero32, 0.0, "zero32")

    def copy(eng, out, in_, reads, writes):
        if eng == "act":
            S.op("act", lambda e: e.copy(out, in_), reads=reads, writes=writes)
        else:
            S.op(eng, lambda e: e.tensor_copy(out, in_), reads=reads, writes=writes)

    def tt(eng, out, a, b, op, reads, writes):
        S.op(eng, lambda e: e.tensor_tensor(out, a, b, op), reads=reads, writes=writes)

    def transpose_to(dst, dkey, src, skey, n, ceng="act", w=128):
        done = 0
        while done < n:
            m = min(8, n - done)
            pb, pk = bank()
            pbb = pb.bitcast(BF16)
            for k in range(m):
                S.op("pe", lambda e, k=k, done=done, pbb=pbb: e.transpose(
                    pbb[0:w, k * 128:(k + 1) * 128], src[:, (done + k) * w:(done + k + 1) * w], ident),
                    reads=[skey, "ident"], writes=[pk])
            copy(ceng, dst[0:w, done * 128:(done + m) * 128], pbb[0:w, 0:m * 128], [pk], [dkey])
            done += m

    def load(dst, src, key, reads=(), **kw):
        S.dma("sp", dst, src, key, reads=reads, writes=[key], **kw)

    def loadc(dst, src, key, reads=()):
        S.dma("pool", dst, src, key, reads=reads, writes=[key])

    def store(dst, src, key, writes):
        S.dma("pool", dst, src, key, reads=[key], writes=writes)

    def rstd_of(st, stk, src, skey, n, junk, jkey):
        S.op("act", lambda e: e.activation(junk, src, AF.Square, accum_out=st[:, 0:1]),
             reads=[skey], writes=[jkey, stk])
        S.op("act", lambda e: e.activation(st[:, 1:2], st[:, 0:1], AF.Ln, bias=epsT, scale=1.0 / n),
             reads=[stk, "epsT"], writes=[stk])
        S.op("act", lambda e: e.activation(st[:, 2:3], st[:, 1:2], AF.Exp, scale=-0.5),
             reads=[stk], writes=[stk])
        return st[:, 2:3]

    def stt(eng, out, a, sc, b, op0, op1, reads, writes):
        S.op(eng, lambda e: e.scalar_tensor_tensor(out, a, sc, b, op0, op1), reads=reads, writes=writes)

    es_keep = ExitStack()
    with nullcontext(es_keep) as es:
        def L(name, shape, dt):
            return es.enter_context(nc.sbuf_tensor(name, list(shape), dt)).ap()
        cT = L("cT", [128, 2, 8], F32)
        cS = L("cS", [128, 2, 8], F32)
        cB = L("cB", [128, 16, 128], BF16)
        wb = Rot([L("s0w%d" % i, [128, 8, 512], BF16) for i in range(2)], "s0w")
        bb = Rot([L("s0b%d" % i, [128, 512], F32) for i in range(2)], "s0b")
        gab = L("gab", [128, D], F32)
        gfb = L("gfb", [128, D], F32)
        load(cT, cvec.rearrange("m (k p) -> p m k", p=128), "cT", allow_slow_non_contiguous=True)
        load(gab, ga.partition_broadcast(128), "gab")
        load(gfb, gf.partition_broadcast(128), "gfb")
        S.op("act", lambda e: e.activation(cS, cT, AF.Silu), reads=["cT"], writes=["cS"])
        for m in range(2):
            for k in range(8):
                copy("dve", cB[:, m * 8 + k, :], cS[:, m, k:k + 1].to_broadcast([128, 128]), ["cS"], ["cB"])
        for j in range(12):
            wt, wk = wb.next()
            loadc(wt, w_ada[:, j * 512:(j + 1) * 512].rearrange("(k p) n -> p k n", p=128), wk)
            bt, bk = bb.next()
            load(bt, b_ada[j * 512:(j + 1) * 512].partition_broadcast(128), bk)
            for m in range(2):
                pb, pk = bank()
                for k in range(8):
                    S.op("pe", lambda e, k=k, m=m, pb=pb, wt=wt: e.matmul(pb, cB[:, m * 8 + k, :], wt[:, k, :], start=(k == 0), stop=(k == 7)),
                         reads=["cB", wk], writes=[pk])
                tt("dve", MOD[m][:, j * 512:(j + 1) * 512], pb, bt, ALU.add, [pk, bk], ["mod%d" % m])
        for m in range(2):
            stt("dve", modp(m, GSC1), modp(m, GSC1), 1.0, gab, ALU.add, ALU.mult, ["mod%d" % m, "gab"], ["mod%d" % m])
            stt("dve", modp(m, GSC2), modp(m, GSC2), 1.0, gfb, ALU.add, ALU.mult, ["mod%d" % m, "gfb"], ["mod%d" % m])
        if STOP[0] == 1:
            S.finish(); return nc, S

    with nullcontext(es_keep) as es:
        def L(name, shape, dt):
            return es.enter_context(nc.sbuf_tensor(name, list(shape), dt)).ap()
        xb = Rot([L("s1x%d" % i, [128, D], F32) for i in range(2)], "s1x")
        jb = Rot([L("s1j%d" % i, [128, D], F32) for i in range(2)], "s1j")
        hbb = Rot([L("s1h%d" % i, [128, D], BF16) for i in range(2)], "s1h")
        hTb = Rot([L("s1t%d" % i, [128, D], BF16) for i in range(2)], "s1t")
        stb = Rot([L("s1s%d" % i, [128, 4], F32) for i in range(2)], "s1s")
        def body(gt):
            si, _ = seq_of_tile(gt)
            yield
            m = seqs[si][2]
            yield
            xt, xk = xb.next(); load(xt, xrows(gt), xk)
            yield
            st, stk = stb.next(); junk, jk = jb.next(); hb, hk = hbb.next(); hT, hTk = hTb.next()
            yield
            r = rstd_of(st, stk, xt, xk, D, junk, jk)
            yield
            stt("dve", junk, xt, r, modp(m, GSC1), ALU.mult, ALU.mult, [xk, stk, "mod%d" % m], [jk])
            yield
            tt("pool", hb, junk, modp(m, SH1), ALU.add, [jk, "mod%d" % m], [hk])
            yield
            transpose_to(hT, hTk, hb, hk, 8, "act")
            yield
            store(HT[gt], hT, hTk, [("HT", gt)])
            yield
        pipeline(body, range(NT), PDEPTH)
        if STOP[0] == 2:
            S.finish(); return nc, S

    def proj_stage(SRC, W, groups, DST, dst_dt, tag):
        with ExitStack() as es:
            def L(name, shape, dt):
                return es.enter_context(nc.sbuf_tensor(name, list(shape), dt)).ap()
            wb = Rot([L(tag + "w%d" % i, [128, 8, 512], BF16) for i in range(3)], tag + "w")
            hall = L(tag + "hall", [128, NT, D], BF16)
            for gt in range(NT):
                load(hall[:, gt, :], SRC[gt], tag + "hall%d" % gt, reads=[("HT", gt)])
            ob = Rot([L(tag + "o%d" % i, [128, 512], dst_dt) for i in range(3)], tag + "o")
            blks = []
            for (g0, g1, tl) in groups:
                c0 = g0
                while c0 < g1:
                    cwid = min(512, g1 - c0)
                    blks.append((c0, cwid, tl))
                    c0 += cwid

            def pj_wload(bi):
                c0, cwid, tl = blks[bi]
                wt, wk = wb.next()
                loadc(wt[:, :, 0:cwid], W[:, c0:c0 + cwid].rearrange("(k p) n -> p k n", p=128), wk)
                return wt, wk
            pj_next = pj_wload(0)
            for bi, (c0, cwid, tl) in enumerate(blks):
                wt, wk = pj_next
                if bi + 1 < len(blks):
                    pj_next = pj_wload(bi + 1)
                for gt in tl:
                    ht, hk = hall[:, gt, :], tag + "hall%d" % gt
                    pb, pk = bank()
                    for k in range(8):
                        S.op("pe", lambda e, k=k, pb=pb, ht=ht, wt=wt, cwid=cwid: e.matmul(
                            pb[:, 0:cwid], ht[:, k * 128:(k + 1) * 128], wt[:, k, 0:cwid], start=(k == 0), stop=(k == 7)),
                            reads=[hk, wk], writes=[pk])
                    ot, ok = ob.next()
                    copy("act" if (gt + bi) % 2 else "dve", ot[:, 0:cwid], pb[:, 0:cwid], [pk], [ok])
                    pr = prow(gt)
                    store(DST[pr:pr + 128, c0:c0 + cwid], ot[:, 0:cwid], ok, [(tag, gt, bi)])
            S.barrier()
            if STOP[0] == 3:
                return True

    for si, (r0, Ls, m) in enumerate(seqs):
        for pr in (r0 + 2 * si, r0 + 2 * si + Ls + 1):
            S.dma("sp", PROJ[pr:pr + 1, C_X:C_X + 1536], zero32[0:1, :], "zero32", reads=["zero32"], writes=[("pad", pr)])
    ALLT = list(range(NT))
    s2_groups = [(C_CKV, C_Z, ALLT), (C_X, C_GM, ALLT), (C_CQ, C_CKV, TWIN), (C_Z, C_X, TWIN), (C_GM, IN_COLS, TWIN)]
    if proj_stage(HT, w_in, s2_groups, PROJ, F32, "s2"):
        S.finish(); return nc, S
    es_keep.close()

    with ExitStack() as es:
        def L(name, shape, dt):
            return es.enter_context(nc.sbuf_tensor(name, list(shape), dt)).ap()
        wuq = L("wuq", [128, 2, 1536], BF16); wukv = L("wukv", [128, 2, 2048], BF16)
        qgb = L("qgb", [128, 256], F32); kvgb = L("kvgb", [128, 256], F32)
        loadc(wuq, w_uq.rearrange("(k p) n -> p k n", p=128), "wuq")
        loadc(wukv, w_ukv.rearrange("(k p) n -> p k n", p=128), "wukv")
        load(qgb, qg.partition_broadcast(128), "qgb"); load(kvgb, kvg.partition_broadcast(128), "kvgb")
        inb = Rot([L("s3i%d" % i, [128, 576], F32) for i in range(3)], "s3i")
        stb = Rot([L("s3s%d" % i, [128, 8], F32) for i in range(3)], "s3s")
        jb = Rot([L("s3j%d" % i, [128, 256], F32) for i in range(3)], "s3j")
        cqb = Rot([L("s3c%d" % i, [128, 256], BF16) for i in range(3)], "s3c")
        cqT = Rot([L("s3ct%d" % i, [128, 256], BF16) for i in range(3)], "s3ct")
        qfb = Rot([L("s3q%d" % i, [128, 1536], F32) for i in range(3)], "s3q")
        qbb = Rot([L("s3qb%d" % i, [128, 1536], BF16) for i in range(3)], "s3qb")
        ckb = Rot([L("s3k%d" % i, [128, 256], F32) for i in range(3)], "s3k")
        ckbb = Rot([L("s3kb%d" % i, [128, 256], BF16) for i in range(3)], "s3kb")
        ckT = Rot([L("s3kt%d" % i, [128, 256], BF16) for i in range(3)], "s3kt")
        kvb = Rot([L("s3v%d" % i, [128, 2048], BF16) for i in range(3)], "s3v")
        krf = Rot([L("s3r%d" % i, [128, 64], F32) for i in range(3)], "s3r")
        krb = Rot([L("s3rb%d" % i, [128, 64], BF16) for i in range(3)], "s3rb")
        csb = Rot([L("s3cs%d" % i, [128, 64], F32) for i in range(3)], "s3cs")
        tmpb = Rot([L("s3t%d" % i, [128, 4, 256], F32) for i in range(3)], "s3t")

        def kv_from(ckn32, ck32k, key_row, kr_bf, kr_k):
            cbf, cbk = ckbb.next()
            copy("dve", cbf, ckn32, [ck32k], [cbk])
            ct, ctk = ckT.next()
            transpose_to(ct, ctk, cbf, cbk, 2, "act")
            kv, kvk = kvb.next()
            for nb in range(4):
                pb, pk = bank()
                for k in range(2):
                    S.op("pe", lambda e, k=k, nb=nb, pb=pb, ct=ct: e.matmul(
                        pb, ct[:, k * 128:(k + 1) * 128], wukv[:, k, nb * 512:(nb + 1) * 512], start=(k == 0), stop=(k == 1)),
                        reads=[ctk, "wukv"], writes=[pk])
                copy("act" if nb % 2 else "dve", kv[:, nb * 512:(nb + 1) * 512], pb, [pk], [kvk])
            store(KVS[key_row:key_row + 128, :], kv, kvk, [("KVS", key_row)])
            store(KRS[key_row:key_row + 128, :], kr_bf, kr_k, [("KRS", key_row)])

        def body(gt):
            si, ti = seq_of_tile(gt)
            yield
            r0s, Ls, lat = seqs[si]
            yield
            r = gt * 128
            yield
            pr = prow(gt)
            yield
            it, ik = inb.next()
            if gt in TWINS:
                load(it, PROJ[pr:pr + 128, 0:576], ik)
            else:
                load(it[:, 256:576], PROJ[pr:pr + 128, 256:576], ik)
            yield
            st, stk = stb.next(); junk, jk = jb.next()
            yield
            if lat:
                cs, csk = csb.next()
                t0 = ti * 128
                load(cs[:, 0:32], cos_t[t0:t0 + 128, :], csk); load(cs[:, 32:64], sin_t[t0:t0 + 128, :], csk)
            if gt in TWINS:
                yield
                rq = rstd_of(st, stk, it[:, 0:256], ik, 256, junk, jk)
                yield
                cq, cqk = cqb.next()
                yield
                stt("dve", cq, it[:, 0:256], rq, qgb, ALU.mult, ALU.mult, [ik, stk, "qgb"], [cqk])
                yield
                ct, ctk = cqT.next()
                yield
                transpose_to(ct, ctk, cq, cqk, 2, "act")
                yield
                qf, qfk = qfb.next()
                yield
                for nb in range(3):
                    pb, pk = bank()
                    for k in range(2):
                        S.op("pe", lambda e, k=k, nb=nb, pb=pb, ct=ct: e.matmul(
                            pb, ct[:, k * 128:(k + 1) * 128], wuq[:, k, nb * 512:(nb + 1) * 512], start=(k == 0), stop=(k == 1)),
                            reads=[ctk, "wuq"], writes=[pk])
                    copy("act" if nb % 2 else "dve", qf[:, nb * 512:(nb + 1) * 512], pb, [pk], [qfk])
                yield
                qb, qbk = qbb.next()
                yield
                copy("dve", qb, qf, [qfk], [qbk])
                yield
                if lat:
                    tmp, tk = tmpb.next()
                    qv = qf.rearrange("p (h d) -> p h d", d=192)[:, :, 128:192].rearrange("p h (a t n) -> p h a t n", a=2, t=2)
                    ov = qb.rearrange("p (h d) -> p h d", d=192)[:, :, 128:192].rearrange("p h (a t n) -> p h a t n", a=2, t=2)
                    x0 = qv[:, :, :, 0, :]; x1 = qv[:, :, :, 1, :]
                    cv = cs[:, 0:32].rearrange("p (a n) -> p a n", a=2).unsqueeze(1).to_broadcast([128, 8, 2, 16])
                    sv = cs[:, 32:64].rearrange("p (a n) -> p a n", a=2).unsqueeze(1).to_broadcast([128, 8, 2, 16])
                    tv = [tmp[:, i, :].rearrange("p (h a n) -> p h a n", h=8, a=2) for i in range(4)]
                    tt("dve", tv[0], x0, cv, ALU.mult, [qfk, csk], [tk])
                    tt("dve", tv[1], x1, sv, ALU.mult, [qfk, csk], [tk])
                    tt("dve", tv[2], x0, sv, ALU.mult, [qfk, csk], [tk])
                    tt("dve", tv[3], x1, cv, ALU.mult, [qfk, csk], [tk])
                    tt("dve", ov[:, :, :, 0, :], tv[0], tv[1], ALU.subtract, [tk], [qbk])
                    tt("dve", ov[:, :, :, 1, :], tv[2], tv[3], ALU.add, [tk], [qbk])
                yield
                store(QS[r:r + 128, :], qb, qbk, [("QS", gt)])
            yield
            rk = rstd_of(st[:, 4:8], stk, it[:, 256:512], ik, 256, junk, jk)
            yield
            ck, ckk = ckb.next()
            yield
            stt("dve", ck, it[:, 256:512], rk, kvgb, ALU.mult, ALU.mult, [ik, stk, "kvgb"], [ckk])
            yield
            kf, kfk = krf.next()
            yield
            if lat:
                tmp, tk = tmpb.next()
                kvw = it[:, 512:576].rearrange("p (a t n) -> p a t n", a=2, t=2)
                okv = kf.rearrange("p (a t n) -> p a t n", a=2, t=2)
                x0 = kvw[:, :, 0, :]; x1 = kvw[:, :, 1, :]
                cv = cs[:, 0:32].rearrange("p (a n) -> p a n", a=2)
                sv = cs[:, 32:64].rearrange("p (a n) -> p a n", a=2)
                tv = [tmp[:, i, 0:32].rearrange("p (a n) -> p a n", a=2) for i in range(4)]
                tt("dve", tv[0], x0, cv, ALU.mult, [ik, csk], [tk])
                tt("dve", tv[1], x1, sv, ALU.mult, [ik, csk], [tk])
                tt("dve", tv[2], x0, sv, ALU.mult, [ik, csk], [tk])
                tt("dve", tv[3], x1, cv, ALU.mult, [ik, csk], [tk])
                tt("dve", okv[:, :, 0, :], tv[0], tv[1], ALU.subtract, [tk], [kfk])
                tt("dve", okv[:, :, 1, :], tv[2], tv[3], ALU.add, [tk], [kfk])
            else:
                copy("dve", kf, it[:, 512:576], [ik], [kfk])
                S.dma("sp", nckv[r:r + 128, :], ck, ckk, reads=[ckk], writes=[("nckv", gt)])
                S.dma("sp", nkr[r:r + 128, :], kf, kfk, reads=[kfk], writes=[("nkr", gt)])
            yield
            kb, kbk = krb.next()
            yield
            copy("dve", kb, kf, [kfk], [kbk])
            yield
            kv_from(ck, ckk, krow(gt), kb, kbk)
            yield
        pipeline(body, range(NT), 3)
        for ct_i in range(PAST // 128):
            ck, ckk = ckb.next(); load(ck, cckv[ct_i * 128:(ct_i + 1) * 128, :], ckk)
            kf, kfk = krf.next(); load(kf, ckr[ct_i * 128:(ct_i + 1) * 128, :], kfk)
            kb, kbk = krb.next()
            copy("dve", kb, kf, [kfk], [kbk])
            kv_from(ck, ckk, RPRM + ct_i * 128, kb, kbk)
        S.barrier()
        if STOP[0] == 4:
            S.finish(); return nc, S

    NKMAX = PAST + LS
    ATTT = scr("ATTT", [8, 128, R], BF16)
    with ExitStack() as es:
        def L(name, shape, dt):
            return es.enter_context(nc.sbuf_tensor(name, list(shape), dt)).ap()
        knT = L("knT", [128, 8, NKMAX], BF16)
        krT = L("krT", [128, NKMAX], BF16)
        memset("dve", krT, 0.0, "krT")
        Vt = L("Vt", [128, NKMAX // 128, 8, 128], BF16)
        onesb = L("onesb", [128, 128], BF16)
        memset("dve", onesb, 1.0, "onesb")
        kvt_b = Rot([L("s4kv%d" % i, [128, 2048], BF16) for i in range(2)], "s4kv")
        krt_b = Rot([L("s4kr%d" % i, [128, 64], BF16) for i in range(2)], "s4kr")
        qt_b = Rot([L("s4q%d" % i, [128, 1536], BF16) for i in range(2)], "s4q")
        qnT_b = Rot([L("s4qn%d" % i, [128, 8, 512], BF16) for i in range(2)], "s4qn")
        qrT_b = Rot([L("s4qr%d" % i, [128, 8, 512], BF16) for i in range(2)], "s4qr")
        for i_ in range(2):
            memset("dve", qrT_b.aps[i_], 0.0, "s4qr%d" % i_)
        pT_b = Rot([L("s4p%d" % i, [128, 512], BF16) for i in range(4)], "s4p")
        pacc_b = Rot([L("s4pa%d" % i, [128, 512], F32) for i in range(2)], "s4pa")
        pab_b = Rot([L("s4pb%d" % i, [128, 512], BF16) for i in range(2)], "s4pb")
        rc_b = Rot([L("s4r%d" % i, [128, 512], F32) for i in range(2)], "s4r")
        ao_b = Rot([L("s4a%d" % i, [128, 512], BF16) for i in range(3)], "s4a")
        scale = 1.0 / math.sqrt(192.0)
        pst["n"] = 6
        pst["i"] = 0
        for si, (r0s, Ls, lat) in enumerate(seqs):
            nk = Ls + (PAST if lat else 0)
            nkt = nk // 128
            kr0 = r0s
            for kt in range(nkt):
                kvt, kvk = kvt_b.next(); load(kvt, KVS[kr0 + kt * 128:kr0 + (kt + 1) * 128, :], kvk)
                krt, krk = krt_b.next(); load(krt, KRS[kr0 + kt * 128:kr0 + (kt + 1) * 128, :], krk)
                pb, pk = bank(); pbb = pb.bitcast(BF16)
                for h in range(8):
                    S.op("pe", lambda e, h=h, pbb=pbb, kvt=kvt: e.transpose(pbb[:, h * 128:(h + 1) * 128], kvt[:, h * 256:h * 256 + 128], ident),
                         reads=[kvk, "ident"], writes=[pk])
                copy("act", knT[:, :, kt * 128:(kt + 1) * 128], pbb.rearrange("p (h k) -> p h k", h=8), [pk], ["knT"])
                pb2, pk2 = bank(); pbb2 = pb2.bitcast(BF16)
                S.op("pe", lambda e, pbb2=pbb2, krt=krt: e.transpose(pbb2[0:64, 0:128], krt, ident), reads=[krk, "ident"], writes=[pk2])
                copy("dve", krT[0:64, kt * 128:(kt + 1) * 128], pbb2[0:64, 0:128], [pk2], ["krT"])
                copy("dve", Vt[:, kt, :, :], kvt.rearrange("p (h d) -> p h d", d=256)[:, :, 128:256], [kvk], ["Vt"])
            if lat:
                qblocks = [[NPT + i for i in range(OWN)], [NPT + OWN, NPT + NTS - 1]]
            else:
                qblocks = [[r0s // 128 + i for i in range(Ls // 128)]]
            for qb_tiles in qblocks:
                nq = len(qb_tiles)
                n = nq * 128
                qn, qnk = qnT_b.next(); qr, qrk = qrT_b.next()
                for qi in range(nq):
                    r = qb_tiles[qi] * 128
                    qt, qk = qt_b.next(); load(qt, QS[r:r + 128, :], qk)
                    pb, pk = bank(); pbb = pb.bitcast(BF16)
                    for h in range(8):
                        S.op("pe", lambda e, h=h, pbb=pbb, qt=qt: e.transpose(pbb[:, h * 128:(h + 1) * 128], qt[:, h * 192:h * 192 + 128], ident),
                             reads=[qk, "ident"], writes=[pk])
                    copy("act", qn[:, :, qi * 128:(qi + 1) * 128], pbb.rearrange("p (h k) -> p h k", h=8), [pk], [qnk])
                    pb2, pk2 = bank(); pbb2 = pb2.bitcast(BF16)
                    for h in range(8):
                        S.op("pe", lambda e, h=h, pbb2=pbb2, qt=qt: e.transpose(pbb2[0:64, h * 128:(h + 1) * 128], qt[:, h * 192 + 128:h * 192 + 192], ident),
                             reads=[qk, "ident"], writes=[pk2])
                    copy("dve", qr[0:64, :, qi * 128:(qi + 1) * 128], pbb2[0:64, :].rearrange("p (h k) -> p h k", h=8), [pk2], [qrk])
                for h in range(8):
                    po, pok = bank_acc()
                    pacc, pack = pacc_b.next()
                    def score(kt, h=h, qn=qn, qr=qr, n=n):
                        ps_, psk = bank()
                        S.op("pe", lambda e: e.matmul(
                            ps_[:, 0:n], knT[:, h, kt * 128:(kt + 1) * 128], qn[:, h, 0:n], start=True, stop=False),
                            reads=["knT", qnk], writes=[psk])
                        S.op("pe", lambda e: e.matmul(
                            ps_[:, 0:n], krT[:, kt * 128:(kt + 1) * 128], qr[:, h, 0:n], start=False, stop=True),
                            reads=["krT", qrk], writes=[psk])
                        pt, ptk = pT_b.next()
                        S.op("act", lambda e: e.activation(pt[:, 0:n], ps_[:, 0:n], AF.Exp, scale=scale),
                             reads=[psk], writes=[ptk])
                        return pt, ptk
                    SK = 3
                    pend = [score(kt) for kt in range(min(SK, nkt))]
                    for kt in range(nkt):
                        pt, ptk = pend.pop(0)
                        if kt + SK < nkt:
                            pend.append(score(kt + SK))
                        S.op("pe", lambda e, kt=kt, h=h, po=po, pt=pt, n=n: e.matmul(
                            po[:, 0:n], Vt[:, kt, h, :], pt[:, 0:n], start=(kt == 0), stop=(kt == nkt - 1)),
                            reads=[ptk, "Vt"], writes=[pok])
                        if kt == 0:
                            copy("dve", pacc[:, 0:n], pt[:, 0:n], [ptk], [pack])
                        else:
                            tt("dve", pacc[:, 0:n], pacc[:, 0:n], pt[:, 0:n], ALU.add, [pack, ptk], [pack])
                    pab, pabk = pab_b.next()
                    copy("dve", pab[:, 0:n], pacc[:, 0:n], [pack], [pabk])
                    psm, psmk = bank()
                    S.op("pe", lambda e, psm=psm, pab=pab, n=n: e.matmul(psm[:, 0:n], onesb, pab[:, 0:n], start=True, stop=True),
                         reads=["onesb", pabk], writes=[psmk])
                    rc, rck = rc_b.next()
                    S.op("dve", lambda e, rc=rc, psm=psm, n=n: e.reciprocal(rc[:, 0:n], psm[:, 0:n]), reads=[psmk], writes=[rck])
                    ao, aok = ao_b.next()
                    tt("dve", ao[:, 0:n], po[:, 0:n], rc[:, 0:n], ALU.mult, [pok, rck], [aok])
                    for qi in range(nq):
                        g_ = qb_tiles[qi]
                        store(ATTT[h][:, g_ * 128:(g_ + 1) * 128], ao[:, qi * 128:(qi + 1) * 128], aok, [("ATTT", h, g_)])
            S.barrier()
            if STOP[0] == 5:
                S.finish(); return nc, S
        pst["n"] = 8

    with ExitStack() as es:
        def L(name, shape, dt):
            return es.enter_context(nc.sbuf_tensor(name, list(shape), dt)).ap()
        cwb = [L("cwb%d" % i, [128, 1536], F32) for i in range(3)]
        cbb = L("cbb", [128, 1536], F32)
        for i in range(3):
            load(cwb[i], cw[i].partition_broadcast(128), "cwb%d" % i)
        load(cbb, cb.partition_broadcast(128), "cbb")
        dtbb = L("dtbb", [128, 32], F32); Ab = L("Ab", [128, 32], F32)
        load(dtbb, dtb.partition_broadcast(128), "dtbb")
        load(Ab, alog.partition_broadcast(128), "Ab")
        S.op("act", lambda e: e.activation(Ab, Ab, AF.Exp), reads=["Ab"], writes=["Ab"])
        S.op("dve", lambda e: e.tensor_scalar(Ab, Ab, -1.0, None, ALU.mult), reads=["Ab"], writes=["Ab"])
        mfb = L("mfb", [32, 2], F32)
        memset("pool", mfb, 1.0, "mfb")
        asel(mfb[:, 0:1], "mfb", [[0, 1]], ALU.is_gt, 0.0, 16, -1)
        S.op("pool", lambda e: e.affine_select(mfb[:, 1:2], mfb[:, 1:2], [[0, 1]], ALU.is_ge, 0.0, base=-16, channel_multiplier=1),
             reads=["mfb"], writes=["mfb"])
        S.op("dve", lambda e: e.tensor_scalar(mfb[:, 1:2], mfb[:, 1:2], -1.0, None, ALU.mult), reads=["mfb"], writes=["mfb"])
        a_b = [Rot([L("s5a%d_%d" % (j, i), [128, 1536], F32) for i in range(2)], "s5a%d_" % j) for j in range(3)]
        t_b = Rot([L("s5t%d" % i, [128, 1536], F32) for i in range(2)], "s5t")
        t2_b = Rot([L("s5u%d" % i, [128, 1536], F32) for i in range(2)], "s5u")
        xo_b = Rot([L("s5x%d" % i, [128, 1536], BF16) for i in range(2)], "s5x")
        d_b = Rot([L("s5d%d" % i, [128, 32], F32) for i in range(2)], "s5d")
        w_b = Rot([L("s5w%d" % i, [128, 8, 32], F32) for i in range(2)], "s5w")
        so_b = Rot([L("s5s%d" % i, [128, 160], F32) for i in range(2)], "s5s")
        g_b = Rot([L("s5g%d" % i, [32, 128], F32) for i in range(2)], "s5g")
        g2_b = Rot([L("s5h%d" % i, [32, 128], F32) for i in range(2)], "s5h")
        def body(gt):
            pr = prow(gt)
            yield
            r = gt * 128
            yield
            av = []
            yield
            si5, ti5 = seq_of_tile(gt)
            lat5 = seqs[si5][2]
            for j in range(3):
                a, ak_ = a_b[j].next()
                if lat5 and j == 0 and ti5 == 0:
                    prl = prow(NPT + NTS - 1) + 127
                    load(a[0:1, :], PROJ[prl:prl + 1, C_X:C_X + 1536], ak_)
                    load(a[1:128, :], PROJ[pr:pr + 127, C_X:C_X + 1536], ak_)
                elif lat5 and j == 2 and ti5 == NTS - 1:
                    prf = prow(NPT)
                    load(a[0:127, :], PROJ[pr + 1:pr + 128, C_X:C_X + 1536], ak_)
                    load(a[127:128, :], PROJ[prf:prf + 1, C_X:C_X + 1536], ak_)
                else:
                    load(a, PROJ[pr - 1 + j:pr - 1 + j + 128, C_X:C_X + 1536], ak_)
                if lat5 and j == 0 and ti5 % OWN == 0:
                    b5 = ti5 // OWN
                    S.op("dve", lambda e, a=a, b5=b5: e.tensor_scalar(a, a, mkt[:, b5:b5 + 1], None, ALU.mult), reads=[ak_, "mkt"], writes=[ak_])
                if lat5 and j == 2 and (ti5 + 1) % OWN == 0:
                    b5 = ((ti5 + 1) % NTS) // OWN
                    S.op("dve", lambda e, a=a, b5=b5: e.tensor_scalar(a, a, mkt[:, 4 + b5:5 + b5], None, ALU.mult), reads=[ak_, "mkt"], writes=[ak_])
                av.append((a, ak_))
            yield
            t, tk = t_b.next(); t2, t2k = t2_b.next()
            yield
            tt("dve", t, av[1][0], cwb[1], ALU.mult, [av[1][1], "cwb1"], [tk])
            yield
            tt("pool", t2, av[0][0], cwb[0], ALU.mult, [av[0][1], "cwb0"], [t2k])
            yield
            tt("dve", t, t, t2, ALU.add, [tk, t2k], [tk])
            yield
            tt("pool", t2, av[2][0], cwb[2], ALU.mult, [av[2][1], "cwb2"], [t2k])
            yield
            tt("dve", t, t, t2, ALU.add, [tk, t2k], [tk])
            yield
            tt("dve", t, t, cbb, ALU.add, [tk, "cbb"], [tk])
            yield
            xo, xok = xo_b.next()
            yield
            S.op("act", lambda e, xo=xo, t=t: e.activation(xo, t, AF.Silu), reads=[tk], writes=[xok])
            yield
            store(XBC[r:r + 128, :], xo, xok, [("XBC", gt)])
            yield
            dt_, dk = d_b.next(); load(dt_, PROJ[pr:pr + 128, C_DT:C_DT + 32], dk)
            yield
            w, wk = w_b.next()
            yield
            V, E, DT, LN, A_, BL, GM, TMP = [w[:, i, :] for i in range(8)]
            yield
            tt("dve", V, dt_, dtbb, ALU.add, [dk, "dtbb"], [wk])
            yield
            S.op("act", lambda e, E=E, V=V: e.activation(E, V, AF.Exp), reads=[wk], writes=[wk])
            yield
            S.op("act", lambda e, E=E, DT=DT: e.activation(DT, E, AF.Ln, bias=oneT), reads=[wk, "oneT"], writes=[wk])
            yield
            S.op("act", lambda e, LN=LN, DT=DT: e.activation(LN, DT, AF.Ln), reads=[wk], writes=[wk])
            yield
            tt("dve", A_, DT, Ab, ALU.mult, [wk, "Ab"], [wk])
            yield
            pb, pk = bank()
            yield
            S.op("pe", lambda e, pb=pb, A_=A_: e.matmul(pb[:, 0:16], tincl, A_[:, 0:16], start=True, stop=True), reads=["tincl", wk], writes=[pk])
            yield
            S.op("pe", lambda e, pb=pb, A_=A_: e.matmul(pb[:, 16:32], texcl, A_[:, 16:32], start=True, stop=True), reads=["texcl", wk], writes=[pk])
            yield
            S.op("pe", lambda e, pb=pb, A_=A_: e.matmul(pb[:, 32:64], ones32, A_, start=True, stop=True), reads=["ones32", wk], writes=[pk])
            yield
            copy("dve", GM[:, 0:16], pb[:, 0:16], [pk], [wk])
            yield
            S.op("dve", lambda e, GM=GM, pb=pb: e.tensor_scalar(GM[:, 16:32], pb[:, 16:32], -1.0, None, ALU.mult), reads=[pk], writes=[wk])
            yield
            so, sok = so_b.next()
            yield
            tt("dve", so[:, 0:32], LN, GM, ALU.subtract, [wk], [sok])
            yield
            tt("dve", TMP[:, 0:16], so[:, 0:16], pb[:, 32:48], ALU.add, [sok, pk], [wk])
            yield
            copy("dve", TMP[:, 16:32], so[:, 16:32], [sok], [wk])
            yield
            S.op("act", lambda e, so=so, TMP=TMP: e.activation(so[:, 32:64], TMP, AF.Exp), reads=[wk], writes=[sok])
            yield
            copy("dve", TMP[:, 0:16], GM[:, 0:16], [wk], [wk])
            yield
            tt("dve", TMP[:, 16:32], GM[:, 16:32], pb[:, 48:64], ALU.add, [wk, pk], [wk])
            yield
            S.op("act", lambda e, so=so, TMP=TMP: e.activation(so[:, 64:96], TMP, AF.Exp), reads=[wk], writes=[sok])
            yield
            S.op("act", lambda e, so=so, pb=pb: e.activation(so[:, 96:128], pb[:, 32:64], AF.Exp), reads=[pk], writes=[sok])
            yield
            copy("dve", so[:, 128:144], A_[:, 0:16], [wk], [sok])
            yield
            S.op("dve", lambda e, so=so, A_=A_: e.tensor_scalar(so[:, 144:160], A_[:, 16:32], -1.0, None, ALU.mult), reads=[wk], writes=[sok])
            yield
            store(STT[r:r + 128, :], so, sok, [("STT", gt)])
            yield
        pipeline(body, range(NT), PDEPTH)
        S.barrier()
        if STOP[0] == 6:
            S.finish(); return nc, S

    with ExitStack() as es:
        def L(name, shape, dt):
            return es.enter_context(nc.sbuf_tensor(name, list(shape), dt)).ap()
        hst_b = Rot([L("hst%d" % i, [64, D], F32) for i in range(2)], "hst")
        hbf_b = Rot([L("s6hb%d" % i, [64, D], BF16) for i in range(2)], "s6hb")
        xb_b = Rot([L("s6x%d" % i, [128, 1536], BF16) for i in range(4)], "s6x")
        so_b = Rot([L("s6s%d" % i, [128, 160], F32) for i in range(4)], "s6s")
        xw_b = Rot([L("s6w%d" % i, [128, D], BF16) for i in range(4)], "s6w")
        sti_b = Rot([L("sti%d" % i, [64, 16, 64], F32) for i in range(2)], "sti")
        sto_b = Rot([L("sto%d" % i, [64, 16, 64], F32) for i in range(2)], "sto")
        h0_b = Rot([L("h0t%d" % i, [64, D], F32) for i in range(2)], "h0t")
        def chain(arg):
            si, d = arg
            r0s, Ls, lat = seqs[si]
            nch = Ls // 128
            hst, hstk = hst_b.next()
            sti, stik = sti_b.next()
            sto, stok = sto_b.next()
            h0t, h0k = h0_b.next()
            yield
            if True:
                if lat:
                    load(sti, st_in[d].rearrange("h p n -> p h n"), stik)
                    for half in range(2):
                        pb, pk = bank(); pb2, pk2 = bank()
                        for hh in range(8):
                            h = half * 8 + hh
                            tgt = pb if hh < 4 else pb2
                            S.op("pe", lambda e, h=h, hh=hh, tgt=tgt: e.transpose(tgt[0:64, (hh % 4) * 64:(hh % 4) * 64 + 64], sti[:, h, :], ident32[0:64, 0:64]),
                                 reads=[stik, "ident32"], writes=[pk if hh < 4 else pk2])
                        copy("dve", h0t[:, half * 512:half * 512 + 256], pb[0:64, 0:256], [pk], [h0k])
                        copy("dve", h0t[:, half * 512 + 256:half * 512 + 512], pb2[0:64, 0:256], [pk2], [h0k])
                    copy("dve", hst, h0t, [h0k], [hstk])
                else:
                    memset("dve", hst, 0.0, hstk)
                order = list(range(nch)) if d == 0 else list(range(nch - 1, -1, -1))
                if lat:
                    order = order + order
                def prep(c):
                    r_ = (r0s // 128 + c) * 128
                    xb_, xbk = xb_b.next(); load(xb_, XBC[r_:r_ + 128, :], xbk)
                    so, sok = so_b.next(); load(so, STT[r_:r_ + 128, :], sok)
                    xw, xwk = xw_b.next()
                    tt("pool", xw.rearrange("p (h q) -> p h q", h=16), xb_[:, 0:1024].rearrange("p (h q) -> p h q", h=16),
                       so[:, 32 + d * 16:48 + d * 16].unsqueeze(2).to_broadcast([128, 16, 64]), ALU.mult, [xbk, sok], [xwk])
                    pA, pAk = bank(); pB_, pBk = bank()
                    for g in range(4):
                        tgt, tk_ = (pA, pAk) if g < 2 else (pB_, pBk)
                        S.op("pe", lambda e, g=g, tgt=tgt, xb_=xb_, xw=xw: e.matmul(
                            tgt[0:64, (g % 2) * 256:(g % 2) * 256 + 256], xb_[:, 1024 + g * 64:1024 + (g + 1) * 64], xw[:, g * 256:(g + 1) * 256], start=True, stop=True),
                            reads=[xbk, xwk], writes=[tk_])
                    return so, sok, pA, pAk, pB_, pBk
                pend_ = prep(order[0])
                yield
                for oi_, c in enumerate(order):
                    gt = r0s // 128 + c
                    so, sok, pA, pAk, pB_, pBk = pend_
                    if lat:
                        bb_ = None
                        if d == 0 and c % OWN == 0:
                            bb_ = c // OWN
                        if d == 1 and (c + 1) % OWN == 0:
                            bb_ = ((c + 1) % NTS) // OWN
                        if bb_ is not None:
                            S.op("dve", lambda e, bb_=bb_: e.tensor_scalar(hst, hst, mkt[0:64, 8 + bb_:9 + bb_], None, ALU.mult),
                                 reads=[hstk, "mkt"], writes=[hstk])
                            stt("dve", hst, h0t, mkt[0:64, 12 + bb_:13 + bb_], hst, ALU.mult, ALU.add, [h0k, "mkt", hstk], [hstk])
                    if (not lat) or oi_ >= nch:
                        hb, hbk = hbf_b.next()
                        copy("act", hb, hst, [hstk], [hbk])
                        store(HIN[gt, d], hb, hbk, [("HIN", gt, d)])
                    yield
                    if oi_ + 1 < len(order):
                        pend_ = prep(order[oi_ + 1])
                        yield
                    tt("dve", hst.rearrange("p (h q) -> p h q", h=16), hst.rearrange("p (h q) -> p h q", h=16),
                       so[0:64, 96 + d * 16:112 + d * 16].unsqueeze(2).to_broadcast([64, 16, 64]), ALU.mult, [hstk, sok], [hstk])
                    tt("dve", hst[:, 0:512], hst[:, 0:512], pA[0:64, :], ALU.add, [hstk, pAk], [hstk])
                    tt("dve", hst[:, 512:1024], hst[:, 512:1024], pB_[0:64, :], ALU.add, [hstk, pBk], [hstk])
                    yield
                if not lat:
                    for half in range(2):
                        pb, pk = bank(); pb2, pk2 = bank()
                        for hh in range(8):
                            h = half * 8 + hh
                            tgt = pb if hh < 4 else pb2
                            S.op("pe", lambda e, h=h, hh=hh, tgt=tgt: e.transpose(tgt[0:64, (hh % 4) * 64:(hh % 4) * 64 + 64], hst[:, h * 64:(h + 1) * 64], ident32[0:64, 0:64]),
                                 reads=[hstk, "ident32"], writes=[pk if hh < 4 else pk2])
                        copy("dve", sto[:, half * 8:half * 8 + 4, :], pb[0:64, 0:256].rearrange("p (h n) -> p h n", h=4), [pk], [stok])
                        copy("dve", sto[:, half * 8 + 4:half * 8 + 8, :], pb2[0:64, 0:256].rearrange("p (h n) -> p h n", h=4), [pk2], [stok])
                    S.dma("sp", nssd[si, d].rearrange("h p n -> p h n"), sto, stok, reads=[stok], writes=[("nssd", si, d)])
        pipeline(chain, [(si_, d_) for si_ in range(NSEQ) for d_ in range(2)], PDEPTH)
        S.barrier()
        if STOP[0] == 7:
            S.finish(); return nc, S
    with ExitStack() as es:
        def L(name, shape, dt):
            return es.enter_context(nc.sbuf_tensor(name, list(shape), dt)).ap()
        xb_b = Rot([L("s6ox%d" % i, [128, 1536], BF16) for i in range(2)], "s6ox")
        so_b = Rot([L("s6os%d" % i, [128, 160], F32) for i in range(2)], "s6os")
        mneg = L("mneg", [128, 2, 4, 128], F32)
        memset("pool", mneg, 0.0, "mneg")
        asel(mneg[:, 0], "mneg", [[0, 4], [1, 128]], ALU.is_ge, NEG, 0, -1)
        asel(mneg[:, 1], "mneg", [[0, 4], [-1, 128]], ALU.is_ge, NEG, 0, 1)
        abc_b = Rot([L("abc%d" % i, [128, 32, 128], F32) for i in range(2)], "abc")
        Dbc = L("Dbc", [128, 16], F32); load(Dbc, Dv.partition_broadcast(128), "Dbc")
        gnb = L("gnb", [128, D], F32); load(gnb, gn.partition_broadcast(128), "gnb")
        gm_b = Rot([L("s6g%d" % i, [32, 128], F32) for i in range(2)], "s6g")
        hin_b = [Rot([L("s6i%d_%d" % (d, i), [128, D], BF16) for i in range(2)], "s6i%d_" % d) for d in range(2)]
        z_b = Rot([L("s6z%d" % i, [128, D], F32) for i in range(2)], "s6z")
        bc_b = Rot([L("s6bc%d" % i, [128, 512], BF16) for i in range(2)], "s6bc")
        scs_b = Rot([L("s6sc%d" % i, [128, 4, 128], BF16) for i in range(2)], "s6sc")
        lt_b = Rot([L("s6l%d" % i, [128, 4, 128], BF16) for i in range(2)], "s6l")
        mt_b = Rot([L("s6m%d" % i, [128, 16, 128], BF16) for i in range(2)], "s6m")
        mb_b = Rot([L("s6n%d" % i, [128, 4, 128], BF16) for i in range(2)], "s6n")
        y_b = Rot([L("s6y%d" % i, [128, D], F32) for i in range(2)], "s6y")
        y2_b = Rot([L("s6v%d" % i, [128, D], F32) for i in range(2)], "s6v")
        y3_b = Rot([L("s6u%d" % i, [128, D], F32) for i in range(2)], "s6u")
        yo_b = Rot([L("s6o%d" % i, [128, D], BF16) for i in range(2)], "s6o")
        st_b = Rot([L("s6t%d" % i, [128, 4], F32) for i in range(2)], "s6t")
        def body(gt):
            r = gt * 128
            yield
            pr = prow(gt)
            yield
            xb_, xbk = xb_b.next(); load(xb_, XBC[r:r + 128, :], xbk)
            yield
            so, sok = so_b.next(); load(so, STT[r:r + 128, :], sok)
            yield
            abc, abck = abc_b.next()
            yield
            copy("dve", abc, so[:, 128:160].unsqueeze(2).to_broadcast([128, 32, 128]), [sok], [abck])
            yield
            hin = []
            yield
            for d in range(2):
                hi, hik = hin_b[d].next()
                load(hi[0:64, :], HIN[gt, d], hik); load(hi[64:128, :], HIN[gt, d], hik)
                hin.append((hi, hik))
            yield
            z, zk = z_b.next(); load(z, PROJ[pr:pr + 128, C_Z:C_Z + 1024], zk)
            yield
            bc, bck = bc_b.next()
            yield
            transpose_to(bc, bck, xb_[:, 1024:1536], xbk, 4, "act")
            yield
            pS2 = [bank(), bank()]
            yield
            scs, sck = scs_b.next()
            yield
            for g in range(4):
                p0 = (g % 2) * 64
                pS, pSk = pS2[g % 2]
                S.op("pe", lambda e, g=g, p0=p0, pS=pS, bc=bc: e.matmul(
                    pS[:, (g // 2) * 128:(g // 2) * 128 + 128], bc[p0:p0 + 64, (g // 2) * 128:(g // 2) * 128 + 128],
                    bc[p0:p0 + 64, (2 + g // 2) * 128:(2 + g // 2) * 128 + 128], start=True, stop=True),
                    reads=[bck], writes=[pSk])
            yield
            for g in range(4):
                pS, pSk = pS2[g % 2]
                copy("dve", scs[:, g, :], pS[:, (g // 2) * 128:(g // 2) * 128 + 128], [pSk], [sck])
            yield
            mt, mtk = mt_b.next()
            yield
            for d in range(2):
                for g in range(4):
                    pL, pLk = bank()
                    S.op("pe", lambda e, d=d, pL=pL: e.matmul(pL, ident32, mneg[:, d].rearrange("p a k -> p (a k)"), start=True, stop=False),
                         reads=["ident32", "mneg"], writes=[pLk])
                    for hh in range(4):
                        dh = d * 16 + g * 4 + hh
                        S.op("pe", lambda e, hh=hh, dh=dh, pL=pL, d=d: e.matmul(
                            pL[:, hh * 128:(hh + 1) * 128], abc[:, dh, :], (tincl if d == 0 else texcl), start=False, stop=(hh == 3)),
                            reads=[abck, "tincl", "texcl"], writes=[pLk])
                    lt, ltk = lt_b.next()
                    for hh in range(4):
                        dh = d * 16 + g * 4 + hh
                        S.op("act", lambda e, hh=hh, dh=dh, lt=lt, pL=pL, so=so: e.activation(
                            lt[:, hh, :], pL[:, hh * 128:(hh + 1) * 128], AF.Exp, bias=so[:, dh:dh + 1]),
                            reads=[pLk, sok], writes=[ltk])
                    if d == 0:
                        tt("dve", mt[:, g * 4:(g + 1) * 4, :], lt, scs[:, g:g + 1, :].to_broadcast([128, 4, 128]), ALU.mult, [ltk, sck], [mtk])
                    else:
                        mb_, mbk = mb_b.next()
                        tt("dve", mb_, lt, scs[:, g:g + 1, :].to_broadcast([128, 4, 128]), ALU.mult, [ltk, sck], [mbk])
                        tt("pool", mt[:, g * 4:(g + 1) * 4, :], mt[:, g * 4:(g + 1) * 4, :], mb_, ALU.add, [mtk, mbk], [mtk])
            yield
            pY = [bank(), bank()]
            yield
            for h in range(16):
                tgt, tk_ = pY[h // 8]
                S.op("pe", lambda e, h=h, tgt=tgt, mt=mt, xb_=xb_: e.matmul(
                    tgt[:, (h % 8) * 64:(h % 8) * 64 + 64], mt[:, h, :], xb_[:, h * 64:(h + 1) * 64], start=True, stop=True),
                    reads=[mtk, xbk], writes=[tk_])
            yield
            y, yk = y_b.next()
            yield
            copy("act", y[:, 0:512], pY[0][0], [pY[0][1]], [yk])
            yield
            copy("act", y[:, 512:1024], pY[1][0], [pY[1][1]], [yk])
            yield
            y2, y2k = y2_b.next()
            yield
            for d in range(2):
                hi, hik = hin[d]
                pZ = [bank(), bank()]
                for g in range(4):
                    p0 = (g % 2) * 64
                    tgt, tk_ = pZ[g % 2]
                    S.op("pe", lambda e, g=g, p0=p0, tgt=tgt, bc=bc, hi=hi: e.matmul(
                        tgt[:, (g // 2) * 256:(g // 2) * 256 + 256], bc[p0:p0 + 64, (2 + g // 2) * 128:(2 + g // 2) * 128 + 128],
                        hi[p0:p0 + 64, g * 256:(g + 1) * 256], start=True, stop=True),
                        reads=[bck, hik], writes=[tk_])
                for g in range(4):
                    tgt, tk_ = pZ[g % 2]
                    src = tgt[:, (g // 2) * 256:(g // 2) * 256 + 256].rearrange("p (h q) -> p h q", h=4)
                    scv = so[:, 64 + d * 16 + g * 4:64 + d * 16 + g * 4 + 4].unsqueeze(2).to_broadcast([128, 4, 64])
                    if d == 0:
                        tt("dve", y2[:, g * 256:(g + 1) * 256].rearrange("p (h q) -> p h q", h=4), src, scv, ALU.mult, [tk_, sok], [y2k])
                    else:
                        y3, y3k = y3_b.next()
                        tt("dve", y3[:, 0:256].rearrange("p (h q) -> p h q", h=4), src, scv, ALU.mult, [tk_, sok], [y3k])
                        tt("pool", y2[:, g * 256:(g + 1) * 256], y2[:, g * 256:(g + 1) * 256], y3[:, 0:256], ALU.add, [y2k, y3k], [y2k])
            yield
            y3, y3k = y3_b.next()
            yield
            tt("dve", y3.rearrange("p (h q) -> p h q", h=16), xb_[:, 0:1024].rearrange("p (h q) -> p h q", h=16),
               Dbc.unsqueeze(2).to_broadcast([128, 16, 64]), ALU.mult, [xbk, "Dbc"], [y3k])
            yield
            tt("pool", y2, y2, y3, ALU.add, [y2k, y3k], [y2k])
            yield
            tt("dve", y, y, y2, ALU.add, [yk, y2k], [yk])
            yield
            S.op("act", lambda e, z=z: e.activation(z, z, AF.Silu), reads=[zk], writes=[zk])
            yield
            tt("dve", y, y, z, ALU.mult, [yk, zk], [yk])
            yield
            st, stk = st_b.next()
            yield
            rr = rstd_of(st, stk, y, yk, D, y2, y2k)
            yield
            yo, yok = yo_b.next()
            yield
            stt("dve", yo, y, rr, gnb, ALU.mult, ALU.mult, [yk, stk, "gnb"], [yok])
            yield
            store(YN[r:r + 128, :], yo, yok, [("YN", gt)])
            yield
        pipeline(body, TWIN, PDEPTH)
        S.barrier()
        if STOP[0] == 8:
            S.finish(); return nc, S

    with ExitStack() as es:
        def L(name, shape, dt):
            return es.enter_context(nc.sbuf_tensor(name, list(shape), dt)).ap()
        wm = L("wm", [128, 8, D], BF16); ws_ = L("ws", [128, 8, D], BF16); wo = L("wo", [128, 8, D], BF16)
        loadc(wm, w_o_mla.rearrange("(k p) n -> p k n", p=128), "wm")
        loadc(ws_, w_o_ssd.rearrange("(k p) n -> p k n", p=128), "ws")
        loadc(wo, w_out.rearrange("(k p) n -> p k n", p=128), "wo")
        at_b = Rot([L("s7a%d" % i, [128, D], BF16) for i in range(2)], "s7a")
        yn_b = Rot([L("s7y%d" % i, [128, D], BF16) for i in range(2)], "s7y")
        aT_b = Rot([L("s7at%d" % i, [128, D], BF16) for i in range(2)], "s7at")
        yT_b = Rot([L("s7yt%d" % i, [128, D], BF16) for i in range(2)], "s7yt")
        g_b = Rot([L("s7g%d" % i, [128, 2 * D], F32) for i in range(2)], "s7g")
        x_b = Rot([L("s7x%d" % i, [128, D], F32) for i in range(2)], "s7x")
        m1_b = Rot([L("s7m%d" % i, [128, D], F32) for i in range(2)], "s7m")
        m2_b = Rot([L("s7n%d" % i, [128, D], F32) for i in range(2)], "s7n")
        mg_b = Rot([L("s7mg%d" % i, [128, D], BF16) for i in range(2)], "s7mg")
        mT_b = Rot([L("s7mt%d" % i, [128, D], BF16) for i in range(2)], "s7mt")
        hb_b = Rot([L("s7h%d" % i, [128, D], BF16) for i in range(2)], "s7h")
        hT_b = Rot([L("s7ht%d" % i, [128, D], BF16) for i in range(2)], "s7ht")
        st_b = Rot([L("s7s%d" % i, [128, 4], F32) for i in range(2)], "s7s")

        def mm2(lhsT, lk, W, wkey):
            res = []
            for nb in range(2):
                pb, pk = bank()
                for k in range(8):
                    S.op("pe", lambda e, k=k, nb=nb, pb=pb: e.matmul(
                        pb, lhsT[:, k * 128:(k + 1) * 128], W[:, k, nb * 512:(nb + 1) * 512], start=(k == 0), stop=(k == 7)),
                        reads=[lk, wkey], writes=[pk])
                res.append((pb, pk))
            return res

        def body(gt):
            si, _ = seq_of_tile(gt)
            yield
            m = seqs[si][2]
            yield
            r = gt * 128
            yield
            pr = prow(gt)
            yield
            yn, ynk = yn_b.next(); load(yn, YN[r:r + 128, :], ynk)
            yield
            gg, gk = g_b.next(); load(gg, PROJ[pr:pr + 128, C_GM:C_GM + 2048], gk)
            yield
            xt, xk = x_b.next(); load(xt, xrows(gt), xk)
            yield
            aT, aTk = aT_b.next(); load(aT.rearrange("p (h t) -> p h t", h=8), ATTT[:, :, r:r + 128].rearrange("h p t -> p h t"), aTk)
            yield
            yT, yTk = yT_b.next(); transpose_to(yT, yTk, yn, ynk, 8, "dve")
            yield
            S.op("act", lambda e, gg=gg: e.activation(gg, gg, AF.Sigmoid), reads=[gk], writes=[gk])
            yield
            om = mm2(aT, aTk, wm, "wm")
            yield
            m1, m1k = m1_b.next()
            yield
            for nb in range(2):
                tt("dve", m1[:, nb * 512:(nb + 1) * 512], om[nb][0], gg[:, nb * 512:(nb + 1) * 512], ALU.mult, [om[nb][1], gk], [m1k])
            yield
            os_ = mm2(yT, yTk, ws_, "ws")
            yield
            m2, m2k = m2_b.next()
            yield
            for nb in range(2):
                tt("dve", m2[:, nb * 512:(nb + 1) * 512], os_[nb][0], gg[:, D + nb * 512:D + (nb + 1) * 512], ALU.mult, [os_[nb][1], gk], [m2k])
            yield
            mg, mgk = mg_b.next()
            yield
            tt("pool", mg, m1, m2, ALU.add, [m1k, m2k], [mgk])
            yield
            mT, mTk = mT_b.next(); transpose_to(mT, mTk, mg, mgk, 8, "act")
            yield
            op_ = mm2(mT, mTk, wo, "wo")
            yield
            for nb in range(2):
                tt("dve", m1[:, nb * 512:(nb + 1) * 512], op_[nb][0], modp(m, G1)[:, nb * 512:(nb + 1) * 512], ALU.mult, [op_[nb][1], "mod%d" % m], [m1k])
            yield
            tt("pool", xt, xt, m1, ALU.add, [xk, m1k], [xk])
            yield
            S.dma("pool", XMID[r:r + 128, :], xt, xk, reads=[xk], writes=[("XMID", gt)])
            yield
            st, stk = st_b.next(); hb, hk = hb_b.next(); hT, hTk = hT_b.next()
            yield
            rr = rstd_of(st, stk, xt, xk, D, m2, m2k)
            yield
            stt("dve", m2, xt, rr, modp(m, GSC2), ALU.mult, ALU.mult, [xk, stk, "mod%d" % m], [m2k])
            yield
            tt("pool", hb, m2, modp(m, SH2), ALU.add, [m2k, "mod%d" % m], [hk])
            yield
            transpose_to(hT, hTk, hb, hk, 8, "act")
            yield
            store(H2T[gt], hT, hTk, [("H2T", gt)])
            yield
        pipeline(body, TWIN, PDEPTH)
        S.barrier()
        if STOP[0] == 9:
            S.finish(); return nc, S

    blocks = []
    for si, (r0s, Ls, lat) in enumerate(seqs):
        if lat:
            blocks.append((NPT, OWN, NPT + NTS - 1, NPT + OWN, (8, 9)))
        else:
            assert Ls // 128 <= 4
            blocks.append((r0s // 128, Ls // 128, None, None, None))
    NCOL = sum(b_[1] * 128 + 2 for b_ in blocks)
    ACTT = scr("ACTT", [22, 128, R], BF16)
    es_wd = ExitStack()
    wd = es_wd.enter_context(nc.sbuf_tensor("wd", [128, 22, D], BF16)).ap()
    with ExitStack() as es:
        def L(name, shape, dt):
            return es.enter_context(nc.sbuf_tensor(name, list(shape), dt)).ap()
        h2all = L("h2all", [128, 8, NCOL], BF16)
        memset("dve", h2all, 0.0, "h2all")
        bcols = []
        c = 0
        for (gt0, ntl, lsrc, rsrc, mcols) in blocks:
            bcols.append(c)
            for ti in range(ntl):
                load(h2all[:, :, c + 1 + ti * 128:c + 1 + (ti + 1) * 128], H2T[gt0 + ti].rearrange("p (k t) -> p k t", k=8), "h2all")
            if lsrc is not None:
                load(h2all[:, :, c:c + 1], H2T[lsrc].rearrange("p (k t) -> p k t", k=8)[:, :, 127:128], "h2all",
                     allow_slow_non_contiguous=True)
                S.op("dve", lambda e, c=c, mc=mcols[0]: e.tensor_scalar(h2all[:, :, c:c + 1], h2all[:, :, c:c + 1], mkt[:, mc:mc + 1], None, ALU.mult),
                     reads=["h2all", "mkt"], writes=["h2all"])
            if rsrc is not None:
                ce = c + ntl * 128 + 1
                load(h2all[:, :, ce:ce + 1], H2T[rsrc].rearrange("p (k t) -> p k t", k=8)[:, :, 0:1], "h2all",
                     allow_slow_non_contiguous=True)
                S.op("dve", lambda e, ce=ce, mc=mcols[1]: e.tensor_scalar(h2all[:, :, ce:ce + 1], h2all[:, :, ce:ce + 1], mkt[:, mc:mc + 1], None, ALU.mult),
                     reads=["h2all", "mkt"], writes=["h2all"])
            c += ntl * 128 + 2
        fwT = L("fwT", [128, 2, 22, 3], F32)
        fbT = L("fbT", [128, 2, 22], F32)
        for t_ in range(2):
            for k in range(3):
                load(fwT[:, t_, :, k], fw[k, t_ * DFF:(t_ + 1) * DFF].rearrange("(j p) -> p j", p=128), "fwT", allow_slow_non_contiguous=True)
            load(fbT[:, t_, :], fb[t_ * DFF:(t_ + 1) * DFF].rearrange("(j p) -> p j", p=128), "fbT", allow_slow_non_contiguous=True)
        wg_b = Rot([L("s8w%d" % i, [128, 2, 8, 128], BF16) for i in range(3)], "s8w")
        ue_b = [Rot([L("s8u%d_%d" % (t_, i), [128, 514], F32) for i in range(2)], "s8u%d_" % t_) for t_ in range(2)]
        tc_b = [Rot([L("s8t%d_%d" % (t_, i), [128, 512], F32) for i in range(2)], "s8t%d_" % t_) for t_ in range(2)]
        sg_b = Rot([L("s8s%d" % i, [128, 512], F32) for i in range(2)], "s8s")
        ao_b = Rot([L("s8a%d" % i, [128, 512], BF16) for i in range(3)], "s8a")
        def s8_wload(j):
            wt, wk = wg_b.next()
            loadc(wt[:, 0], w_up[:, j * 128:(j + 1) * 128].rearrange("(k p) n -> p k n", p=128), wk)
            loadc(wt[:, 1], w_up[:, DFF + j * 128:DFF + (j + 1) * 128].rearrange("(k p) n -> p k n", p=128), wk)
            return wt, wk
        s8_next = s8_wload(0)
        for j in range(22):
            wt, wk = s8_next
            if j + 1 < 22:
                s8_next = s8_wload(j + 1)
            loadc(wd[:, j, :], w_down[j * 128:(j + 1) * 128, :], "wd")
            for bi, (gt0, ntl, lsrc_, rsrc_, mcols_) in enumerate(blocks):
                n = ntl * 128
                c = bcols[bi]
                tcs = []
                for t_ in range(2):
                    pa, pak = bank()
                    ph, phk = bank()
                    for k in range(8):
                        S.op("pe", lambda e, k=k, t_=t_, pa=pa, wt=wt, c=c, n=n: e.matmul(
                            pa[:, 0:n], wt[:, t_, k, :], h2all[:, k, c + 1:c + 1 + n], start=(k == 0), stop=(k == 7)),
                            reads=[wk, "h2all"], writes=[pak])
                    for k in range(8):
                        S.op("pe", lambda e, k=k, t_=t_, ph=ph, wt=wt, c=c, n=n: e.matmul(
                            ph[:, 0:2], wt[:, t_, k, :], h2all[:, k, c:c + n + 2:n + 1], start=(k == 0), stop=(k == 7)),
                            reads=[wk, "h2all"], writes=[phk])
                    ue, uek = ue_b[t_].next()
                    copy("act", ue[:, 1:n + 1], pa[:, 0:n], [pak], [uek])
                    copy("dve", ue[:, 0:n + 2:n + 1], ph[:, 0:2], [phk], [uek])
                    tcv, tck = tc_b[t_].next()
                    S.op("act", lambda e, tcv=tcv, pa=pa, t_=t_, j=j, n=n: e.activation(
                        tcv[:, 0:n], pa[:, 0:n], AF.Copy, scale=fwT[:, t_, j, 1:2]), reads=[pak, "fwT"], writes=[tck])
                    stt("dve", tcv[:, 0:n], ue[:, 0:n], fwT[:, t_, j, 0:1], tcv[:, 0:n], ALU.mult, ALU.add, [uek, "fwT", tck], [tck])
                    stt("dve", tcv[:, 0:n], ue[:, 2:n + 2], fwT[:, t_, j, 2:3], tcv[:, 0:n], ALU.mult, ALU.add, [uek, "fwT", tck], [tck])
                    tcs.append((tcv, tck))
                sg, sgk = sg_b.next()
                S.op("act", lambda e, sg=sg, tcv=tcs[0][0], j=j, n=n: e.activation(sg[:, 0:n], tcv[:, 0:n], AF.Silu, bias=fbT[:, 0, j:j + 1]),
                     reads=[tcs[0][1], "fbT"], writes=[sgk])
                ao, aok = ao_b.next()
                stt("dve", ao[:, 0:n], tcs[1][0][:, 0:n], fbT[:, 1, j:j + 1], sg[:, 0:n], ALU.add, ALU.mult, [tcs[1][1], "fbT", sgk], [aok])
                store(ACTT[j][:, gt0 * 128:gt0 * 128 + n], ao[:, 0:n], aok, [("ACTT", j, bi)])
        S.barrier()
        if STOP[0] == 10:
            S.finish(); return nc, S

    with ExitStack() as es:
        def L(name, shape, dt):
            return es.enter_context(nc.sbuf_tensor(name, list(shape), dt)).ap()
        fngb = L("fngb", [128, D], F32); load(fngb, fng.partition_broadcast(128), "fngb")
        aT_b = Rot([L("s10at%d" % i, [128, 22, 128], BF16) for i in range(3)], "s10at")
        x_b = Rot([L("s10x%d" % i, [128, D], F32) for i in range(3)], "s10x")
        o_b = Rot([L("s10o%d" % i, [128, D], F32) for i in range(3)], "s10o")
        st_b = Rot([L("s10s%d" % i, [128, 4], F32) for i in range(3)], "s10s")
        def body(gt):
            si, _ = seq_of_tile(gt)
            yield
            m = seqs[si][2]
            yield
            r = gt * 128
            yield
            aT, aTk = aT_b.next(); load(aT, ACTT[:, :, r:r + 128].rearrange("j p t -> p j t"), aTk)
            yield
            xt, xk = x_b.next(); load(xt, XMID[r:r + 128, :], xk)
            yield
            ot, ok = o_b.next()
            yield
            for nb in range(2):
                pb, pk = bank()
                for k in range(22):
                    S.op("pe", lambda e, k=k, nb=nb, pb=pb, aT=aT: e.matmul(
                        pb, aT[:, k, :], wd[:, k, nb * 512:(nb + 1) * 512], start=(k == 0), stop=(k == 21)),
                        reads=[aTk, "wd"], writes=[pk])
                tt("dve", ot[:, nb * 512:(nb + 1) * 512], pb, modp(m, G2)[:, nb * 512:(nb + 1) * 512], ALU.mult, [pk, "mod%d" % m], [ok])
            yield
            tt("pool", xt, xt, ot, ALU.add, [xk, ok], [xk])
            yield
            st, stk = st_b.next()
            yield
            rr = rstd_of(st, stk, xt, xk, D, ot, ok)
            yield
            stt("dve", ot, xt, rr, fngb, ALU.mult, ALU.mult, [xk, stk, "fngb"], [ok])
            yield
            S.dma("sp", yrows(gt), ot, ok, reads=[ok], writes=[("Y", gt)])
            yield
        pipeline(body, TOWN, 3)
        S.barrier()
    es_wd.close()
    S.finish()
    return nc, S


def rope_tables(LS):
    t = np.arange(LS)
    row = (t // 64).astype(np.float32)
    col = (t % 64).astype(np.float32)
    n = 16
    inv = (10000.0 ** (-np.arange(n, dtype=np.float32) / n)).astype(np.float32)
    ang = np.stack([row[:, None] * inv, col[:, None] * inv], axis=1).reshape(LS, 32).astype(np.float32)
    return np.cos(ang).astype(np.float32), np.sin(ang).astype(np.float32)


_CACHE = {}


def run(inputs, NP, LP, LS, PAST, n_cores=8):
    key = (NP, LP, LS, PAST)
    if key not in _CACHE:
        _CACHE[key] = build(NP, LP, LS, PAST)[0]
    nc = _CACHE[key]
    f = lambda a: np.ascontiguousarray(np.asarray(a, dtype=np.float32))
    cos_t, sin_t = rope_tables(LS)
    shared = {
        "w_ada": f(inputs["w_ada"][0]), "b_ada": f(inputs["b_ada"][0]), "ga": f(inputs["norm_attn_g"][0]),
        "w_in": f(inputs["w_in"][0]), "qg": f(inputs["q_norm_g"][0]), "kvg": f(inputs["kv_norm_g"][0]),
        "w_uq": f(inputs["w_uq"][0]), "w_ukv": f(inputs["w_ukv"][0]), "w_o_mla": f(inputs["w_o_mla"][0]),
        "cw": f(inputs["ssd_conv_w"][0]), "cb": f(inputs["ssd_conv_b"][0]),
        "dtb": f(inputs["ssd_dt_bias"][0]).reshape(32), "alog": f(inputs["ssd_A_log"][0]).reshape(32),
        "Dv": f(inputs["ssd_D"][0]), "gn": f(inputs["ssd_norm_g"][0]), "w_o_ssd": f(inputs["w_o_ssd"][0]),
        "w_out": f(inputs["w_out"][0]), "gf": f(inputs["norm_ffn_g"][0]), "w_up": f(inputs["w_up"][0]),
        "fw": f(inputs["ffn_conv_w"][0]), "fb": f(inputs["ffn_conv_b"][0]), "w_down": f(inputs["w_down"][0]),
        "fng": f(inputs["final_norm_g"]), "cos_t": cos_t, "sin_t": sin_t,
    }
    xpr = f(inputs["x_prompt"]); xsm = f(inputs["x_sample"]); c = f(inputs["c"]); cctx = f(inputs["c_ctx"])
    in_maps = []
    G4 = 4
    NTS = LS // 128
    OWN = NTS // G4
    rots = []
    for core in range(n_cores):
        sq = (core // G4) % xsm.shape[0]
        rr = core % G4
        rot = OWN * rr
        rots.append((sq, rot))
        d = dict(shared)
        d["xp"] = np.ascontiguousarray(xpr[core * NP:(core + 1) * NP].reshape(NP * LP, D))
        d["xs"] = np.ascontiguousarray(np.roll(xsm[sq], -rot * 128, axis=0))
        d["cos_t"] = np.ascontiguousarray(np.roll(cos_t, -rot * 128, axis=0))
        d["sin_t"] = np.ascontiguousarray(np.roll(sin_t, -rot * 128, axis=0))
        mkv = np.ones((128, 16), np.float32)
        for b_ in range(G4):
            m_ = 0.0 if (b_ + rr) % G4 == 0 else 1.0
            mkv[0, b_] = m_
            mkv[127, 4 + b_] = m_
            mkv[:, 8 + b_] = m_
            mkv[:, 12 + b_] = 1.0 - m_
        d["mk"] = mkv
        d["cvec"] = np.ascontiguousarray(np.stack([cctx, c[sq]], 0))
        d["cckv"] = f(inputs["cache_ckv"][sq, 0]); d["ckr"] = f(inputs["cache_krope"][sq, 0])
        d["st"] = f(inputs["state_ssd"][sq, 0])
        in_maps.append(d)
    res = run_bass_kernel_spmd(nc, in_maps, core_ids=list(range(n_cores)))
    outs = res.results
    if DEBUG[0]:
        DBG_OUT.append(outs)
    B = xpr.shape[0]
    y_prompt = np.concatenate([outs[i]["yp"].reshape(NP, LP, D) for i in range(n_cores)], 0)[:B]
    y_sample = np.zeros(xsm.shape, np.float32)
    for core in range(n_cores):
        sq, rot = rots[core]
        y_sample[sq, rot * 128:(rot + OWN) * 128] = outs[core]["ys"]
    new_ckv = np.concatenate([outs[i]["nckv"].reshape(NP, 1, LP, 256) for i in range(n_cores)], 0)[:B]
    new_kr = np.concatenate([outs[i]["nkr"].reshape(NP, 1, LP, 64) for i in range(n_cores)], 0)[:B]
    new_ssd = np.concatenate([outs[i]["nssd"].reshape(NP, 1, 2, 16, 64, 64) for i in range(n_cores)], 0)[:B]
    return (y_prompt.astype(np.float32), y_sample.astype(np.float32), new_ckv.astype(np.float32),
            new_kr.astype(np.float32), new_ssd.astype(np.float32))


def kernel(**inputs):
    return run(inputs, 4, 256, 2048, 512, 8)
```

```python
import math
from contextlib import ExitStack, nullcontext
import numpy as np
import concourse.bass as bass
import concourse.mybir as mybir
from concourse.bass_utils import run_bass_kernel_spmd

F32 = mybir.dt.float32
BF16 = mybir.dt.bfloat16
AF = mybir.ActivationFunctionType
ALU = mybir.AluOpType

D = 1024
IN_COLS = 5216
C_CQ, C_CKV, C_KR, C_Z, C_X, C_DT, C_GM = 0, 256, 512, 576, 1600, 3136, 3168
DFF = 2816
EPS = 1e-6
NEG = -30000.0


class Sched:
    def __init__(self, nc):
        self.nc = nc
        self.e = {"pe": nc.tensor, "act": nc.scalar, "dve": nc.vector,
                  "pool": nc.gpsimd, "sp": nc.sync}
        self.sems = {}
        self.cnt = {}
        for k in ("pe", "act", "dve", "pool"):
            self.sems[k] = nc.alloc_semaphore("c_" + k)
            self.cnt[k] = 0
        self.seen = {k: {} for k in self.e}
        self.lastw = {}
        self.readers = {}
        self.nins = 0
        self.phys = {}
        self.physq = {}
        self.free = {}
        self.nphys = 0

    def _sem(self, sk, q):
        if sk not in self.phys:
            fl = self.free.setdefault(q, [])
            if fl:
                pid = fl.pop()
            else:
                pid = "d_%d" % self.nphys
                self.nphys += 1
                self.sems[pid] = self.nc.alloc_semaphore(pid)
                self.cnt[pid] = 0
            self.phys[sk] = pid
            self.physq[pid] = q
        return self.phys[sk]

    def _deps(self, reads, writes):
        best = {}

        def add(sk, v):
            if best.get(sk, 0) < v:
                best[sk] = v
        for k in reads:
            if k in self.lastw:
                add(*self.lastw[k])
        for k in writes:
            if k in self.lastw:
                add(*self.lastw[k])
            for sk, v in self.readers.get(k, {}).items():
                add(sk, v)
        return best

    def _wait(self, eng, best):
        for sk, v in best.items():
            if sk == eng and eng == "pe":
                continue
            if self.seen[eng].get(sk, 0) >= v:
                continue
            self.seen[eng][sk] = v
            self.e[eng].wait_ge(self.sems[sk], v)
            self.nins += 1

    def _record(self, tok, reads, writes):
        for k in writes:
            self.lastw[k] = tok
            self.readers[k] = {}
        for k in reads:
            r = self.readers.setdefault(k, {})
            if r.get(tok[0], 0) < tok[1]:
                r[tok[0]] = tok[1]

    def op(self, eng, fn, reads=(), writes=()):
        self._wait(eng, self._deps(reads, writes))
        ins = fn(self.e[eng])
        self.cnt[eng] += 1
        ins.then_inc(self.sems[eng], 1)
        self.nins += 1
        self._record((eng, self.cnt[eng]), reads, writes)

    def dma(self, q, out, in_, key, reads=(), writes=(), **kw):
        self._wait(q, self._deps(reads, writes))
        pid = self._sem("dma:%s:%s" % (q, key), q)
        ins = self.e[q].dma_start(out=out, in_=in_, **kw)
        self.cnt[pid] += 16
        ins.then_inc(self.sems[pid], 16)
        self.nins += 1
        self._record((pid, self.cnt[pid]), reads, writes)

    def _all(self):
        best = {}
        for sk, v in self.lastw.values():
            if best.get(sk, 0) < v:
                best[sk] = v
        for r in self.readers.values():
            for sk, v in r.items():
                if best.get(sk, 0) < v:
                    best[sk] = v
        return best

    def barrier(self):
        best = self._all()
        for eng in self.e:
            self._wait(eng, dict(best))
        self.lastw = {}
        self.readers = {}
        for pid in self.phys.values():
            self.free.setdefault(self.physq[pid], []).append(pid)
        self.phys = {}

    def finish(self):
        self._wait("sp", self._all())


PIPE = {"slot": None, "depth": 1}


class Rot:
    def __init__(self, aps, name):
        self.aps = aps
        self.name = name
        self.i = 0
        self.si = {}

    def next(self):
        sl = PIPE["slot"]
        d = PIPE["depth"]
        if sl is None or len(self.aps) < d:
            self.i = (self.i + 1) % len(self.aps)
            j = self.i
        else:
            idx = [i for i in range(len(self.aps)) if i % d == sl]
            c = (self.si.get(sl, 0) + 1) % len(idx)
            self.si[sl] = c
            j = idx[c]
        return self.aps[j], "%s%d" % (self.name, j)


def pipeline(make_body, items, depth):
    PIPE["depth"] = depth
    active = {}
    it = iter(items)
    done = False
    while True:
        for sl in range(depth):
            if sl not in active and not done:
                x = next(it, None)
                if x is None:
                    done = True
                else:
                    active[sl] = make_body(x)
        if not active:
            break
        for sl in sorted(active):
            PIPE["slot"] = sl
            try:
                next(active[sl])
            except StopIteration:
                del active[sl]
    PIPE["slot"] = None
    PIPE["depth"] = 1


STOP = [0]
PDEPTH = 2
SUB = [0]
DEBUG = [0]
DBG_OUT = []


def build(NP, LP, LS, PAST):
    nc = bass.Bass("TRN2", target_bir_lowering=False)
    S = Sched(nc)
    RPRM = NP * LP
    R = RPRM + LS
    NT = R // 128
    NPT = RPRM // 128
    NTS = LS // 128
    G4 = 4
    OWN = NTS // G4
    assert OWN * G4 == NTS and 1 <= OWN <= 4
    WIN = [NPT + NTS - 1] + [NPT + i for i in range(OWN + 1)]
    TWIN = list(range(NPT)) + WIN
    TWINS = set(TWIN)
    TOWN = list(range(NPT)) + [NPT + i for i in range(OWN)]
    seqs = [(i * LP, LP, 0) for i in range(NP)] + [(RPRM, LS, 1)]
    NSEQ = len(seqs)
    RP = R + 2 * NSEQ
    KRT = R + PAST

    def seq_of_tile(gt):
        r = gt * 128
        for si, (r0, L, m) in enumerate(seqs):
            if r0 <= r < r0 + L:
                return si, (r - r0) // 128
        raise ValueError

    def prow(gt):
        si, _ = seq_of_tile(gt)
        return gt * 128 + 2 * si + 1

    def krow(gt):
        si, _ = seq_of_tile(gt)
        return gt * 128 + (PAST if seqs[si][2] else 0)

    def din(name, shape):
        return nc.dram_tensor(name, list(shape), F32, kind="ExternalInput").ap()

    def dout(name, shape):
        return nc.dram_tensor(name, list(shape), F32, kind="ExternalOutput").ap()

    xp = din("xp", [RPRM, D]); xs = din("xs", [LS, D]); cvec = din("cvec", [2, D])
    cckv = din("cckv", [PAST, 256]); ckr = din("ckr", [PAST, 64]); st_in = din("st", [2, 16, 64, 64])
    w_ada = din("w_ada", [D, 6 * D]); b_ada = din("b_ada", [6 * D]); ga = din("ga", [D])
    w_in = din("w_in", [D, IN_COLS]); qg = din("qg", [256]); kvg = din("kvg", [256])
    w_uq = din("w_uq", [256, 1536]); w_ukv = din("w_ukv", [256, 2048]); w_o_mla = din("w_o_mla", [D, D])
    cw = din("cw", [3, 1536]); cb = din("cb", [1536]); dtb = din("dtb", [32]); alog = din("alog", [32])
    Dv = din("Dv", [16]); gn = din("gn", [D]); w_o_ssd = din("w_o_ssd", [D, D]); w_out = din("w_out", [D, D])
    gf = din("gf", [D]); w_up = din("w_up", [D, 2 * DFF]); fw = din("fw", [3, 2 * DFF]); fb = din("fb", [2 * DFF])
    w_down = din("w_down", [DFF, D]); fng = din("fng", [D])
    cos_t = din("cos_t", [LS, 32]); sin_t = din("sin_t", [LS, 32]); mk = din("mk", [128, 16])
    yp = dout("yp", [RPRM, D]); ys = dout("ys", [OWN * 128, D]); nckv = dout("nckv", [RPRM, 256])
    nkr = dout("nkr", [RPRM, 64]); nssd = dout("nssd", [NP, 2, 16, 64, 64])

    def xrows(gt):
        r = gt * 128
        return xp[r:r + 128, :] if r < RPRM else xs[r - RPRM:r - RPRM + 128, :]

    def yrows(gt):
        r = gt * 128
        return yp[r:r + 128, :] if r < RPRM else ys[r - RPRM:r - RPRM + 128, :]

    def scr(name, shape, dt):
        if DEBUG[0]:
            return nc.dram_tensor(name, list(shape), dt, kind="ExternalOutput").ap()
        return nc.dram_tensor(name, list(shape), dt).ap()
    HT = scr("HT", [NT, 128, D], BF16)
    PROJ = scr("PROJ", [RP, IN_COLS], F32)
    QS = scr("QS", [R, 1536], BF16)
    KVS = scr("KVS", [KRT, 2048], BF16)
    KRS = scr("KRS", [KRT, 64], BF16)
    XBC = scr("XBC", [R, 1536], BF16)
    STT = scr("STT", [R, 160], F32)
    GMT = scr("GMT", [NT, 32, 128], F32)
    HIN = scr("HIN", [NT, 2, 64, D], BF16)
    YN = scr("YN", [R, D], BF16)
    XMID = scr("XMID", [R, D], F32)
    H2T = scr("H2T", [NT, 128, D], BF16)

    def G(name, shape, dt):
        return nc.alloc_sbuf_tensor(name, list(shape), dt).ap()
    PB = [nc.alloc_psum_tensor("pb%d" % i, [128, 512], F32).ap() for i in range(8)]
    pst = {"i": 0, "n": 8, "a": 0}

    def bank():
        sl = PIPE["slot"]
        if sl is not None and PIPE["depth"] >= 2:
            nb_ = 8 // PIPE["depth"]
            k = "s%d" % sl
            pst[k] = (pst.get(k, 0) + 1) % nb_
            j = sl * nb_ + pst[k]
            return PB[j], "pb%d" % j
        pst["i"] = (pst["i"] + 1) % pst["n"]
        return PB[pst["i"]], "pb%d" % pst["i"]

    def bank_acc():
        pst["a"] = 1 - pst["a"]
        return PB[6 + pst["a"]], "pb%d" % (6 + pst["a"])

    ident = G("ident", [128, 128], BF16)
    ident32 = G("ident32", [128, 128], F32)
    ones32 = G("ones32", [128, 128], F32)
    tincl = G("tincl", [128, 128], F32)
    texcl = G("texcl", [128, 128], F32)
    epsT = G("epsT", [128, 1], F32)
    oneT = G("oneT", [128, 1], F32)
    zeroT = G("zeroT", [128, 1408], BF16)
    zero32 = G("zero32", [128, 1536], F32)
    MOD = [G("mod%d" % m, [128, 6 * D], F32) for m in range(2)]
    SH1, GSC1, G1, SH2, GSC2, G2 = range(6)

    def modp(m, part):
        return MOD[m][:, part * D:(part + 1) * D]

    def memset(eng, ap, val, key):
        S.op(eng, lambda e: e.memset(ap, val), writes=[key])

    def asel(ap, key, pattern, cmp, fill, base, cm):
        S.op("pool", lambda e: e.affine_select(ap, ap, pattern, cmp, fill, base=base, channel_multiplier=cm),
             reads=[key], writes=[key])

    memset("pool", ident, 1.0, "ident"); asel(ident, "ident", [[-1, 128]], ALU.is_equal, 0.0, 0, 1)
    memset("pool", ident32, 1.0, "ident32"); asel(ident32, "ident32", [[-1, 128]], ALU.is_equal, 0.0, 0, 1)
    memset("dve", ones32, 1.0, "ones32")
    memset("pool", tincl, 1.0, "tincl"); asel(tincl, "tincl", [[1, 128]], ALU.is_ge, 0.0, 0, -1)
    memset("pool", texcl, 1.0, "texcl"); asel(texcl, "texcl", [[1, 128]], ALU.is_gt, 0.0, 0, -1)
    memset("dve", epsT, EPS, "epsT"); memset("dve", oneT, 1.0, "oneT")
    mkt = G("mkt", [128, 16], F32)
    S.dma("sp", mkt, mk, "mkt", writes=["mkt"])
    memset("dve", zeroT, 0.0, "zeroT"); memset("dve", zero32, 0.0, "zero32")

    def copy(eng, out, in_, reads, writes):
        if eng == "act":
            S.op("act", lambda e: e.copy(out, in_), reads=reads, writes=writes)
        else:
            S.op(eng, lambda e: e.tensor_copy(out, in_), reads=reads, writes=writes)

    def tt(eng, out, a, b, op, reads, writes):
        S.op(eng, lambda e: e.tensor_tensor(out, a, b, op), reads=reads, writes=writes)

    def transpose_to(dst, dkey, src, skey, n, ceng="act", w=128):
        done = 0
        while done < n:
            m = min(8, n - done)
            pb, pk = bank()
            pbb = pb.bitcast(BF16)
            for k in range(m):
                S.op("pe", lambda e, k=k, done=done, pbb=pbb: e.transpose(
                    pbb[0:w, k * 128:(k + 1) * 128], src[:, (done + k) * w:(done + k + 1) * w], ident),
                    reads=[skey, "ident"], writes=[pk])
            copy(ceng, dst[0:w, done * 128:(done + m) * 128], pbb[0:w, 0:m * 128], [pk], [dkey])
            done += m

    def load(dst, src, key, reads=(), **kw):
        S.dma("sp", dst, src, key, reads=reads, writes=[key], **kw)

    def loadc(dst, src, key, reads=()):
        S.dma("pool", dst, src, key, reads=reads, writes=[key])

    def store(dst, src, key, writes):
        S.dma("pool", dst, src, key, reads=[key], writes=writes)

    def rstd_of(st, stk, src, skey, n, junk, jkey):
        S.op("act", lambda e: e.activation(junk, src, AF.Square, accum_out=st[:, 0:1]),
             reads=[skey], writes=[jkey, stk])
        S.op("act", lambda e: e.activation(st[:, 1:2], st[:, 0:1], AF.Ln, bias=epsT, scale=1.0 / n),
             reads=[stk, "epsT"], writes=[stk])
        S.op("act", lambda e: e.activation(st[:, 2:3], st[:, 1:2], AF.Exp, scale=-0.5),
             reads=[stk], writes=[stk])
        return st[:, 2:3]

    def stt(eng, out, a, sc, b, op0, op1, reads, writes):
        S.op(eng, lambda e: e.scalar_tensor_tensor(out, a, sc, b, op0, op1), reads=reads, writes=writes)

    es_keep = ExitStack()
    with nullcontext(es_keep) as es:
        def L(name, shape, dt):
            return es.enter_context(nc.sbuf_tensor(name, list(shape), dt)).ap()
        cT = L("cT", [128, 2, 8], F32)
        cS = L("cS", [128, 2, 8], F32)
        cB = L("cB", [128, 16, 128], BF16)
        wb = Rot([L("s0w%d" % i, [128, 8, 512], BF16) for i in range(2)], "s0w")
        bb = Rot([L("s0b%d" % i, [128, 512], F32) for i in range(2)], "s0b")
        gab = L("gab", [128, D], F32)
        gfb = L("gfb", [128, D], F32)
        load(cT, cvec.rearrange("m (k p) -> p m k", p=128), "cT", allow_slow_non_contiguous=True)
        load(gab, ga.partition_broadcast(128), "gab")
        load(gfb, gf.partition_broadcast(128), "gfb")
        S.op("act", lambda e: e.activation(cS, cT, AF.Silu), reads=["cT"], writes=["cS"])
        for m in range(2):
            for k in range(8):
                copy("dve", cB[:, m * 8 + k, :], cS[:, m, k:k + 1].to_broadcast([128, 128]), ["cS"], ["cB"])
        for j in range(12):
            wt, wk = wb.next()
            loadc(wt, w_ada[:, j * 512:(j + 1) * 512].rearrange("(k p) n -> p k n", p=128), wk)
            bt, bk = bb.next()
            load(bt, b_ada[j * 512:(j + 1) * 512].partition_broadcast(128), bk)
            for m in range(2):
                pb, pk = bank()
                for k in range(8):
                    S.op("pe", lambda e, k=k, m=m, pb=pb, wt=wt: e.matmul(pb, cB[:, m * 8 + k, :], wt[:, k, :], start=(k == 0), stop=(k == 7)),
                         reads=["cB", wk], writes=[pk])
                tt("dve", MOD[m][:, j * 512:(j + 1) * 512], pb, bt, ALU.add, [pk, bk], ["mod%d" % m])
        for m in range(2):
            stt("dve", modp(m, GSC1), modp(m, GSC1), 1.0, gab, ALU.add, ALU.mult, ["mod%d" % m, "gab"], ["mod%d" % m])
            stt("dve", modp(m, GSC2), modp(m, GSC2), 1.0, gfb, ALU.add, ALU.mult, ["mod%d" % m, "gfb"], ["mod%d" % m])
        if STOP[0] == 1:
            S.finish(); return nc, S

    with nullcontext(es_keep) as es:
        def L(name, shape, dt):
            return es.enter_context(nc.sbuf_tensor(name, list(shape), dt)).ap()
        xb = Rot([L("s1x%d" % i, [128, D], F32) for i in range(2)], "s1x")
        jb = Rot([L("s1j%d" % i, [128, D], F32) for i in range(2)], "s1j")
        hbb = Rot([L("s1h%d" % i, [128, D], BF16) for i in range(2)], "s1h")
        hTb = Rot([L("s1t%d" % i, [128, D], BF16) for i in range(2)], "s1t")
        stb = Rot([L("s1s%d" % i, [128, 4], F32) for i in range(2)], "s1s")
        def body(gt):
            si, _ = seq_of_tile(gt)
            yield
            m = seqs[si][2]
            yield
            xt, xk = xb.next(); load(xt, xrows(gt), xk)
            yield
            st, stk = stb.next(); junk, jk = jb.next(); hb, hk = hbb.next(); hT, hTk = hTb.next()
            yield
            r = rstd_of(st, stk, xt, xk, D, junk, jk)
            yield
            stt("dve", junk, xt, r, modp(m, GSC1), ALU.mult, ALU.mult, [xk, stk, "mod%d" % m], [jk])
            yield
            tt("pool", hb, junk, modp(m, SH1), ALU.add, [jk, "mod%d" % m], [hk])
            yield
            transpose_to(hT, hTk, hb, hk, 8, "act")
            yield
            store(HT[gt], hT, hTk, [("HT", gt)])
            yield
        pipeline(body, range(NT), PDEPTH)
        if STOP[0] == 2:
            S.finish(); return nc, S

    def proj_stage(SRC, W, groups, DST, dst_dt, tag):
        with ExitStack() as es:
            def L(name, shape, dt):
                return es.enter_context(nc.sbuf_tensor(name, list(shape), dt)).ap()
            wb = Rot([L(tag + "w%d" % i, [128, 8, 512], BF16) for i in range(3)], tag + "w")
            hall = L(tag + "hall", [128, NT, D], BF16)
            for gt in range(NT):
                load(hall[:, gt, :], SRC[gt], tag + "hall%d" % gt, reads=[("HT", gt)])
            ob = Rot([L(tag + "o%d" % i, [128, 512], dst_dt) for i in range(3)], tag + "o")
            blks = []
            for (g0, g1, tl) in groups:
                c0 = g0
                while c0 < g1:
                    cwid = min(512, g1 - c0)
                    blks.append((c0, cwid, tl))
                    c0 += cwid

            def pj_wload(bi):
                c0, cwid, tl = blks[bi]
                wt, wk = wb.next()
                loadc(wt[:, :, 0:cwid], W[:, c0:c0 + cwid].rearrange("(k p) n -> p k n", p=128), wk)
                return wt, wk
            pj_next = pj_wload(0)
            for bi, (c0, cwid, tl) in enumerate(blks):
                wt, wk = pj_next
                if bi + 1 < len(blks):
                    pj_next = pj_wload(bi + 1)
                for gt in tl:
                    ht, hk = hall[:, gt, :], tag + "hall%d" % gt
                    pb, pk = bank()
                    for k in range(8):
                        S.op("pe", lambda e, k=k, pb=pb, ht=ht, wt=wt, cwid=cwid: e.matmul(
                            pb[:, 0:cwid], ht[:, k * 128:(k + 1) * 128], wt[:, k, 0:cwid], start=(k == 0), stop=(k == 7)),
                            reads=[hk, wk], writes=[pk])
                    ot, ok = ob.next()
                    copy("act" if (gt + bi) % 2 else "dve", ot[:, 0:cwid], pb[:, 0:cwid], [pk], [ok])
                    pr = prow(gt)
                    store(DST[pr:pr + 128, c0:c0 + cwid], ot[:, 0:cwid], ok, [(tag, gt, bi)])
            S.barrier()
            if STOP[0] == 3:
                return True

    for si, (r0, Ls, m) in enumerate(seqs):
        for pr in (r0 + 2 * si, r0 + 2 * si + Ls + 1):
            S.dma("sp", PROJ[pr:pr + 1, C_X:C_X + 1536], zero32[0:1, :], "zero32", reads=["zero32"], writes=[("pad", pr)])
    ALLT = list(range(NT))
    s2_groups = [(C_CKV, C_Z, ALLT), (C_X, C_GM, ALLT), (C_CQ, C_CKV, TWIN), (C_Z, C_X, TWIN), (C_GM, IN_COLS, TWIN)]
    if proj_stage(HT, w_in, s2_groups, PROJ, F32, "s2"):
        S.finish(); return nc, S
    es_keep.close()

    with ExitStack() as es:
        def L(name, shape, dt):
            return es.enter_context(nc.sbuf_tensor(name, list(shape), dt)).ap()
        wuq = L("wuq", [128, 2, 1536], BF16); wukv = L("wukv", [128, 2, 2048], BF16)
        qgb = L("qgb", [128, 256], F32); kvgb = L("kvgb", [128, 256], F32)
        loadc(wuq, w_uq.rearrange("(k p) n -> p k n", p=128), "wuq")
        loadc(wukv, w_ukv.rearrange("(k p) n -> p k n", p=128), "wukv")
        load(qgb, qg.partition_broadcast(128), "qgb"); load(kvgb, kvg.partition_broadcast(128), "kvgb")
        inb = Rot([L("s3i%d" % i, [128, 576], F32) for i in range(3)], "s3i")
        stb = Rot([L("s3s%d" % i, [128, 8], F32) for i in range(3)], "s3s")
        jb = Rot([L("s3j%d" % i, [128, 256], F32) for i in range(3)], "s3j")
        cqb = Rot([L("s3c%d" % i, [128, 256], BF16) for i in range(3)], "s3c")
        cqT = Rot([L("s3ct%d" % i, [128, 256], BF16) for i in range(3)], "s3ct")
        qfb = Rot([L("s3q%d" % i, [128, 1536], F32) for i in range(3)], "s3q")
        qbb = Rot([L("s3qb%d" % i, [128, 1536], BF16) for i in range(3)], "s3qb")
        ckb = Rot([L("s3k%d" % i, [128, 256], F32) for i in range(3)], "s3k")
        ckbb = Rot([L("s3kb%d" % i, [128, 256], BF16) for i in range(3)], "s3kb")
        ckT = Rot([L("s3kt%d" % i, [128, 256], BF16) for i in range(3)], "s3kt")
        kvb = Rot([L("s3v%d" % i, [128, 2048], BF16) for i in range(3)], "s3v")
        krf = Rot([L("s3r%d" % i, [128, 64], F32) for i in range(3)], "s3r")
        krb = Rot([L("s3rb%d" % i, [128, 64], BF16) for i in range(3)], "s3rb")
        csb = Rot([L("s3cs%d" % i, [128, 64], F32) for i in range(3)], "s3cs")
        tmpb = Rot([L("s3t%d" % i, [128, 4, 256], F32) for i in range(3)], "s3t")

        def kv_from(ckn32, ck32k, key_row, kr_bf, kr_k):
            cbf, cbk = ckbb.next()
            copy("dve", cbf, ckn32, [ck32k], [cbk])
            ct, ctk = ckT.next()
            transpose_to(ct, ctk, cbf, cbk, 2, "act")
            kv, kvk = kvb.next()
            for nb in range(4):
                pb, pk = bank()
                for k in range(2):
                    S.op("pe", lambda e, k=k, nb=nb, pb=pb, ct=ct: e.matmul(
                        pb, ct[:, k * 128:(k + 1) * 128], wukv[:, k, nb * 512:(nb + 1) * 512], start=(k == 0), stop=(k == 1)),
                        reads=[ctk, "wukv"], writes=[pk])
                copy("act" if nb % 2 else "dve", kv[:, nb * 512:(nb + 1) * 512], pb, [pk], [kvk])
            store(KVS[key_row:key_row + 128, :], kv, kvk, [("KVS", key_row)])
            store(KRS[key_row:key_row + 128, :], kr_bf, kr_k, [("KRS", key_row)])

        def body(gt):
            si, ti = seq_of_tile(gt)
            yield
            r0s, Ls, lat = seqs[si]
            yield
            r = gt * 128
            yield
            pr = prow(gt)
            yield
            it, ik = inb.next()
            if gt in TWINS:
                load(it, PROJ[pr:pr + 128, 0:576], ik)
            else:
                load(it[:, 256:576], PROJ[pr:pr + 128, 256:576], ik)
            yield
            st, stk = stb.next(); junk, jk = jb.next()
            yield
            if lat:
                cs, csk = csb.next()
                t0 = ti * 128
                load(cs[:, 0:32], cos_t[t0:t0 + 128, :], csk); load(cs[:, 32:64], sin_t[t0:t0 + 128, :], csk)
            if gt in TWINS:
                yield
                rq = rstd_of(st, stk, it[:, 0:256], ik, 256, junk, jk)
                yield
                cq, cqk = cqb.next()
                yield
                stt("dve", cq, it[:, 0:256], rq, qgb, ALU.mult, ALU.mult, [ik, stk, "qgb"], [cqk])
                yield
                ct, ctk = cqT.next()
                yield
                transpose_to(ct, ctk, cq, cqk, 2, "act")
                yield
                qf, qfk = qfb.next()
                yield
                for nb in range(3):
                    pb, pk = bank()
                    for k in range(2):
                        S.op("pe", lambda e, k=k, nb=nb, pb=pb, ct=ct: e.matmul(
                            pb, ct[:, k * 128:(k + 1) * 128], wuq[:, k, nb * 512:(nb + 1) * 512], start=(k == 0), stop=(k == 1)),
                            reads=[ctk, "wuq"], writes=[pk])
                    copy("act" if nb % 2 else "dve", qf[:, nb * 512:(nb + 1) * 512], pb, [pk], [qfk])
                yield
                qb, qbk = qbb.next()
                yield
                copy("dve", qb, qf, [qfk], [qbk])
                yield
                if lat:
                    tmp, tk = tmpb.next()
                    qv = qf.rearrange("p (h d) -> p h d", d=192)[:, :, 128:192].rearrange("p h (a t n) -> p h a t n", a=2, t=2)
                    ov = qb.rearrange("p (h d) -> p h d", d=192)[:, :, 128:192].rearrange("p h (a t n) -> p h a t n", a=2, t=2)
                    x0 = qv[:, :, :, 0, :]; x1 = qv[:, :, :, 1, :]
                    cv = cs[:, 0:32].rearrange("p (a n) -> p a n", a=2).unsqueeze(1).to_broadcast([128, 8, 2, 16])
                    sv = cs[:, 32:64].rearrange("p (a n) -> p a n", a=2).unsqueeze(1).to_broadcast([128, 8, 2, 16])
                    tv = [tmp[:, i, :].rearrange("p (h a n) -> p h a n", h=8, a=2) for i in range(4)]
                    tt("dve", tv[0], x0, cv, ALU.mult, [qfk, csk], [tk])
                    tt("dve", tv[1], x1, sv, ALU.mult, [qfk, csk], [tk])
                    tt("dve", tv[2], x0, sv, ALU.mult, [qfk, csk], [tk])
                    tt("dve", tv[3], x1, cv, ALU.mult, [qfk, csk], [tk])
                    tt("dve", ov[:, :, :, 0, :], tv[0], tv[1], ALU.subtract, [tk], [qbk])
                    tt("dve", ov[:, :, :, 1, :], tv[2], tv[3], ALU.add, [tk], [qbk])
                yield
                store(QS[r:r + 128, :], qb, qbk, [("QS", gt)])
            yield
            rk = rstd_of(st[:, 4:8], stk, it[:, 256:512], ik, 256, junk, jk)
            yield
            ck, ckk = ckb.next()
            yield
            stt("dve", ck, it[:, 256:512], rk, kvgb, ALU.mult, ALU.mult, [ik, stk, "kvgb"], [ckk])
            yield
            kf, kfk = krf.next()
            yield
            if lat:
                tmp, tk = tmpb.next()
                kvw = it[:, 512:576].rearrange("p (a t n) -> p a t n", a=2, t=2)
                okv = kf.rearrange("p (a t n) -> p a t n", a=2, t=2)
                x0 = kvw[:, :, 0, :]; x1 = kvw[:, :, 1, :]
                cv = cs[:, 0:32].rearrange("p (a n) -> p a n", a=2)
                sv = cs[:, 32:64].rearrange("p (a n) -> p a n", a=2)
                tv = [tmp[:, i, 0:32].rearrange("p (a n) -> p a n", a=2) for i in range(4)]
                tt("dve", tv[0], x0, cv, ALU.mult, [ik, csk], [tk])
                tt("dve", tv[1], x1, sv, ALU.mult, [ik, csk], [tk])
                tt("dve", tv[2], x0, sv, ALU.mult, [ik, csk], [tk])
                tt("dve", tv[3], x1, cv, ALU.mult, [ik, csk], [tk])
                tt("dve", okv[:, :, 0, :], tv[0], tv[1], ALU.subtract, [tk], [kfk])
                tt("dve", okv[:, :, 1, :], tv[2], tv[3], ALU.add, [tk], [kfk])
            else:
                copy("dve", kf, it[:, 512:576], [ik], [kfk])
                S.dma("sp", nckv[r:r + 128, :], ck, ckk, reads=[ckk], writes=[("nckv", gt)])
                S.dma("sp", nkr[r:r + 128, :], kf, kfk, reads=[kfk], writes=[("nkr", gt)])
            yield
            kb, kbk = krb.next()
            yield
            copy("dve", kb, kf, [kfk], [kbk])
            yield
            kv_from(ck, ckk, krow(gt), kb, kbk)
            yield
        pipeline(body, range(NT), 3)
        for ct_i in range(PAST // 128):
            ck, ckk = ckb.next(); load(ck, cckv[ct_i * 128:(ct_i + 1) * 128, :], ckk)
            kf, kfk = krf.next(); load(kf, ckr[ct_i * 128:(ct_i + 1) * 128, :], kfk)
            kb, kbk = krb.next()
            copy("dve", kb, kf, [kfk], [kbk])
            kv_from(ck, ckk, RPRM + ct_i * 128, kb, kbk)
        S.barrier()
        if STOP[0] == 4:
            S.finish(); return nc, S

    NKMAX = PAST + LS
    ATTT = scr("ATTT", [8, 128, R], BF16)
    with ExitStack() as es:
        def L(name, shape, dt):
            return es.enter_context(nc.sbuf_tensor(name, list(shape), dt)).ap()
        knT = L("knT", [128, 8, NKMAX], BF16)
        krT = L("krT", [128, NKMAX], BF16)
        memset("dve", krT, 0.0, "krT")
        Vt = L("Vt", [128, NKMAX // 128, 8, 128], BF16)
        onesb = L("onesb", [128, 128], BF16)
        memset("dve", onesb, 1.0, "onesb")
        kvt_b = Rot([L("s4kv%d" % i, [128, 2048], BF16) for i in range(2)], "s4kv")
        krt_b = Rot([L("s4kr%d" % i, [128, 64], BF16) for i in range(2)], "s4kr")
        qt_b = Rot([L("s4q%d" % i, [128, 1536], BF16) for i in range(2)], "s4q")
        qnT_b = Rot([L("s4qn%d" % i, [128, 8, 512], BF16) for i in range(2)], "s4qn")
        qrT_b = Rot([L("s4qr%d" % i, [128, 8, 512], BF16) for i in range(2)], "s4qr")
        for i_ in range(2):
            memset("dve", qrT_b.aps[i_], 0.0, "s4qr%d" % i_)
        pT_b = Rot([L("s4p%d" % i, [128, 512], BF16) for i in range(4)], "s4p")
        pacc_b = Rot([L("s4pa%d" % i, [128, 512], F32) for i in range(2)], "s4pa")
        pab_b = Rot([L("s4pb%d" % i, [128, 512], BF16) for i in range(2)], "s4pb")
        rc_b = Rot([L("s4r%d" % i, [128, 512], F32) for i in range(2)], "s4r")
        ao_b = Rot([L("s4a%d" % i, [128, 512], BF16) for i in range(3)], "s4a")
        scale = 1.0 / math.sqrt(192.0)
        pst["n"] = 6
        pst["i"] = 0
        for si, (r0s, Ls, lat) in enumerate(seqs):
            nk = Ls + (PAST if lat else 0)
            nkt = nk // 128
            kr0 = r0s
            for kt in range(nkt):
                kvt, kvk = kvt_b.next(); load(kvt, KVS[kr0 + kt * 128:kr0 + (kt + 1) * 128, :], kvk)
                krt, krk = krt_b.next(); load(krt, KRS[kr0 + kt * 128:kr0 + (kt + 1) * 128, :], krk)
                pb, pk = bank(); pbb = pb.bitcast(BF16)
                for h in range(8):
                    S.op("pe", lambda e, h=h, pbb=pbb, kvt=kvt: e.transpose(pbb[:, h * 128:(h + 1) * 128], kvt[:, h * 256:h * 256 + 128], ident),
                         reads=[kvk, "ident"], writes=[pk])
                copy("act", knT[:, :, kt * 128:(kt + 1) * 128], pbb.rearrange("p (h k) -> p h k", h=8), [pk], ["knT"])
                pb2, pk2 = bank(); pbb2 = pb2.bitcast(BF16)
                S.op("pe", lambda e, pbb2=pbb2, krt=krt: e.transpose(pbb2[0:64, 0:128], krt, ident), reads=[krk, "ident"], writes=[pk2])
                copy("dve", krT[0:64, kt * 128:(kt + 1) * 128], pbb2[0:64, 0:128], [pk2], ["krT"])
                copy("dve", Vt[:, kt, :, :], kvt.rearrange("p (h d) -> p h d", d=256)[:, :, 128:256], [kvk], ["Vt"])
            if lat:
                qblocks = [[NPT + i for i in range(OWN)], [NPT + OWN, NPT + NTS - 1]]
            else:
                qblocks = [[r0s // 128 + i for i in range(Ls // 128)]]
            for qb_tiles in qblocks:
                nq = len(qb_tiles)
                n = nq * 128
                qn, qnk = qnT_b.next(); qr, qrk = qrT_b.next()
                for qi in range(nq):
                    r = qb_tiles[qi] * 128
                    qt, qk = qt_b.next(); load(qt, QS[r:r + 128, :], qk)
                    pb, pk = bank(); pbb = pb.bitcast(BF16)
                    for h in range(8):
                        S.op("pe", lambda e, h=h, pbb=pbb, qt=qt: e.transpose(pbb[:, h * 128:(h + 1) * 128], qt[:, h * 192:h * 192 + 128], ident),
                             reads=[qk, "ident"], writes=[pk])
                    copy("act", qn[:, :, qi * 128:(qi + 1) * 128], pbb.rearrange("p (h k) -> p h k", h=8), [pk], [qnk])
                    pb2, pk2 = bank(); pbb2 = pb2.bitcast(BF16)
                    for h in range(8):
                        S.op("pe", lambda e, h=h, pbb2=pbb2, qt=qt: e.transpose(pbb2[0:64, h * 128:(h + 1) * 128], qt[:, h * 192 + 128:h * 192 + 192], ident),
                             reads=[qk, "ident"], writes=[pk2])
                    copy("dve", qr[0:64, :, qi * 128:(qi + 1) * 128], pbb2[0:64, :].rearrange("p (h k) -> p h k", h=8), [pk2], [qrk])
                for h in range(8):
                    po, pok = bank_acc()
                    pacc, pack = pacc_b.next()
                    def score(kt, h=h, qn=qn, qr=qr, n=n):
                        ps_, psk = bank()
                        S.op("pe", lambda e: e.matmul(
                            ps_[:, 0:n], knT[:, h, kt * 128:(kt + 1) * 128], qn[:, h, 0:n], start=True, stop=False),
                            reads=["knT", qnk], writes=[psk])
                        S.op("pe", lambda e: e.matmul(
                            ps_[:, 0:n], krT[:, kt * 128:(kt + 1) * 128], qr[:, h, 0:n], start=False, stop=True),
                            reads=["krT", qrk], writes=[psk])
                        pt, ptk = pT_b.next()
                        S.op("act", lambda e: e.activation(pt[:, 0:n], ps_[:, 0:n], AF.Exp, scale=scale),
                             reads=[psk], writes=[ptk])
                        return pt, ptk
                    SK = 2
                    pend = [score(kt) for kt in range(min(SK, nkt))]
                    for kt in range(nkt):
                        pt, ptk = pend.pop(0)
                        if kt + SK < nkt:
                            pend.append(score(kt + SK))
                        S.op("pe", lambda e, kt=kt, h=h, po=po, pt=pt, n=n: e.matmul(
                            po[:, 0:n], Vt[:, kt, h, :], pt[:, 0:n], start=(kt == 0), stop=(kt == nkt - 1)),
                            reads=[ptk, "Vt"], writes=[pok])
                        if kt == 0:
                            copy("dve", pacc[:, 0:n], pt[:, 0:n], [ptk], [pack])
                        else:
                            tt("dve", pacc[:, 0:n], pacc[:, 0:n], pt[:, 0:n], ALU.add, [pack, ptk], [pack])
                    pab, pabk = pab_b.next()
                    copy("dve", pab[:, 0:n], pacc[:, 0:n], [pack], [pabk])
                    psm, psmk = bank()
                    S.op("pe", lambda e, psm=psm, pab=pab, n=n: e.matmul(psm[:, 0:n], onesb, pab[:, 0:n], start=True, stop=True),
                         reads=["onesb", pabk], writes=[psmk])
                    rc, rck = rc_b.next()
                    S.op("dve", lambda e, rc=rc, psm=psm, n=n: e.reciprocal(rc[:, 0:n], psm[:, 0:n]), reads=[psmk], writes=[rck])
                    ao, aok = ao_b.next()
                    tt("dve", ao[:, 0:n], po[:, 0:n], rc[:, 0:n], ALU.mult, [pok, rck], [aok])
                    for qi in range(nq):
                        g_ = qb_tiles[qi]
                        store(ATTT[h][:, g_ * 128:(g_ + 1) * 128], ao[:, qi * 128:(qi + 1) * 128], aok, [("ATTT", h, g_)])
            S.barrier()
            if STOP[0] == 5:
                S.finish(); return nc, S
        pst["n"] = 8

    with ExitStack() as es:
        def L(name, shape, dt):
            return es.enter_context(nc.sbuf_tensor(name, list(shape), dt)).ap()
        cwb = [L("cwb%d" % i, [128, 1536], F32) for i in range(3)]
        cbb = L("cbb", [128, 1536], F32)
        for i in range(3):
            load(cwb[i], cw[i].partition_broadcast(128), "cwb%d" % i)
        load(cbb, cb.partition_broadcast(128), "cbb")
        dtbb = L("dtbb", [128, 32], F32); Ab = L("Ab", [128, 32], F32)
        load(dtbb, dtb.partition_broadcast(128), "dtbb")
        load(Ab, alog.partition_broadcast(128), "Ab")
        S.op("act", lambda e: e.activation(Ab, Ab, AF.Exp), reads=["Ab"], writes=["Ab"])
        S.op("dve", lambda e: e.tensor_scalar(Ab, Ab, -1.0, None, ALU.mult), reads=["Ab"], writes=["Ab"])
        mfb = L("mfb", [32, 2], F32)
        memset("pool", mfb, 1.0, "mfb")
        asel(mfb[:, 0:1], "mfb", [[0, 1]], ALU.is_gt, 0.0, 16, -1)
        S.op("pool", lambda e: e.affine_select(mfb[:, 1:2], mfb[:, 1:2], [[0, 1]], ALU.is_ge, 0.0, base=-16, channel_multiplier=1),
             reads=["mfb"], writes=["mfb"])
        S.op("dve", lambda e: e.tensor_scalar(mfb[:, 1:2], mfb[:, 1:2], -1.0, None, ALU.mult), reads=["mfb"], writes=["mfb"])
        a_b = [Rot([L("s5a%d_%d" % (j, i), [128, 1536], F32) for i in range(3)], "s5a%d_" % j) for j in range(3)]
        t_b = Rot([L("s5t%d" % i, [128, 1536], F32) for i in range(3)], "s5t")
        t2_b = Rot([L("s5u%d" % i, [128, 1536], F32) for i in range(3)], "s5u")
        xo_b = Rot([L("s5x%d" % i, [128, 1536], BF16) for i in range(3)], "s5x")
        d_b = Rot([L("s5d%d" % i, [128, 32], F32) for i in range(3)], "s5d")
        w_b = Rot([L("s5w%d" % i, [128, 8, 32], F32) for i in range(3)], "s5w")
        so_b = Rot([L("s5s%d" % i, [128, 160], F32) for i in range(3)], "s5s")
        g_b = Rot([L("s5g%d" % i, [32, 128], F32) for i in range(3)], "s5g")
        g2_b = Rot([L("s5h%d" % i, [32, 128], F32) for i in range(3)], "s5h")
        def body(gt):
            pr = prow(gt)
            yield
            r = gt * 128
            yield
            av = []
            yield
            si5, ti5 = seq_of_tile(gt)
            lat5 = seqs[si5][2]
            for j in range(3):
                a, ak_ = a_b[j].next()
                if lat5 and j == 0 and ti5 == 0:
                    prl = prow(NPT + NTS - 1) + 127
                    load(a[0:1, :], PROJ[prl:prl + 1, C_X:C_X + 1536], ak_)
                    load(a[1:128, :], PROJ[pr:pr + 127, C_X:C_X + 1536], ak_)
                elif lat5 and j == 2 and ti5 == NTS - 1:
                    prf = prow(NPT)
                    load(a[0:127, :], PROJ[pr + 1:pr + 128, C_X:C_X + 1536], ak_)
                    load(a[127:128, :], PROJ[prf:prf + 1, C_X:C_X + 1536], ak_)
                else:
                    load(a, PROJ[pr - 1 + j:pr - 1 + j + 128, C_X:C_X + 1536], ak_)
                if lat5 and j == 0 and ti5 % OWN == 0:
                    b5 = ti5 // OWN
                    S.op("dve", lambda e, a=a, b5=b5: e.tensor_scalar(a, a, mkt[:, b5:b5 + 1], None, ALU.mult), reads=[ak_, "mkt"], writes=[ak_])
                if lat5 and j == 2 and (ti5 + 1) % OWN == 0:
                    b5 = ((ti5 + 1) % NTS) // OWN
                    S.op("dve", lambda e, a=a, b5=b5: e.tensor_scalar(a, a, mkt[:, 4 + b5:5 + b5], None, ALU.mult), reads=[ak_, "mkt"], writes=[ak_])
                av.append((a, ak_))
            yield
            t, tk = t_b.next(); t2, t2k = t2_b.next()
            yield
            tt("dve", t, av[1][0], cwb[1], ALU.mult, [av[1][1], "cwb1"], [tk])
            yield
            tt("pool", t2, av[0][0], cwb[0], ALU.mult, [av[0][1], "cwb0"], [t2k])
            yield
            tt("dve", t, t, t2, ALU.add, [tk, t2k], [tk])
            yield
            tt("pool", t2, av[2][0], cwb[2], ALU.mult, [av[2][1], "cwb2"], [t2k])
            yield
            tt("dve", t, t, t2, ALU.add, [tk, t2k], [tk])
            yield
            tt("dve", t, t, cbb, ALU.add, [tk, "cbb"], [tk])
            yield
            xo, xok = xo_b.next()
            yield
            S.op("act", lambda e, xo=xo, t=t: e.activation(xo, t, AF.Silu), reads=[tk], writes=[xok])
            yield
            store(XBC[r:r + 128, :], xo, xok, [("XBC", gt)])
            yield
            dt_, dk = d_b.next(); load(dt_, PROJ[pr:pr + 128, C_DT:C_DT + 32], dk)
            yield
            w, wk = w_b.next()
            yield
            V, E, DT, LN, A_, BL, GM, TMP = [w[:, i, :] for i in range(8)]
            yield
            tt("dve", V, dt_, dtbb, ALU.add, [dk, "dtbb"], [wk])
            yield
            S.op("act", lambda e, E=E, V=V: e.activation(E, V, AF.Exp), reads=[wk], writes=[wk])
            yield
            S.op("act", lambda e, E=E, DT=DT: e.activation(DT, E, AF.Ln, bias=oneT), reads=[wk, "oneT"], writes=[wk])
            yield
            S.op("act", lambda e, LN=LN, DT=DT: e.activation(LN, DT, AF.Ln), reads=[wk], writes=[wk])
            yield
            tt("dve", A_, DT, Ab, ALU.mult, [wk, "Ab"], [wk])
            yield
            pb, pk = bank()
            yield
            S.op("pe", lambda e, pb=pb, A_=A_: e.matmul(pb[:, 0:16], tincl, A_[:, 0:16], start=True, stop=True), reads=["tincl", wk], writes=[pk])
            yield
            S.op("pe", lambda e, pb=pb, A_=A_: e.matmul(pb[:, 16:32], texcl, A_[:, 16:32], start=True, stop=True), reads=["texcl", wk], writes=[pk])
            yield
            S.op("pe", lambda e, pb=pb, A_=A_: e.matmul(pb[:, 32:64], ones32, A_, start=True, stop=True), reads=["ones32", wk], writes=[pk])
            yield
            copy("dve", GM[:, 0:16], pb[:, 0:16], [pk], [wk])
            yield
            S.op("dve", lambda e, GM=GM, pb=pb: e.tensor_scalar(GM[:, 16:32], pb[:, 16:32], -1.0, None, ALU.mult), reads=[pk], writes=[wk])
            yield
            so, sok = so_b.next()
            yield
            tt("dve", so[:, 0:32], LN, GM, ALU.subtract, [wk], [sok])
            yield
            tt("dve", TMP[:, 0:16], so[:, 0:16], pb[:, 32:48], ALU.add, [sok, pk], [wk])
            yield
            copy("dve", TMP[:, 16:32], so[:, 16:32], [sok], [wk])
            yield
            S.op("act", lambda e, so=so, TMP=TMP: e.activation(so[:, 32:64], TMP, AF.Exp), reads=[wk], writes=[sok])
            yield
            copy("dve", TMP[:, 0:16], GM[:, 0:16], [wk], [wk])
            yield
            tt("dve", TMP[:, 16:32], GM[:, 16:32], pb[:, 48:64], ALU.add, [wk, pk], [wk])
            yield
            S.op("act", lambda e, so=so, TMP=TMP: e.activation(so[:, 64:96], TMP, AF.Exp), reads=[wk], writes=[sok])
            yield
            S.op("act", lambda e, so=so, pb=pb: e.activation(so[:, 96:128], pb[:, 32:64], AF.Exp), reads=[pk], writes=[sok])
            yield
            copy("dve", so[:, 128:144], A_[:, 0:16], [wk], [sok])
            yield
            S.op("dve", lambda e, so=so, A_=A_: e.tensor_scalar(so[:, 144:160], A_[:, 16:32], -1.0, None, ALU.mult), reads=[wk], writes=[sok])
            yield
            store(STT[r:r + 128, :], so, sok, [("STT", gt)])
            yield
        pipeline(body, range(NT), 3)
        S.barrier()
        if STOP[0] == 6:
            S.finish(); return nc, S

    with ExitStack() as es:
        def L(name, shape, dt):
            return es.enter_context(nc.sbuf_tensor(name, list(shape), dt)).ap()
        hst_b = Rot([L("hst%d" % i, [64, D], F32) for i in range(2)], "hst")
        hbf_b = Rot([L("s6hb%d" % i, [64, D], BF16) for i in range(2)], "s6hb")
        xb_b = Rot([L("s6x%d" % i, [128, 1536], BF16) for i in range(4)], "s6x")
        so_b = Rot([L("s6s%d" % i, [128, 160], F32) for i in range(4)], "s6s")
        xw_b = Rot([L("s6w%d" % i, [128, D], BF16) for i in range(4)], "s6w")
        sti_b = Rot([L("sti%d" % i, [64, 16, 64], F32) for i in range(2)], "sti")
        sto_b = Rot([L("sto%d" % i, [64, 16, 64], F32) for i in range(2)], "sto")
        h0_b = Rot([L("h0t%d" % i, [64, D], F32) for i in range(2)], "h0t")
        def chain(arg):
            si, d = arg
            r0s, Ls, lat = seqs[si]
            nch = Ls // 128
            hst, hstk = hst_b.next()
            sti, stik = sti_b.next()
            sto, stok = sto_b.next()
            h0t, h0k = h0_b.next()
            yield
            if True:
                if lat:
                    load(sti, st_in[d].rearrange("h p n -> p h n"), stik)
                    for half in range(2):
                        pb, pk = bank(); pb2, pk2 = bank()
                        for hh in range(8):
                            h = half * 8 + hh
                            tgt = pb if hh < 4 else pb2
                            S.op("pe", lambda e, h=h, hh=hh, tgt=tgt: e.transpose(tgt[0:64, (hh % 4) * 64:(hh % 4) * 64 + 64], sti[:, h, :], ident32[0:64, 0:64]),
                                 reads=[stik, "ident32"], writes=[pk if hh < 4 else pk2])
                        copy("dve", h0t[:, half * 512:half * 512 + 256], pb[0:64, 0:256], [pk], [h0k])
                        copy("dve", h0t[:, half * 512 + 256:half * 512 + 512], pb2[0:64, 0:256], [pk2], [h0k])
                    copy("dve", hst, h0t, [h0k], [hstk])
                else:
                    memset("dve", hst, 0.0, hstk)
                order = list(range(nch)) if d == 0 else list(range(nch - 1, -1, -1))
                if lat:
                    order = order + order
                def prep(c):
                    r_ = (r0s // 128 + c) * 128
                    xb_, xbk = xb_b.next(); load(xb_, XBC[r_:r_ + 128, :], xbk)
                    so, sok = so_b.next(); load(so, STT[r_:r_ + 128, :], sok)
                    xw, xwk = xw_b.next()
                    tt("pool", xw.rearrange("p (h q) -> p h q", h=16), xb_[:, 0:1024].rearrange("p (h q) -> p h q", h=16),
                       so[:, 32 + d * 16:48 + d * 16].unsqueeze(2).to_broadcast([128, 16, 64]), ALU.mult, [xbk, sok], [xwk])
                    pA, pAk = bank(); pB_, pBk = bank()
                    for g in range(4):
                        tgt, tk_ = (pA, pAk) if g < 2 else (pB_, pBk)
                        S.op("pe", lambda e, g=g, tgt=tgt, xb_=xb_, xw=xw: e.matmul(
                            tgt[0:64, (g % 2) * 256:(g % 2) * 256 + 256], xb_[:, 1024 + g * 64:1024 + (g + 1) * 64], xw[:, g * 256:(g + 1) * 256], start=True, stop=True),
                            reads=[xbk, xwk], writes=[tk_])
                    return so, sok, pA, pAk, pB_, pBk
                pend_ = prep(order[0])
                yield
                for oi_, c in enumerate(order):
                    gt = r0s // 128 + c
                    so, sok, pA, pAk, pB_, pBk = pend_
                    if lat:
                        bb_ = None
                        if d == 0 and c % OWN == 0:
                            bb_ = c // OWN
                        if d == 1 and (c + 1) % OWN == 0:
                            bb_ = ((c + 1) % NTS) // OWN
                        if bb_ is not None:
                            S.op("dve", lambda e, bb_=bb_: e.tensor_scalar(hst, hst, mkt[0:64, 8 + bb_:9 + bb_], None, ALU.mult),
                                 reads=[hstk, "mkt"], writes=[hstk])
                            stt("dve", hst, h0t, mkt[0:64, 12 + bb_:13 + bb_], hst, ALU.mult, ALU.add, [h0k, "mkt", hstk], [hstk])
                    if (not lat) or oi_ >= nch:
                        hb, hbk = hbf_b.next()
                        copy("act", hb, hst, [hstk], [hbk])
                        store(HIN[gt, d], hb, hbk, [("HIN", gt, d)])
                    yield
                    if oi_ + 1 < len(order):
                        pend_ = prep(order[oi_ + 1])
                        yield
                    tt("dve", hst.rearrange("p (h q) -> p h q", h=16), hst.rearrange("p (h q) -> p h q", h=16),
                       so[0:64, 96 + d * 16:112 + d * 16].unsqueeze(2).to_broadcast([64, 16, 64]), ALU.mult, [hstk, sok], [hstk])
                    tt("dve", hst[:, 0:512], hst[:, 0:512], pA[0:64, :], ALU.add, [hstk, pAk], [hstk])
                    tt("dve", hst[:, 512:1024], hst[:, 512:1024], pB_[0:64, :], ALU.add, [hstk, pBk], [hstk])
                    yield
                if not lat:
                    for half in range(2):
                        pb, pk = bank(); pb2, pk2 = bank()
                        for hh in range(8):
                            h = half * 8 + hh
                            tgt = pb if hh < 4 else pb2
                            S.op("pe", lambda e, h=h, hh=hh, tgt=tgt: e.transpose(tgt[0:64, (hh % 4) * 64:(hh % 4) * 64 + 64], hst[:, h * 64:(h + 1) * 64], ident32[0:64, 0:64]),
                                 reads=[hstk, "ident32"], writes=[pk if hh < 4 else pk2])
                        copy("dve", sto[:, half * 8:half * 8 + 4, :], pb[0:64, 0:256].rearrange("p (h n) -> p h n", h=4), [pk], [stok])
                        copy("dve", sto[:, half * 8 + 4:half * 8 + 8, :], pb2[0:64, 0:256].rearrange("p (h n) -> p h n", h=4), [pk2], [stok])
                    S.dma("sp", nssd[si, d].rearrange("h p n -> p h n"), sto, stok, reads=[stok], writes=[("nssd", si, d)])
        pipeline(chain, [(si_, d_) for si_ in range(NSEQ) for d_ in range(2)], PDEPTH)
        S.barrier()
        if STOP[0] == 7:
            S.finish(); return nc, S
    with ExitStack() as es:
        def L(name, shape, dt):
            return es.enter_context(nc.sbuf_tensor(name, list(shape), dt)).ap()
        xb_b = Rot([L("s6ox%d" % i, [128, 1536], BF16) for i in range(2)], "s6ox")
        so_b = Rot([L("s6os%d" % i, [128, 160], F32) for i in range(2)], "s6os")
        mneg = L("mneg", [128, 2, 4, 128], F32)
        memset("pool", mneg, 0.0, "mneg")
        asel(mneg[:, 0], "mneg", [[0, 4], [1, 128]], ALU.is_ge, NEG, 0, -1)
        asel(mneg[:, 1], "mneg", [[0, 4], [-1, 128]], ALU.is_ge, NEG, 0, 1)
        abc_b = Rot([L("abc%d" % i, [128, 32, 128], F32) for i in range(2)], "abc")
        Dbc = L("Dbc", [128, 16], F32); load(Dbc, Dv.partition_broadcast(128), "Dbc")
        gnb = L("gnb", [128, D], F32); load(gnb, gn.partition_broadcast(128), "gnb")
        gm_b = Rot([L("s6g%d" % i, [32, 128], F32) for i in range(2)], "s6g")
        hin_b = [Rot([L("s6i%d_%d" % (d, i), [128, D], BF16) for i in range(2)], "s6i%d_" % d) for d in range(2)]
        z_b = Rot([L("s6z%d" % i, [128, D], F32) for i in range(2)], "s6z")
        bc_b = Rot([L("s6bc%d" % i, [128, 512], BF16) for i in range(2)], "s6bc")
        scs_b = Rot([L("s6sc%d" % i, [128, 4, 128], BF16) for i in range(2)], "s6sc")
        lt_b = Rot([L("s6l%d" % i, [128, 4, 128], BF16) for i in range(2)], "s6l")
        mt_b = Rot([L("s6m%d" % i, [128, 16, 128], BF16) for i in range(2)], "s6m")
        mb_b = Rot([L("s6n%d" % i, [128, 4, 128], BF16) for i in range(2)], "s6n")
        y_b = Rot([L("s6y%d" % i, [128, D], F32) for i in range(2)], "s6y")
        y2_b = Rot([L("s6v%d" % i, [128, D], F32) for i in range(2)], "s6v")
        y3_b = Rot([L("s6u%d" % i, [128, D], F32) for i in range(2)], "s6u")
        yo_b = Rot([L("s6o%d" % i, [128, D], BF16) for i in range(2)], "s6o")
        st_b = Rot([L("s6t%d" % i, [128, 4], F32) for i in range(2)], "s6t")
        def body(gt):
            r = gt * 128
            yield
            pr = prow(gt)
            yield
            xb_, xbk = xb_b.next(); load(xb_, XBC[r:r + 128, :], xbk)
            yield
            so, sok = so_b.next(); load(so, STT[r:r + 128, :], sok)
            yield
            abc, abck = abc_b.next()
            yield
            copy("dve", abc, so[:, 128:160].unsqueeze(2).to_broadcast([128, 32, 128]), [sok], [abck])
            yield
            hin = []
            yield
            for d in range(2):
                hi, hik = hin_b[d].next()
                load(hi[0:64, :], HIN[gt, d], hik); load(hi[64:128, :], HIN[gt, d], hik)
                hin.append((hi, hik))
            yield
            z, zk = z_b.next(); load(z, PROJ[pr:pr + 128, C_Z:C_Z + 1024], zk)
            yield
            bc, bck = bc_b.next()
            yield
            transpose_to(bc, bck, xb_[:, 1024:1536], xbk, 4, "act")
            yield
            pS2 = [bank(), bank()]
            yield
            scs, sck = scs_b.next()
            yield
            for g in range(4):
                p0 = (g % 2) * 64
                pS, pSk = pS2[g % 2]
                S.op("pe", lambda e, g=g, p0=p0, pS=pS, bc=bc: e.matmul(
                    pS[:, (g // 2) * 128:(g // 2) * 128 + 128], bc[p0:p0 + 64, (g // 2) * 128:(g // 2) * 128 + 128],
                    bc[p0:p0 + 64, (2 + g // 2) * 128:(2 + g // 2) * 128 + 128], start=True, stop=True),
                    reads=[bck], writes=[pSk])
            yield
            for g in range(4):
                pS, pSk = pS2[g % 2]
                copy("dve", scs[:, g, :], pS[:, (g // 2) * 128:(g // 2) * 128 + 128], [pSk], [sck])
            yield
            mt, mtk = mt_b.next()
            yield
            for d in range(2):
                for g in range(4):
                    pL, pLk = bank()
                    S.op("pe", lambda e, d=d, pL=pL: e.matmul(pL, ident32, mneg[:, d].rearrange("p a k -> p (a k)"), start=True, stop=False),
                         reads=["ident32", "mneg"], writes=[pLk])
                    for hh in range(4):
                        dh = d * 16 + g * 4 + hh
                        S.op("pe", lambda e, hh=hh, dh=dh, pL=pL, d=d: e.matmul(
                            pL[:, hh * 128:(hh + 1) * 128], abc[:, dh, :], (tincl if d == 0 else texcl), start=False, stop=(hh == 3)),
                            reads=[abck, "tincl", "texcl"], writes=[pLk])
                    lt, ltk = lt_b.next()
                    for hh in range(4):
                        dh = d * 16 + g * 4 + hh
                        S.op("act", lambda e, hh=hh, dh=dh, lt=lt, pL=pL, so=so: e.activation(
                            lt[:, hh, :], pL[:, hh * 128:(hh + 1) * 128], AF.Exp, bias=so[:, dh:dh + 1]),
                            reads=[pLk, sok], writes=[ltk])
                    if d == 0:
                        tt("dve", mt[:, g * 4:(g + 1) * 4, :], lt, scs[:, g:g + 1, :].to_broadcast([128, 4, 128]), ALU.mult, [ltk, sck], [mtk])
                    else:
                        mb_, mbk = mb_b.next()
                        tt("dve", mb_, lt, scs[:, g:g + 1, :].to_broadcast([128, 4, 128]), ALU.mult, [ltk, sck], [mbk])
                        tt("pool", mt[:, g * 4:(g + 1) * 4, :], mt[:, g * 4:(g + 1) * 4, :], mb_, ALU.add, [mtk, mbk], [mtk])
            yield
            pY = [bank(), bank()]
            yield
            for h in range(16):
                tgt, tk_ = pY[h // 8]
                S.op("pe", lambda e, h=h, tgt=tgt, mt=mt, xb_=xb_: e.matmul(
                    tgt[:, (h % 8) * 64:(h % 8) * 64 + 64], mt[:, h, :], xb_[:, h * 64:(h + 1) * 64], start=True, stop=True),
                    reads=[mtk, xbk], writes=[tk_])
            yield
            y, yk = y_b.next()
            yield
            copy("act", y[:, 0:512], pY[0][0], [pY[0][1]], [yk])
            yield
            copy("act", y[:, 512:1024], pY[1][0], [pY[1][1]], [yk])
            yield
            y2, y2k = y2_b.next()
            yield
            for d in range(2):
                hi, hik = hin[d]
                pZ = [bank(), bank()]
                for g in range(4):
                    p0 = (g % 2) * 64
                    tgt, tk_ = pZ[g % 2]
                    S.op("pe", lambda e, g=g, p0=p0, tgt=tgt, bc=bc, hi=hi: e.matmul(
                        tgt[:, (g // 2) * 256:(g // 2) * 256 + 256], bc[p0:p0 + 64, (2 + g // 2) * 128:(2 + g // 2) * 128 + 128],
                        hi[p0:p0 + 64, g * 256:(g + 1) * 256], start=True, stop=True),
                        reads=[bck, hik], writes=[tk_])
                for g in range(4):
                    tgt, tk_ = pZ[g % 2]
                    src = tgt[:, (g // 2) * 256:(g // 2) * 256 + 256].rearrange("p (h q) -> p h q", h=4)
                    scv = so[:, 64 + d * 16 + g * 4:64 + d * 16 + g * 4 + 4].unsqueeze(2).to_broadcast([128, 4, 64])
                    if d == 0:
                        tt("dve", y2[:, g * 256:(g + 1) * 256].rearrange("p (h q) -> p h q", h=4), src, scv, ALU.mult, [tk_, sok], [y2k])
                    else:
                        y3, y3k = y3_b.next()
                        tt("dve", y3[:, 0:256].rearrange("p (h q) -> p h q", h=4), src, scv, ALU.mult, [tk_, sok], [y3k])
                        tt("pool", y2[:, g * 256:(g + 1) * 256], y2[:, g * 256:(g + 1) * 256], y3[:, 0:256], ALU.add, [y2k, y3k], [y2k])
            yield
            y3, y3k = y3_b.next()
            yield
            tt("dve", y3.rearrange("p (h q) -> p h q", h=16), xb_[:, 0:1024].rearrange("p (h q) -> p h q", h=16),
               Dbc.unsqueeze(2).to_broadcast([128, 16, 64]), ALU.mult, [xbk, "Dbc"], [y3k])
            yield
            tt("pool", y2, y2, y3, ALU.add, [y2k, y3k], [y2k])
            yield
            tt("dve", y, y, y2, ALU.add, [yk, y2k], [yk])
            yield
            S.op("act", lambda e, z=z: e.activation(z, z, AF.Silu), reads=[zk], writes=[zk])
            yield
            tt("dve", y, y, z, ALU.mult, [yk, zk], [yk])
            yield
            st, stk = st_b.next()
            yield
            rr = rstd_of(st, stk, y, yk, D, y2, y2k)
            yield
            yo, yok = yo_b.next()
            yield
            stt("dve", yo, y, rr, gnb, ALU.mult, ALU.mult, [yk, stk, "gnb"], [yok])
            yield
            store(YN[r:r + 128, :], yo, yok, [("YN", gt)])
            yield
        pipeline(body, TWIN, PDEPTH)
        S.barrier()
        if STOP[0] == 8:
            S.finish(); return nc, S

    with ExitStack() as es:
        def L(name, shape, dt):
            return es.enter_context(nc.sbuf_tensor(name, list(shape), dt)).ap()
        wm = L("wm", [128, 8, D], BF16); ws_ = L("ws", [128, 8, D], BF16); wo = L("wo", [128, 8, D], BF16)
        loadc(wm, w_o_mla.rearrange("(k p) n -> p k n", p=128), "wm")
        loadc(ws_, w_o_ssd.rearrange("(k p) n -> p k n", p=128), "ws")
        loadc(wo, w_out.rearrange("(k p) n -> p k n", p=128), "wo")
        at_b = Rot([L("s7a%d" % i, [128, D], BF16) for i in range(2)], "s7a")
        yn_b = Rot([L("s7y%d" % i, [128, D], BF16) for i in range(2)], "s7y")
        aT_b = Rot([L("s7at%d" % i, [128, D], BF16) for i in range(2)], "s7at")
        yT_b = Rot([L("s7yt%d" % i, [128, D], BF16) for i in range(2)], "s7yt")
        g_b = Rot([L("s7g%d" % i, [128, 2 * D], F32) for i in range(2)], "s7g")
        x_b = Rot([L("s7x%d" % i, [128, D], F32) for i in range(2)], "s7x")
        m1_b = Rot([L("s7m%d" % i, [128, D], F32) for i in range(2)], "s7m")
        m2_b = Rot([L("s7n%d" % i, [128, D], F32) for i in range(2)], "s7n")
        mg_b = Rot([L("s7mg%d" % i, [128, D], BF16) for i in range(2)], "s7mg")
        mT_b = Rot([L("s7mt%d" % i, [128, D], BF16) for i in range(2)], "s7mt")
        hb_b = Rot([L("s7h%d" % i, [128, D], BF16) for i in range(2)], "s7h")
        hT_b = Rot([L("s7ht%d" % i, [128, D], BF16) for i in range(2)], "s7ht")
        st_b = Rot([L("s7s%d" % i, [128, 4], F32) for i in range(2)], "s7s")

        def mm2(lhsT, lk, W, wkey):
            res = []
            for nb in range(2):
                pb, pk = bank()
                for k in range(8):
                    S.op("pe", lambda e, k=k, nb=nb, pb=pb: e.matmul(
                        pb, lhsT[:, k * 128:(k + 1) * 128], W[:, k, nb * 512:(nb + 1) * 512], start=(k == 0), stop=(k == 7)),
                        reads=[lk, wkey], writes=[pk])
                res.append((pb, pk))
            return res

        def body(gt):
            si, _ = seq_of_tile(gt)
            yield
            m = seqs[si][2]
            yield
            r = gt * 128
            yield
            pr = prow(gt)
            yield
            yn, ynk = yn_b.next(); load(yn, YN[r:r + 128, :], ynk)
            yield
            gg, gk = g_b.next(); load(gg, PROJ[pr:pr + 128, C_GM:C_GM + 2048], gk)
            yield
            xt, xk = x_b.next(); load(xt, xrows(gt), xk)
            yield
            aT, aTk = aT_b.next(); load(aT.rearrange("p (h t) -> p h t", h=8), ATTT[:, :, r:r + 128].rearrange("h p t -> p h t"), aTk)
            yield
            yT, yTk = yT_b.next(); transpose_to(yT, yTk, yn, ynk, 8, "dve")
            yield
            S.op("act", lambda e, gg=gg: e.activation(gg, gg, AF.Sigmoid), reads=[gk], writes=[gk])
            yield
            om = mm2(aT, aTk, wm, "wm")
            yield
            m1, m1k = m1_b.next()
            yield
            for nb in range(2):
                tt("dve", m1[:, nb * 512:(nb + 1) * 512], om[nb][0], gg[:, nb * 512:(nb + 1) * 512], ALU.mult, [om[nb][1], gk], [m1k])
            yield
            os_ = mm2(yT, yTk, ws_, "ws")
            yield
            m2, m2k = m2_b.next()
            yield
            for nb in range(2):
                tt("dve", m2[:, nb * 512:(nb + 1) * 512], os_[nb][0], gg[:, D + nb * 512:D + (nb + 1) * 512], ALU.mult, [os_[nb][1], gk], [m2k])
            yield
            mg, mgk = mg_b.next()
            yield
            tt("pool", mg, m1, m2, ALU.add, [m1k, m2k], [mgk])
            yield
            mT, mTk = mT_b.next(); transpose_to(mT, mTk, mg, mgk, 8, "act")
            yield
            op_ = mm2(mT, mTk, wo, "wo")
            yield
            for nb in range(2):
                tt("dve", m1[:, nb * 512:(nb + 1) * 512], op_[nb][0], modp(m, G1)[:, nb * 512:(nb + 1) * 512], ALU.mult, [op_[nb][1], "mod%d" % m], [m1k])
            yield
            tt("pool", xt, xt, m1, ALU.add, [xk, m1k], [xk])
            yield
            S.dma("pool", XMID[r:r + 128, :], xt, xk, reads=[xk], writes=[("XMID", gt)])
            yield
            st, stk = st_b.next(); hb, hk = hb_b.next(); hT, hTk = hT_b.next()
            yield
            rr = rstd_of(st, stk, xt, xk, D, m2, m2k)
            yield
            stt("dve", m2, xt, rr, modp(m, GSC2), ALU.mult, ALU.mult, [xk, stk, "mod%d" % m], [m2k])
            yield
            tt("pool", hb, m2, modp(m, SH2), ALU.add, [m2k, "mod%d" % m], [hk])
            yield
            transpose_to(hT, hTk, hb, hk, 8, "act")
            yield
            store(H2T[gt], hT, hTk, [("H2T", gt)])
            yield
        pipeline(body, TWIN, PDEPTH)
        S.barrier()
        if STOP[0] == 9:
            S.finish(); return nc, S

    blocks = []
    for si, (r0s, Ls, lat) in enumerate(seqs):
        if lat:
            blocks.append((NPT, OWN, NPT + NTS - 1, NPT + OWN, (8, 9)))
        else:
            assert Ls // 128 <= 4
            blocks.append((r0s // 128, Ls // 128, None, None, None))
    NCOL = sum(b_[1] * 128 + 2 for b_ in blocks)
    ACTT = scr("ACTT", [22, 128, R], BF16)
    es_wd = ExitStack()
    wd = es_wd.enter_context(nc.sbuf_tensor("wd", [128, 22, D], BF16)).ap()
    with ExitStack() as es:
        def L(name, shape, dt):
            return es.enter_context(nc.sbuf_tensor(name, list(shape), dt)).ap()
        h2all = L("h2all", [128, 8, NCOL], BF16)
        memset("dve", h2all, 0.0, "h2all")
        bcols = []
        c = 0
        for (gt0, ntl, lsrc, rsrc, mcols) in blocks:
            bcols.append(c)
            for ti in range(ntl):
                load(h2all[:, :, c + 1 + ti * 128:c + 1 + (ti + 1) * 128], H2T[gt0 + ti].rearrange("p (k t) -> p k t", k=8), "h2all")
            if lsrc is not None:
                load(h2all[:, :, c:c + 1], H2T[lsrc].rearrange("p (k t) -> p k t", k=8)[:, :, 127:128], "h2all",
                     allow_slow_non_contiguous=True)
                S.op("dve", lambda e, c=c, mc=mcols[0]: e.tensor_scalar(h2all[:, :, c:c + 1], h2all[:, :, c:c + 1], mkt[:, mc:mc + 1], None, ALU.mult),
                     reads=["h2all", "mkt"], writes=["h2all"])
            if rsrc is not None:
                ce = c + ntl * 128 + 1
                load(h2all[:, :, ce:ce + 1], H2T[rsrc].rearrange("p (k t) -> p k t", k=8)[:, :, 0:1], "h2all",
                     allow_slow_non_contiguous=True)
                S.op("dve", lambda e, ce=ce, mc=mcols[1]: e.tensor_scalar(h2all[:, :, ce:ce + 1], h2all[:, :, ce:ce + 1], mkt[:, mc:mc + 1], None, ALU.mult),
                     reads=["h2all", "mkt"], writes=["h2all"])
            c += ntl * 128 + 2
        fwT = L("fwT", [128, 2, 22, 3], F32)
        fbT = L("fbT", [128, 2, 22], F32)
        for t_ in range(2):
            for k in range(3):
                load(fwT[:, t_, :, k], fw[k, t_ * DFF:(t_ + 1) * DFF].rearrange("(j p) -> p j", p=128), "fwT", allow_slow_non_contiguous=True)
            load(fbT[:, t_, :], fb[t_ * DFF:(t_ + 1) * DFF].rearrange("(j p) -> p j", p=128), "fbT", allow_slow_non_contiguous=True)
        wg_b = Rot([L("s8w%d" % i, [128, 2, 8, 128], BF16) for i in range(3)], "s8w")
        ue_b = [Rot([L("s8u%d_%d" % (t_, i), [128, 514], F32) for i in range(2)], "s8u%d_" % t_) for t_ in range(2)]
        tc_b = [Rot([L("s8t%d_%d" % (t_, i), [128, 512], F32) for i in range(2)], "s8t%d_" % t_) for t_ in range(2)]
        sg_b = Rot([L("s8s%d" % i, [128, 512], F32) for i in range(2)], "s8s")
        ao_b = Rot([L("s8a%d" % i, [128, 512], BF16) for i in range(3)], "s8a")
        def s8_wload(j):
            wt, wk = wg_b.next()
            loadc(wt[:, 0], w_up[:, j * 128:(j + 1) * 128].rearrange("(k p) n -> p k n", p=128), wk)
            loadc(wt[:, 1], w_up[:, DFF + j * 128:DFF + (j + 1) * 128].rearrange("(k p) n -> p k n", p=128), wk)
            return wt, wk
        s8_next = s8_wload(0)
        for j in range(22):
            wt, wk = s8_next
            if j + 1 < 22:
                s8_next = s8_wload(j + 1)
            loadc(wd[:, j, :], w_down[j * 128:(j + 1) * 128, :], "wd")
            for bi, (gt0, ntl, lsrc_, rsrc_, mcols_) in enumerate(blocks):
                n = ntl * 128
                c = bcols[bi]
                tcs = []
                for t_ in range(2):
                    pa, pak = bank()
                    ph, phk = bank()
                    for k in range(8):
                        S.op("pe", lambda e, k=k, t_=t_, pa=pa, wt=wt, c=c, n=n: e.matmul(
                            pa[:, 0:n], wt[:, t_, k, :], h2all[:, k, c + 1:c + 1 + n], start=(k == 0), stop=(k == 7)),
                            reads=[wk, "h2all"], writes=[pak])
                    for k in range(8):
                        S.op("pe", lambda e, k=k, t_=t_, ph=ph, wt=wt, c=c, n=n: e.matmul(
                            ph[:, 0:2], wt[:, t_, k, :], h2all[:, k, c:c + n + 2:n + 1], start=(k == 0), stop=(k == 7)),
                            reads=[wk, "h2all"], writes=[phk])
                    ue, uek = ue_b[t_].next()
                    copy("act", ue[:, 1:n + 1], pa[:, 0:n], [pak], [uek])
                    copy("dve", ue[:, 0:n + 2:n + 1], ph[:, 0:2], [phk], [uek])
                    tcv, tck = tc_b[t_].next()
                    S.op("act", lambda e, tcv=tcv, pa=pa, t_=t_, j=j, n=n: e.activation(
                        tcv[:, 0:n], pa[:, 0:n], AF.Copy, scale=fwT[:, t_, j, 1:2]), reads=[pak, "fwT"], writes=[tck])
                    stt("dve", tcv[:, 0:n], ue[:, 0:n], fwT[:, t_, j, 0:1], tcv[:, 0:n], ALU.mult, ALU.add, [uek, "fwT", tck], [tck])
                    stt("dve", tcv[:, 0:n], ue[:, 2:n + 2], fwT[:, t_, j, 2:3], tcv[:, 0:n], ALU.mult, ALU.add, [uek, "fwT", tck], [tck])
                    tcs.append((tcv, tck))
                sg, sgk = sg_b.next()
                S.op("act", lambda e, sg=sg, tcv=tcs[0][0], j=j, n=n: e.activation(sg[:, 0:n], tcv[:, 0:n], AF.Silu, bias=fbT[:, 0, j:j + 1]),
                     reads=[tcs[0][1], "fbT"], writes=[sgk])
                ao, aok = ao_b.next()
                stt("dve", ao[:, 0:n], tcs[1][0][:, 0:n], fbT[:, 1, j:j + 1], sg[:, 0:n], ALU.add, ALU.mult, [tcs[1][1], "fbT", sgk], [aok])
                store(ACTT[j][:, gt0 * 128:gt0 * 128 + n], ao[:, 0:n], aok, [("ACTT", j, bi)])
        S.barrier()
        if STOP[0] == 10:
            S.finish(); return nc, S

    with ExitStack() as es:
        def L(name, shape, dt):
            return es.enter_context(nc.sbuf_tensor(name, list(shape), dt)).ap()
        fngb = L("fngb", [128, D], F32); load(fngb, fng.partition_broadcast(128), "fngb")
        aT_b = Rot([L("s10at%d" % i, [128, 22, 128], BF16) for i in range(3)], "s10at")
        x_b = Rot([L("s10x%d" % i, [128, D], F32) for i in range(3)], "s10x")
        o_b = Rot([L("s10o%d" % i, [128, D], F32) for i in range(3)], "s10o")
        st_b = Rot([L("s10s%d" % i, [128, 4], F32) for i in range(3)], "s10s")
        def body(gt):
            si, _ = seq_of_tile(gt)
            yield
            m = seqs[si][2]
            yield
            r = gt * 128
            yield
            aT, aTk = aT_b.next(); load(aT, ACTT[:, :, r:r + 128].rearrange("j p t -> p j t"), aTk)
            yield
            xt, xk = x_b.next(); load(xt, XMID[r:r + 128, :], xk)
            yield
            ot, ok = o_b.next()
            yield
            for nb in range(2):
                pb, pk = bank()
                for k in range(22):
                    S.op("pe", lambda e, k=k, nb=nb, pb=pb, aT=aT: e.matmul(
                        pb, aT[:, k, :], wd[:, k, nb * 512:(nb + 1) * 512], start=(k == 0), stop=(k == 21)),
                        reads=[aTk, "wd"], writes=[pk])
                tt("dve", ot[:, nb * 512:(nb + 1) * 512], pb, modp(m, G2)[:, nb * 512:(nb + 1) * 512], ALU.mult, [pk, "mod%d" % m], [ok])
            yield
            tt("pool", xt, xt, ot, ALU.add, [xk, ok], [xk])
            yield
            st, stk = st_b.next()
            yield
            rr = rstd_of(st, stk, xt, xk, D, ot, ok)
            yield
            stt("dve", ot, xt, rr, fngb, ALU.mult, ALU.mult, [xk, stk, "fngb"], [ok])
            yield
            S.dma("sp", yrows(gt), ot, ok, reads=[ok], writes=[("Y", gt)])
            yield
        pipeline(body, TOWN, 3)
        S.barrier()
    es_wd.close()
    S.finish()
    return nc, S


def rope_tables(LS):
    t = np.arange(LS)
    row = (t // 64).astype(np.float32)
    col = (t % 64).astype(np.float32)
    n = 16
    inv = (10000.0 ** (-np.arange(n, dtype=np.float32) / n)).astype(np.float32)
    ang = np.stack([row[:, None] * inv, col[:, None] * inv], axis=1).reshape(LS, 32).astype(np.float32)
    return np.cos(ang).astype(np.float32), np.sin(ang).astype(np.float32)


_CACHE = {}


def run(inputs, NP, LP, LS, PAST, n_cores=8):
    key = (NP, LP, LS, PAST)
    if key not in _CACHE:
        _CACHE[key] = build(NP, LP, LS, PAST)[0]
    nc = _CACHE[key]
    f = lambda a: np.ascontiguousarray(np.asarray(a, dtype=np.float32))
    cos_t, sin_t = rope_tables(LS)
    shared = {
        "w_ada": f(inputs["w_ada"][0]), "b_ada": f(inputs["b_ada"][0]), "ga": f(inputs["norm_attn_g"][0]),
        "w_in": f(inputs["w_in"][0]), "qg": f(inputs["q_norm_g"][0]), "kvg": f(inputs["kv_norm_g"][0]),
        "w_uq": f(inputs["w_uq"][0]), "w_ukv": f(inputs["w_ukv"][0]), "w_o_mla": f(inputs["w_o_mla"][0]),
        "cw": f(inputs["ssd_conv_w"][0]), "cb": f(inputs["ssd_conv_b"][0]),
        "dtb": f(inputs["ssd_dt_bias"][0]).reshape(32), "alog": f(inputs["ssd_A_log"][0]).reshape(32),
        "Dv": f(inputs["ssd_D"][0]), "gn": f(inputs["ssd_norm_g"][0]), "w_o_ssd": f(inputs["w_o_ssd"][0]),
        "w_out": f(inputs["w_out"][0]), "gf": f(inputs["norm_ffn_g"][0]), "w_up": f(inputs["w_up"][0]),
        "fw": f(inputs["ffn_conv_w"][0]), "fb": f(inputs["ffn_conv_b"][0]), "w_down": f(inputs["w_down"][0]),
        "fng": f(inputs["final_norm_g"]), "cos_t": cos_t, "sin_t": sin_t,
    }
    xpr = f(inputs["x_prompt"]); xsm = f(inputs["x_sample"]); c = f(inputs["c"]); cctx = f(inputs["c_ctx"])
    in_maps = []
    G4 = 4
    NTS = LS // 128
    OWN = NTS // G4
    rots = []
    for core in range(n_cores):
        sq = (core // G4) % xsm.shape[0]
        rr = core % G4
        rot = OWN * rr
        rots.append((sq, rot))
        d = dict(shared)
        d["xp"] = np.ascontiguousarray(xpr[core * NP:(core + 1) * NP].reshape(NP * LP, D))
        d["xs"] = np.ascontiguousarray(np.roll(xsm[sq], -rot * 128, axis=0))
        d["cos_t"] = np.ascontiguousarray(np.roll(cos_t, -rot * 128, axis=0))
        d["sin_t"] = np.ascontiguousarray(np.roll(sin_t, -rot * 128, axis=0))
        mkv = np.ones((128, 16), np.float32)
        for b_ in range(G4):
            m_ = 0.0 if (b_ + rr) % G4 == 0 else 1.0
            mkv[0, b_] = m_
            mkv[127, 4 + b_] = m_
            mkv[:, 8 + b_] = m_
            mkv[:, 12 + b_] = 1.0 - m_
        d["mk"] = mkv
        d["cvec"] = np.ascontiguousarray(np.stack([cctx, c[sq]], 0))
        d["cckv"] = f(inputs["cache_ckv"][sq, 0]); d["ckr"] = f(inputs["cache_krope"][sq, 0])
        d["st"] = f(inputs["state_ssd"][sq, 0])
        in_maps.append(d)
    res = run_bass_kernel_spmd(nc, in_maps, core_ids=list(range(n_cores)))
    outs = res.results
    if DEBUG[0]:
        DBG_OUT.append(outs)
    B = xpr.shape[0]
    y_prompt = np.concatenate([outs[i]["yp"].reshape(NP, LP, D) for i in range(n_cores)], 0)[:B]
    y_sample = np.zeros(xsm.shape, np.float32)
    for core in range(n_cores):
        sq, rot = rots[core]
        y_sample[sq, rot * 128:(rot + OWN) * 128] = outs[core]["ys"]
    new_ckv = np.concatenate([outs[i]["nckv"].reshape(NP, 1, LP, 256) for i in range(n_cores)], 0)[:B]
    new_kr = np.concatenate([outs[i]["nkr"].reshape(NP, 1, LP, 64) for i in range(n_cores)], 0)[:B]
    new_ssd = np.concatenate([outs[i]["nssd"].reshape(NP, 1, 2, 16, 64, 64) for i in range(n_cores)], 0)[:B]
    return (y_prompt.astype(np.float32), y_sample.astype(np.float32), new_ckv.astype(np.float32),
            new_kr.astype(np.float32), new_ssd.astype(np.float32))


def kernel(**inputs):
    return run(inputs, 4, 256, 2048, 512, 8)
```
